# Optimizing a Trainium2 kernel written in Bass

```python
import jax, jax.numpy as jnp
from jax import lax
import numpy as np

D_MODEL = 1024
BATCH = 8
SEQ = 8192
DEPTH = 2

POOL_WINDOWS = (2, 4, 8, 16)
POOL_GROUPS = 4
WIDTH_A = D_MODEL // 2
POOL_GROUP_DIM = WIDTH_A // POOL_GROUPS
CHUNK = 128
SGU_HEADS = 4
WIDTH_B = D_MODEL // 2
SGU_HEAD_DIM = WIDTH_B // SGU_HEADS
WIDTH_C = D_MODEL // 2
CONV_WIDTH = 3
N_BRANCHES = 3
W_IN_COLS = WIDTH_A + 2 * WIDTH_B + 3 * WIDTH_C + N_BRANCHES * D_MODEL
D_FF = 2816
EPS = 1e-6

kernel_name = "hybrid_pool_sgu_shortconv_gated_block"


def rmsnorm(x, g):
    xf = x.astype(jnp.float32)
    y = xf * lax.rsqrt(jnp.mean(xf * xf, axis=-1, keepdims=True) + EPS)
    return (y * g.astype(jnp.float32)).astype(x.dtype)


def causal_dwconv3(z, w):
    s = z.shape[1]
    zp = jnp.pad(z, ((0, 0), (CONV_WIDTH - 1, 0), (0, 0)))
    return zp[:, :s] * w[0] + zp[:, 1:s + 1] * w[1] + zp[:, 2:s + 2] * w[2]


def multiscale_pool(a):
    bsz, s = a.shape[0], a.shape[1]
    af = a.astype(jnp.float32)
    cs = jnp.cumsum(af, axis=1)
    cs_pad = jnp.pad(cs, ((0, 0), (1, 0), (0, 0), (0, 0)))
    t = jnp.arange(s, dtype=jnp.float32)
    outs = []
    for g, w in enumerate(POOL_WINDOWS):
        upper = cs[:, :, g]
        lower = jnp.concatenate(
            [jnp.zeros((bsz, w - 1, a.shape[3]), jnp.float32), cs_pad[:, :s - w + 1, g]], axis=1)
        cnt = jnp.minimum(t + 1.0, float(w))[None, :, None]
        outs.append((upper - lower) / cnt - af[:, :, g])
    return jnp.stack(outs, axis=2).astype(a.dtype)


def setup_inputs(seed: int = 0) -> dict:
    key = jax.random.key(seed)
    k = jax.random.split(key, 20)
    n = jax.random.normal
    f32 = jnp.float32
    L = DEPTH
    tri = jnp.tril(jnp.ones((CHUNK, CHUNK), f32))
    row_scale = (jnp.arange(CHUNK, dtype=f32) + 1.0) ** -0.5
    w_spatial = n(k[5], (L, SGU_HEADS, CHUNK, CHUNK), f32) * tri * row_scale[:, None]
    return {
        "x": n(k[0], (BATCH, SEQ, D_MODEL), f32),
        "g_mix": 1.0 + 0.02 * n(k[1], (L, D_MODEL), f32),
        "w_in": n(k[2], (L, D_MODEL, W_IN_COLS), f32) * D_MODEL ** -0.5,
        "w_pool": n(k[3], (L, POOL_GROUPS, POOL_GROUP_DIM, POOL_GROUP_DIM), f32) * POOL_GROUP_DIM ** -0.5,
        "pool_scale": 1.0 + 0.1 * n(k[4], (L, WIDTH_A), f32),
        "g_sgu": 1.0 + 0.02 * n(k[6], (L, WIDTH_B), f32),
        "w_spatial": w_spatial,
        "b_spatial": 1.0 + 0.01 * n(k[7], (L, SGU_HEADS, CHUNK), f32),
        "conv_c": n(k[8], (L, CONV_WIDTH, WIDTH_C), f32) * CONV_WIDTH ** -0.5,
        "w_branch_a": n(k[9], (L, WIDTH_A, D_MODEL), f32) * WIDTH_A ** -0.5,
        "w_branch_b": n(k[10], (L, WIDTH_B, D_MODEL), f32) * WIDTH_B ** -0.5,
        "w_branch_c": n(k[11], (L, WIDTH_C, D_MODEL), f32) * WIDTH_C ** -0.5,
        "w_o": n(k[12], (L, D_MODEL, D_MODEL), f32) * D_MODEL ** -0.5,
        "g_ffn": 1.0 + 0.02 * n(k[13], (L, D_MODEL), f32),
        "w_up": n(k[14], (L, D_MODEL, 2 * D_FF), f32) * D_MODEL ** -0.5,
        "conv_ffn": n(k[15], (L, CONV_WIDTH, 2 * D_FF), f32) * CONV_WIDTH ** -0.5,
        "conv_ffn_b": 0.01 * n(k[16], (L, 2 * D_FF), f32),
        "w_down": n(k[17], (L, D_FF, D_MODEL), f32) * D_FF ** -0.5,
        "g_final": 1.0 + 0.02 * n(k[18], (D_MODEL,), f32),
    }


def reference(x, g_mix, w_in, w_pool, pool_scale, g_sgu, w_spatial, b_spatial, conv_c,
              w_branch_a, w_branch_b, w_branch_c, w_o, g_ffn, w_up, conv_ffn, conv_ffn_b,
              w_down, g_final):
    bsz, s, _ = x.shape
    n_chunks = s // CHUNK
    splits = np.cumsum([WIDTH_A, 2 * WIDTH_B, WIDTH_C, WIDTH_C, WIDTH_C, D_MODEL, D_MODEL]).tolist()
    for l in range(DEPTH):
        h = rmsnorm(x, g_mix[l])
        p = h @ w_in[l]
        a, uv, c_b, c_c, c_x, ga, gb, gc = jnp.split(p, splits, axis=-1)

        a = a.reshape(bsz, s, POOL_GROUPS, POOL_GROUP_DIM)
        pa = multiscale_pool(a)
        ya = jnp.einsum("bsgd,gde->bsge", pa, w_pool[l]).reshape(bsz, s, WIDTH_A) * pool_scale[l]

        uv = jax.nn.gelu(uv)
        u, v = jnp.split(uv, 2, axis=-1)
        v = rmsnorm(v, g_sgu[l])
        v = v.reshape(bsz, n_chunks, CHUNK, SGU_HEADS, SGU_HEAD_DIM)
        ws = jnp.tril(w_spatial[l])
        sv = jnp.einsum("gts,bcsgd->bctgd", ws, v) + b_spatial[l].T[None, None, :, :, None]
        yb = u * sv.reshape(bsz, s, WIDTH_B)

        yc = c_b * causal_dwconv3(c_c * c_x, conv_c[l])

        merged = (jax.nn.sigmoid(ga) * (ya @ w_branch_a[l])
                  + jax.nn.sigmoid(gb) * (yb @ w_branch_b[l])
                  + jax.nn.sigmoid(gc) * (yc @ w_branch_c[l]))
        x = x + merged @ w_o[l]

        h = rmsnorm(x, g_ffn[l])
        up = causal_dwconv3(h @ w_up[l], conv_ffn[l]) + conv_ffn_b[l]
        gate, val = jnp.split(up, 2, axis=-1)
        x = x + (jax.nn.silu(gate) * val) @ w_down[l]
    return rmsnorm(x, g_final)
```

```python
import numpy as np
from contextlib import ExitStack
import concourse.bass as bass
import concourse.mybir as mybir
from concourse.bass_utils import run_bass_kernel_spmd

F32 = mybir.dt.float32
BF16 = mybir.dt.bfloat16
AF = mybir.ActivationFunctionType
ALU = mybir.AluOpType

D = 1024
L = 2
T = 512
NQ = T // 128
DFF = 2816
NFC = DFF // 128
EPS = 1e-6
SEQ = 8192
DEBUG = False
GPE = "gp"
DBG_IT, DBG_L = 0, 0
NCORES = 8
NSUB_L = 1072
NSUB = NSUB_L * L
CH = 16
NCH = NSUB // CH
R = 6
PIECE = 8
NPIECE = (NCH + PIECE - 1) // PIECE
NVL = 208
NV = NVL * L + 8
POOL_W = (2, 4, 8, 16)

ENGS = ("pe", "act", "dve", "gp", "sp")


class Sched:
    def __init__(self):
        self.prog = {e: [] for e in ENGS}
        self.cnt = {}
        self.waited = {e: {} for e in ENGS}
        self.lw = {}
        self.rd = {}
        self.same_sync = {"pe": False, "act": True, "dve": True, "gp": True, "sp": False}
        self.big_ok = {"act": True, "dve": True}
        self.big = {}

    def deps(self, reads, writes):
        d = {}
        for r in reads:
            x = self.lw.get(r)
            if x is not None and x[1] > d.get(x[0], 0):
                d[x[0]] = x[1]
        for w in writes:
            x = self.lw.get(w)
            if x is not None and x[1] > d.get(x[0], 0):
                d[x[0]] = x[1]
            for k, v in self.rd.get(w, {}).items():
                if v > d.get(k, 0):
                    d[k] = v
        return d

    def _waits(self, eng, d, big=False):
        waits = []
        wd = self.waited[eng]
        for k, v in d.items():
            if k == eng and not self.same_sync[eng]:
                continue
            if k == eng and big and self.big_ok.get(eng) and self.big.get((eng, v)):
                continue
            if wd.get(k, 0) >= v:
                continue
            wd[k] = v
            waits.append((k, v))
        return waits

    def _record(self, sk, val, reads, writes):
        for w in writes:
            self.lw[w] = (sk, val)
            self.rd[w] = {}
        for r in reads:
            if r in writes:
                continue
            m = self.rd.setdefault(r, {})
            if val > m.get(sk, 0):
                m[sk] = val

    def op(self, eng, fn, reads=(), writes=(), semkey=None, inc=1, big=False):
        waits = self._waits(eng, self.deps(reads, writes), big)
        sk = semkey if semkey is not None else eng
        val = self.cnt.get(sk, 0) + inc
        self.cnt[sk] = val
        if big and semkey is None:
            self.big[(eng, val)] = True
        self.prog[eng].append(([(waits, fn)], sk, inc))
        self._record(sk, val, reads, writes)

    def pe_group(self, mms, out_region):
        val = self.cnt.get("pe", 0) + 1
        items = []
        allreads = []
        for i, (fn, reads) in enumerate(mms):
            d = self.deps(reads, [out_region] if i == 0 else [])
            items.append((self._waits("pe", d), fn))
            allreads.extend(reads)
        self.cnt["pe"] = val
        self.prog["pe"].append((items, "pe", 1))
        self._record("pe", val, allreads, [out_region])


def build_program(S):
    NT = S // T
    nc = bass.Bass("TRN2", target_bir_lowering=False)
    xT = nc.dram_tensor("xT", [D, S], F32, kind="ExternalInput").ap()
    wsrc = nc.dram_tensor("wsrc", [128, NSUB * 128], F32, kind="ExternalInput").ap()
    wsm = nc.dram_tensor("wsm", [128, L * 8 * 128], F32, kind="ExternalInput").ap()
    vecs = nc.dram_tensor("vecs", [128, NV], F32, kind="ExternalInput").ap()
    gsg = nc.dram_tensor("gsg", [128, L * 512], F32, kind="ExternalInput").ap()
    bsp = nc.dram_tensor("bsp", [1, L * 4 * 128], F32, kind="ExternalInput").ap()
    cst = nc.dram_tensor("cst", [128, 128 + 64], F32, kind="ExternalInput").ap()
    outT = nc.dram_tensor("outT", [D, S], F32, kind="ExternalOutput").ap()
    wbf = nc.dram_tensor("wbf", [128, NSUB * 128], BF16, kind="Internal").ap()
    dbg = {}
    if DEBUG:
        dbg["y"] = nc.dram_tensor("dbg_y", [128, 12, T], BF16, kind="ExternalOutput").ap()
        dbg["x1"] = nc.dram_tensor("dbg_x1", [128, 8, T], F32, kind="ExternalOutput").ap()
        dbg["act"] = nc.dram_tensor("dbg_act", [128, NFC, T], BF16, kind="ExternalOutput").ap()
        dbg["x2"] = nc.dram_tensor("dbg_x2", [128, 8, T], F32, kind="ExternalOutput").ap()
        dbg["mg"] = nc.dram_tensor("dbg_mg", [128, 8, T], BF16, kind="ExternalOutput").ap()
        dbg["ab"] = nc.dram_tensor("dbg_ab", [128, 4, T + 16], F32, kind="ExternalOutput").ap()
        dbg["pa"] = nc.dram_tensor("dbg_pa", [128, 4, T], BF16, kind="ExternalOutput").ap()
        dbg["h"] = nc.dram_tensor("dbg_h", [128, 8, T], BF16, kind="ExternalOutput").ap()
    xTv = xT.rearrange("(c p) s -> p c s", p=128)
    outTv = outT.rearrange("(c p) s -> p c s", p=128)

    S_ = Sched()
    W = T + 16
    with ExitStack() as st:
        def sb(name, shape, dt):
            return st.enter_context(nc.sbuf_tensor(name, shape, dt))

        xbufs = [sb("xres0", [128, 8, T], F32), sb("xres1", [128, 8, T], F32)]
        hbuf = sb("hbuf", [128, 8, T], BF16)
        rsb = sb("rsb", [128, T], F32)
        ones_bf = sb("ones_bf", [128, 128], BF16)
        wsmall = sb("wsmall", [128, L, 8, 128], BF16)
        bpad = sb("bpad", [128, L, 4, 128], BF16)
        gsgu = sb("gsgu", [128, L * 512], F32)
        vec = sb("vec", [128, NV], F32)
        cst_sb = sb("cst_sb", [128, 192], F32)
        ahalo = sb("ahalo", [128, L, 4, 16], F32)
        zhalo = sb("zhalo", [128, L, 4, 2], F32)
        uph = sb("uph", [128, L * 2, 2 * NFC, 2], F32)
        corrA = sb("corrA", [128, 2 * NFC, 2], F32)
        tmpc = sb("tmpc", [128, 2 * NFC], F32)
        abuf = sb("abuf", [128, 4, W], F32)
        ptmp = sb("ptmp", [128, 2, W], F32)
        pabuf = sb("pabuf", [128, 4, T], BF16)
        ubuf = sb("ubuf", [128, 4, T], F32)
        vstage = sb("vstage", [128, 2, 512], F32)
        vsq = sb("vsq", [128, 512], BF16)
        vss = sb("vss", [128, NQ], F32)
        vrs = sb("vrs", [128, NQ], F32)
        vrr = sb("vrr", [128, NQ], F32)
        vtok = sb("vtok", [128, NQ, 512], BF16)
        cc_sb = sb("cc_sb", [128, 2, T], F32)
        zbuf = sb("zbuf", [128, 2, T + 2], F32)
        cacc = sb("cacc", [128, 2, T], F32)
        ybuf = sb("ybuf", [128, 12, T], BF16)
        gm = sb("gm", [128, 12, T], F32)
        mg = sb("mg", [128, 8, T], BF16)
        actb = sb("actb", [128, NFC, T], BF16)
        facc = sb("facc", [128, 4, T], F32)
        fsg = sb("fsg", [128, 2, T], F32)
        ring = sb("ring", [128, R, CH * 128], BF16)
        ps = [st.enter_context(nc.psum_tensor("ps%d" % i, [128, 512], F32)) for i in range(8)]

        semkeys = ["pe", "act", "dve", "gp", "xl0", "xl1", "ost", "cst0", "cst1", "cst2", "cst3", "cst4"] + \
                  [("wl", s) for s in range(R)] + [("cv", p) for p in range(NPIECE)]
        sems = {}
        for i, k in enumerate(semkeys):
            sems[k] = st.enter_context(nc.semaphore("sem%d" % i))

        state = {"bank": 0, "next_load": 0, "next_conv": 0, "reserved": set()}

        def nextbank():
            while True:
                b = state["bank"]
                state["bank"] = (b + 1) % 8
                if b not in state["reserved"]:
                    return b

        def isbig(ap):
            n = 1
            for d in ap.shape[1:]:
                n *= int(d)
            return n >= 256

        def ACT(out, in_, func, reads, writes, scale=1.0, bias=None, accum=None):
            def fn(e):
                kw = {}
                if bias is not None:
                    kw["bias"] = bias
                if accum is not None:
                    kw["accum_out"] = accum
                return e.activation(out=out, in_=in_, func=func, scale=scale, **kw)
            S_.op("act", fn, reads, writes, big=isbig(out) and accum is None)

        def TT(eng, out, in0, in1, op, reads, writes):
            S_.op(eng, lambda e: e.tensor_tensor(out=out, in0=in0, in1=in1, op=op), reads, writes, big=isbig(out))

        def TS(eng, out, in0, s1, s2, op0, op1, reads, writes):
            if op1 is None:
                S_.op(eng, lambda e: e.tensor_scalar(out=out, in0=in0, scalar1=s1, scalar2=None, op0=op0),
                      reads, writes, big=isbig(out))
            else:
                S_.op(eng, lambda e: e.tensor_scalar(out=out, in0=in0, scalar1=s1, scalar2=s2, op0=op0, op1=op1),
                      reads, writes, big=isbig(out))

        def STT(out, in0, scalar, in1, op0, op1, reads, writes):
            S_.op("dve", lambda e: e.scalar_tensor_tensor(out=out, in0=in0, scalar=scalar, in1=in1,
                                                          op0=op0, op1=op1), reads, writes, big=isbig(out))

        def COPY(eng, out, in_, reads, writes):
            S_.op(eng, lambda e: e.tensor_copy(out=out, in_=in_), reads, writes, big=isbig(out))

        def MEMSET(eng, ap, val, writes):
            S_.op(eng, lambda e: e.memset(ap, val), (), writes)

        def DMA(eng, out, in_, reads, writes, semkey, **kw):
            S_.op(eng, lambda e: e.dma_start(out=out, in_=in_, **kw), reads, writes, semkey=semkey, inc=16)

        def MM(out, lhsT, rhs, start, stop):
            return lambda e: e.matmul(out, lhsT, rhs, start=start, stop=stop)

        total_chunks = NT * NCH

        def issue_conv(p):
            while state["next_conv"] <= min(p, NPIECE - 1):
                q = state["next_conv"]
                a = q * PIECE * CH * 128
                b = min((q + 1) * PIECE, NCH) * CH * 128
                DMA("gp", wbf[:, a:b], wsrc[:, a:b], [], [("wbf", q)], ("cv", q), max_dma_last_dim=8192)
                state["next_conv"] += 1

        def wload_upto(gc):
            while state["next_load"] <= min(gc, total_chunks - 1):
                k = state["next_load"]
                slot = k % R
                c = k % NCH
                if k < NCH:
                    issue_conv(c // PIECE + 2)
                DMA("sp", ring[:, slot, :], wbf[:, c * CH * 128:(c + 1) * CH * 128],
                    [("wbf", c // PIECE)], [("w", slot)], ("wl", slot))
                state["next_load"] += 1

        def wsub(i, n=1):
            gc = i // CH
            slot = gc % R
            off = i % CH
            assert off + n <= CH
            return ring[:, slot, off * 128:(off + n) * 128], ("w", slot), gc

        def group(bank, specs):
            chunks = [x[2] for sp_ in specs for x in (sp_[1], sp_[2]) if x[2] is not None]
            if chunks:
                wload_upto(max(chunks))
            mms = []
            for (o, lh, rh, s0, s1) in specs:
                mms.append((MM(o, lh[0], rh[0], s0, s1), [lh[1], rh[1]]))
            S_.pe_group(mms, ("ps", bank))
            if chunks:
                wload_upto(min(chunks) + R - 1)

        def proj_group(bank, wi, nk, rhs_of):
            specs = []
            for kc in range(nk):
                specs.append((ps[bank][:, :], wsub(wi + kc), rhs_of(kc), kc == 0, kc == nk - 1))
            group(bank, specs)

        def h_of(kc):
            return (hbuf[:, kc, :], ("h", kc), None)

        DMA("sp", vec[:, :], vecs[:, :], [], [("vec",)], "cst0")
        DMA("sp", gsgu[:, :], gsg[:, :], [], [("gsgu",)], "cst1")
        DMA("sp", cst_sb[:, :], cst[:, :], [], [("cstsb",)], "cst2")
        gmflat = gm[:, 0:4, :]
        DMA("sp", gmflat, wsm.rearrange("p (a b) -> p a b", a=4), [], [("gm", k) for k in range(4)], "cst3")
        MEMSET("dve", ones_bf[:, :], 1.0, [("ones",)])
        MEMSET("dve", ahalo[:, :, :, :], 0.0, [("ahalo", l) for l in range(L)])
        MEMSET("dve", zhalo[:, :, :, :], 0.0, [("zhalo", l, j) for l in range(L) for j in range(4)])
        MEMSET("dve", uph[:, :, :, :], 0.0, [("uph", l, p_) for l in range(L) for p_ in range(2)])
        MEMSET("dve", bpad[:, :, :, :], 0.0, [("bpad",)])
        DMA("gp", bpad[0:1, :, :, :], bsp.rearrange("o (l g t) -> o l g t", l=L, g=4), [], [("bpad",)], "cst4")
        for l in range(L):
            for g in range(8):
                idx = l * 8 + g
                src = gm[:, idx // 4, (idx % 4) * 128:(idx % 4 + 1) * 128]
                if g < 4:
                    COPY("dve", wsmall[:, l, g, :], src, [("gm", idx // 4)], [("wsmall",)])
                else:
                    TT("dve", wsmall[:, l, g, :], src, cst_sb[:, 0:128], ALU.mult,
                       [("gm", idx // 4), ("cstsb",)], [("wsmall",)])
        issue_conv(2)

        eps_t = sb("eps_t", [128, 1], F32)
        eps_ap = eps_t[:, 0:1]
        MEMSET("dve", eps_t[:, :], EPS, [("vec2",)])

        def norm_begin():
            bank = nextbank()
            state["reserved"].add(bank)
            return bank

        def norm_chunk(bank, c, xp):
            xres = xbufs[xp]
            sqacc, sqreg = cacc[:, 0, :], ("cacc", 0)
            if c == 0:
                ACT(sqacc, xres[:, c, :], AF.Square, [("x", xp, c)], [sqreg])
            elif c < 7:
                i = c % 2
                ACT(vstage[:, i, :], xres[:, c, :], AF.Square, [("x", xp, c)], [("vst", i)])
                if c < 6:
                    TT("gp", sqacc, sqacc, vstage[:, i, :], ALU.add, [sqreg, ("vst", i)], [sqreg])
                else:
                    TT("gp", vsq[:, :], sqacc, vstage[:, i, :], ALU.add, [sqreg, ("vst", i)], [("vsq",)])
                    S_.pe_group([(MM(ps[bank][:, :], ones_bf[:, :], vsq[:, :], True, False),
                                  [("ones",), ("vsq",)])], ("ps", bank))
            else:
                ACT(mg[:, c, :], xres[:, c, :], AF.Square, [("x", xp, c)], [("mg", c)])
                S_.pe_group([(MM(ps[bank][:, :], ones_bf[:, :], mg[:, c, :], False, True), [("ones",), ("mg", c)])],
                            ("ps", bank))

        def norm_finish(bank, gbase, to_out, xp):
            xres = xbufs[xp]
            ACT(rsb[:, :], ps[bank][:, :], AF.Sqrt, [("ps", bank), ("vec2",)], [("rsb",)], scale=1.0 / D,
                bias=eps_ap)
            S_.op("dve", lambda e: e.reciprocal(out=ps[bank][:, :], in_=rsb[:, :]), [("rsb",)], [("ps", bank)],
                  big=True)
            for c in range(8):
                if to_out:
                    dst, reg = gm[:, c, :], ("gm", c)
                else:
                    dst, reg = hbuf[:, c, :], ("h", c)
                STT(dst, xres[:, c, :], vec[:, gbase + c:gbase + c + 1], ps[bank][:, :], ALU.mult, ALU.mult,
                    [("x", xp, c), ("ps", bank), ("vec",)], [reg])
                if to_out:
                    DMA("sp", outTv[:, c, to_out[0]:to_out[0] + T], gm[:, c, :], [("gm", c)], [], "ost")
            if to_out:
                for c in range(8):
                    S_.rd[("gm", c)]["ost"] = S_.cnt["ost"]
            state["reserved"].discard(bank)

        wi = 0
        for it in range(NT):
            t0 = it * T
            par = it % 2
            xp = it % 2
            xres = xbufs[xp]
            if it == 0:
                DMA("sp", xres[:, :, :], xTv[:, :, t0:t0 + T], [], [("x", xp, c) for c in range(8)], "xl%d" % xp)
                nbank = norm_begin()
                for c in range(8):
                    norm_chunk(nbank, c, xp)
                norm_finish(nbank, 0, False, xp)
            def prefetch_x(c):
                sk = "xl%d" % (1 - xp)
                DMA("sp", xbufs[1 - xp][:, c, :], xTv[:, c, t0 + T:t0 + 2 * T], [], [("x", 1 - xp, c)], sk)
                if c == 7:
                    for cc_ in range(8):
                        S_.lw[("x", 1 - xp, cc_)] = (sk, S_.cnt[sk])
            wi = it * NSUB
            for l in range(L):
                vb = l * NVL
                if l > 0:
                    norm_finish(nbank, vb + 0, False, xp)
                COPY(GPE, abuf[:, :, 0:16], ahalo[:, l, :, :], [("ahalo", l)], [("abuf", g) for g in range(4)])
                for g in range(4):
                    bank = nextbank()
                    proj_group(bank, wi, 8, h_of)
                    wi += 8
                    ACT(abuf[:, g, 16:W], ps[bank][:, :], AF.Copy, [("ps", bank)], [("abuf", g)])
                COPY(GPE, ahalo[:, l, :, :], abuf[:, :, T:W], [("abuf", g) for g in range(4)], [("ahalo", l)])
                for g in range(4):
                    cur, cur_reg = abuf[:, g, :], ("abuf", g)
                    sh, k = 1, 0
                    for step in range(g + 1):
                        lo = 2 * sh - 1
                        TT(GPE, ptmp[:, k, lo:W], cur[:, lo:W], cur[:, lo - sh:W - sh], ALU.add,
                           [cur_reg], [("ptmp", k)])
                        cur, cur_reg = ptmp[:, k, :], ("ptmp", k)
                        k ^= 1
                        sh *= 2
                    w = POOL_W[g]
                    TS(GPE, cur[:, 16:W], cur[:, 16:W], 1.0 / w, 0.0, ALU.mult, ALU.add, [cur_reg], [cur_reg])
                    if it == 0:
                        TT(GPE, cur[:, 16:32], cur[:, 16:32], cst_sb[:, 128 + g * 16:128 + (g + 1) * 16], ALU.mult,
                           [cur_reg, ("cstsb",)], [cur_reg])
                    TT(GPE, pabuf[:, g, :], cur[:, 16:W], abuf[:, g, 16:W], ALU.subtract,
                       [cur_reg, ("abuf", g)], [("pa", g)])
                for j in range(4):
                    bank = nextbank()
                    proj_group(bank, wi, 8, h_of)
                    wi += 8
                    ACT(ubuf[:, j, :], ps[bank][:, :], AF.Gelu_apprx_tanh, [("ps", bank)], [("u", j)])
                vbase = wi
                wi += 32
                for q in range(NQ):
                    bank = nextbank()
                    specs = []
                    for kc in range(8):
                        specs.append((ps[bank][:, :], (hbuf[:, kc, q * 128:(q + 1) * 128], ("h", kc), None),
                                      wsub(vbase + kc * 4, 4), kc == 0, kc == 7))
                    group(bank, specs)
                    i = q % 2
                    ACT(vstage[:, i, :], ps[bank][:, :], AF.Gelu_apprx_tanh, [("ps", bank)], [("vst", i)])
                    ACT(vsq[:, :], vstage[:, i, :], AF.Square, [("vst", i)], [("vsq",), ("vss", q)],
                        accum=vss[:, q:q + 1])
                    ACT(vrs[:, q:q + 1], vss[:, q:q + 1], AF.Sqrt, [("vss", q), ("vec2",)], [("vrs", q)],
                        scale=1.0 / 512, bias=eps_ap)
                    S_.op("dve", (lambda q: lambda e: e.reciprocal(out=vrr[:, q:q + 1], in_=vrs[:, q:q + 1]))(q),
                          [("vrs", q)], [("vrr", q)])
                    STT(vtok[:, q, :], vstage[:, i, :], vrr[:, q:q + 1], gsgu[:, l * 512:(l + 1) * 512],
                        ALU.mult, ALU.mult, [("vst", i), ("vrr", q), ("gsgu",)], [("vtok", q)])
                for j in range(4):
                    i = j % 2
                    banks = []
                    for _ in range(3):
                        bank = nextbank()
                        proj_group(bank, wi, 8, h_of)
                        wi += 8
                        banks.append(bank)
                    b_cc, b_cx, b_cb = banks
                    ACT(cc_sb[:, i, :], ps[b_cc][:, :], AF.Copy, [("ps", b_cc)], [("cc", i)])
                    COPY(GPE, zbuf[:, i, 0:2], zhalo[:, l, j, :], [("zhalo", l, j)], [("z", i)])
                    TT("dve", zbuf[:, i, 2:T + 2], ps[b_cx][:, :], cc_sb[:, i, :], ALU.mult,
                       [("ps", b_cx), ("cc", i)], [("z", i)])
                    COPY(GPE, zhalo[:, l, j, :], zbuf[:, i, T:T + 2], [("z", i)], [("zhalo", l, j)])
                    cb = vb + 12
                    TS("dve", cacc[:, i, :], zbuf[:, i, 2:T + 2], vec[:, cb + 8 + j:cb + 9 + j], None, ALU.mult, None,
                       [("z", i), ("vec",)], [("cacc", i)])
                    STT(cacc[:, i, :], zbuf[:, i, 1:T + 1], vec[:, cb + 4 + j:cb + 5 + j], cacc[:, i, :],
                        ALU.mult, ALU.add, [("z", i), ("vec",), ("cacc", i)], [("cacc", i)])
                    STT(cacc[:, i, :], zbuf[:, i, 0:T], vec[:, cb + j:cb + 1 + j], cacc[:, i, :],
                        ALU.mult, ALU.add, [("z", i), ("vec",), ("cacc", i)], [("cacc", i)])
                    TT("dve", ybuf[:, 8 + j, :], ps[b_cb][:, :], cacc[:, i, :], ALU.mult,
                       [("ps", b_cb), ("cacc", i)], [("y", 8 + j)])
                for g in range(4):
                    bank = nextbank()
                    S_.pe_group([(MM(ps[bank][:, :], wsmall[:, l, g, :], pabuf[:, g, :], True, True),
                                  [("wsmall",), ("pa", g)])], ("ps", bank))
                    ACT(ybuf[:, g, :], ps[bank][:, :], AF.Identity, [("ps", bank), ("vec",)], [("y", g)],
                        scale=vec[:, vb + 8 + g:vb + 9 + g])
                for g in range(4):
                    bank = nextbank()
                    mms = []
                    for q in range(NQ):
                        o = ps[bank][:, q * 128:(q + 1) * 128]
                        mms.append((MM(o, vtok[:, q, g * 128:(g + 1) * 128], wsmall[:, l, 4 + g, :], True, False),
                                    [("vtok", q), ("wsmall",)]))
                        mms.append((MM(o, ones_bf[:, :], bpad[:, l, g, :], False, True), [("ones",), ("bpad",)]))
                    S_.pe_group(mms, ("ps", bank))
                    TT("dve", ybuf[:, 4 + g, :], ps[bank][:, :], ubuf[:, g, :], ALU.mult,
                       [("ps", bank), ("u", g)], [("y", 4 + g)])
                for m in range(8):
                    i = m % 2
                    gbanks = []
                    for b in range(3):
                        bank = nextbank()
                        proj_group(bank, wi, 8, h_of)
                        wi += 8
                        gbanks.append(bank)
                    for b in range(3):
                        ACT(gm[:, i * 3 + b, :], ps[gbanks[b]][:, :], AF.Sigmoid, [("ps", gbanks[b])],
                            [("gm", i * 3 + b)])
                    bbanks = []
                    for b in range(3):
                        bank = nextbank()
                        proj_group(bank, wi, 4, (lambda b: lambda kc: (ybuf[:, 4 * b + kc, :], ("y", 4 * b + kc), None))(b))
                        wi += 4
                        bbanks.append(bank)
                    for b in range(3):
                        TT("dve", gm[:, 6 + i * 3 + b, :], ps[bbanks[b]][:, :], gm[:, i * 3 + b, :], ALU.mult,
                           [("ps", bbanks[b]), ("gm", i * 3 + b)], [("gm", 6 + i * 3 + b)])
                    TT(GPE, gm[:, 6 + i * 3, :], gm[:, 6 + i * 3, :], gm[:, 7 + i * 3, :], ALU.add,
                       [("gm", 6 + i * 3), ("gm", 7 + i * 3)], [("gm", 6 + i * 3)])
                    TT(GPE, mg[:, m, :], gm[:, 6 + i * 3, :], gm[:, 8 + i * 3, :], ALU.add,
                       [("gm", 6 + i * 3), ("gm", 8 + i * 3)], [("mg", m)])
                    if l == 0 and it + 1 < NT:
                        prefetch_x(m)
                if DEBUG and it == DBG_IT and l == DBG_L:
                    DMA("sp", dbg["y"][:, :, :], ybuf[:, :, :], [("y", k) for k in range(12)], [], "ost")
                    DMA("sp", dbg["mg"][:, :, :], mg[:, :, :], [("mg", k) for k in range(8)], [], "ost")
                obanks = []
                for m in range(8):
                    bank = nextbank()
                    proj_group(bank, wi, 8, lambda kc: (mg[:, kc, :], ("mg", kc), None))
                    wi += 8
                    obanks.append(bank)
                    TT("dve", xres[:, m, :], ps[bank][:, :], xres[:, m, :], ALU.add,
                       [("ps", bank), ("x", xp, m)], [("x", xp, m)])
                nbank = norm_begin()
                for c in range(8):
                    norm_chunk(nbank, c, xp)
                if DEBUG and it == DBG_IT and l == DBG_L:
                    DMA("sp", dbg["x1"][:, :, :], xres[:, :, :], [("x", xp, k) for k in range(8)], [], "ost")
                norm_finish(nbank, vb + 24, False, xp)
                if DEBUG and it == DBG_IT and l == DBG_L:
                    DMA("sp", dbg["h"][:, :, :], hbuf[:, :, :], [("h", k) for k in range(8)], [], "ost")
                fb = vb + 32
                bb = vb + 164
                hold = ("uph", l, par)
                hnew = ("uph", l, 1 - par)
                uold = uph[:, l * 2 + par, :, :]
                unew = uph[:, l * 2 + 1 - par, :, :]

                def silu_mul(c, i):
                    ACT(fsg[:, i, :], facc[:, i * 2, :], AF.Silu, [("facc", i * 2)], [("fsg", i)])
                    TT(GPE, actb[:, c, :], fsg[:, i, :], facc[:, i * 2 + 1, :], ALU.mult,
                       [("fsg", i), ("facc", i * 2 + 1)], [("act", c)])

                W0 = vec[:, fb:fb + 44]
                W1 = vec[:, fb + 44:fb + 88]
                TT("dve", tmpc[:, :], uold[:, :, 1], W1, ALU.mult, [hold, ("vec",)], [("tmpc",)])
                TT("dve", corrA[:, :, 0], uold[:, :, 0], W0, ALU.mult, [hold, ("vec",)], [("corrA",)])
                TT("dve", corrA[:, :, 0], corrA[:, :, 0], tmpc[:, :], ALU.add, [("corrA",), ("tmpc",)], [("corrA",)])
                TT("dve", corrA[:, :, 1], uold[:, :, 1], W0, ALU.mult, [hold, ("vec",)], [("corrA",)])
                prev = None
                for c in range(NFC):
                    i = c % 2
                    for gv, cc in enumerate((c, NFC + c)):
                        bank = nextbank()
                        proj_group(bank, wi, 8, h_of)
                        wi += 8
                        fr = ("facc", i * 2 + gv)
                        acc = facc[:, i * 2 + gv, :]
                        pb = ps[bank]
                        w0 = vec[:, fb + cc:fb + cc + 1]
                        w1 = vec[:, fb + 44 + cc:fb + 45 + cc]
                        w2 = vec[:, fb + 88 + cc:fb + 89 + cc]
                        ACT(acc, pb[:, :], AF.Identity, [("ps", bank), ("vec",)], [fr], scale=w2,
                            bias=vec[:, bb + cc:bb + cc + 1])
                        ACT(unew[:, cc, :], pb[:, T - 2:T], AF.Copy, [("ps", bank)], [hnew])
                        TT("dve", acc[:, 0:2], acc[:, 0:2], corrA[:, cc, :], ALU.add, [fr, ("corrA",)], [fr])
                        STT(acc[:, 1:T], pb[:, 0:T - 1], w1, acc[:, 1:T], ALU.mult, ALU.add,
                            [("ps", bank), ("vec",), fr], [fr])
                        STT(acc[:, 2:T], pb[:, 0:T - 2], w0, acc[:, 2:T], ALU.mult, ALU.add,
                            [("ps", bank), ("vec",), fr], [fr])
                    if prev is not None:
                        silu_mul(*prev)
                    prev = (c, i)
                silu_mul(*prev)
                if l == L - 1 and it + 1 < NT:
                    nb2 = norm_begin()
                    for c in range(8):
                        norm_chunk(nb2, c, 1 - xp)
                    norm_finish(nb2, 0, False, 1 - xp)
                for m in range(8):
                    bank = nextbank()
                    proj_group(bank, wi, NFC, lambda kc: (actb[:, kc, :], ("act", kc), None))
                    wi += NFC
                    TT("dve", xres[:, m, :], ps[bank][:, :], xres[:, m, :], ALU.add,
                       [("ps", bank), ("x", xp, m)], [("x", xp, m)])
                nbank = norm_begin()
                for c in range(8):
                    norm_chunk(nbank, c, xp)
                if DEBUG and it == DBG_IT and l == DBG_L:
                    DMA("sp", dbg["act"][:, :, :], actb[:, :, :], [("act", k) for k in range(NFC)], [], "ost")
                    DMA("sp", dbg["x2"][:, :, :], xres[:, :, :], [("x", xp, k) for k in range(8)], [], "ost")
            assert wi == (it + 1) * NSUB, (wi, it)
            norm_finish(nbank, NVL * L, (t0,), xp)

        S_.prog["sp"].append(([([("ost", S_.cnt["ost"])], None)], None, 0))

        with nc.Block() as block:
            def make(engname):
                def body(e):
                    for items, sk, inc in S_.prog[engname]:
                        last = None
                        for waits, fn in items:
                            for k, v in waits:
                                e.wait_ge(sems[k], v)
                            if fn is not None:
                                last = fn(e)
                        if last is not None and inc:
                            last.then_inc(sems[sk], inc)
                return body
            block.tensor(make("pe"))
            block.scalar(make("act"))
            block.vector(make("dve"))
            block.gpsimd(make("gp"))
            block.sync(make("sp"))
    return nc


def _fm(v):
    v = np.asarray(v, np.float32)
    return v.reshape(-1, 128).T


def prep_weights(inp):
    w_in, w_up, w_down, w_o = inp["w_in"], inp["w_up"], inp["w_down"], inp["w_o"]
    wbr = (inp["w_branch_a"], inp["w_branch_b"], inp["w_branch_c"])
    wsrc = np.empty((NSUB, 128, 128), np.float32)
    n = 0
    for l in range(L):
        Wl = w_in[l]

        def cols(Wm, col0, nk):
            nonlocal n
            blk = Wm[:nk * 128, col0:col0 + 128].reshape(nk, 128, 128)
            wsrc[n:n + nk] = blk
            n += nk
        for g in range(4):
            cols(Wl, g * 128, 8)
        for j in range(4):
            cols(Wl, 512 + j * 128, 8)
        for kc in range(8):
            for q4 in range(4):
                wsrc[n] = Wl[kc * 128:(kc + 1) * 128, 1024 + q4 * 128:1024 + (q4 + 1) * 128]
                n += 1
        for j in range(4):
            for off in (2048, 2560, 1536):
                cols(Wl, off + j * 128, 8)
        for m in range(8):
            for off in (3072, 4096, 5120):
                cols(Wl, off + m * 128, 8)
            for b in range(3):
                cols(wbr[b][l], m * 128, 4)
        for m in range(8):
            cols(w_o[l], m * 128, 8)
        for c in range(NFC):
            for cc in (c, NFC + c):
                cols(w_up[l], cc * 128, 8)
        for m in range(8):
            cols(w_down[l], m * 128, NFC)
    assert n == NSUB
    wsrc = np.ascontiguousarray(wsrc.transpose(1, 0, 2)).reshape(128, NSUB * 128)

    wsm = np.empty((L, 8, 128, 128), np.float32)
    for l in range(L):
        for g in range(4):
            wsm[l, g] = inp["w_pool"][l, g]
            wsm[l, 4 + g] = inp["w_spatial"][l, g].T
    wsm = np.ascontiguousarray(wsm.reshape(L * 8, 128, 128).transpose(1, 0, 2)).reshape(128, L * 8 * 128)

    vecs = np.empty((128, NV), np.float32)
    for l in range(L):
        vb = l * NVL
        vecs[:, vb:vb + 8] = _fm(inp["g_mix"][l])
        vecs[:, vb + 8:vb + 12] = _fm(inp["pool_scale"][l])
        for tap in range(3):
            vecs[:, vb + 12 + tap * 4:vb + 16 + tap * 4] = _fm(inp["conv_c"][l, tap])
        vecs[:, vb + 24:vb + 32] = _fm(inp["g_ffn"][l])
        for tap in range(3):
            vecs[:, vb + 32 + tap * 44:vb + 32 + (tap + 1) * 44] = _fm(inp["conv_ffn"][l, tap])
        vecs[:, vb + 164:vb + 208] = _fm(inp["conv_ffn_b"][l])
    vecs[:, NVL * L:NVL * L + 8] = _fm(inp["g_final"])

    gsg = np.ascontiguousarray(np.broadcast_to(np.asarray(inp["g_sgu"], np.float32).reshape(1, L * 512), (128, L * 512)))
    bspv = np.ascontiguousarray(np.asarray(inp["b_spatial"], np.float32).reshape(1, L * 4 * 128))
    cst = np.zeros((128, 192), np.float32)
    s_idx = np.arange(128)[:, None]
    t_idx = np.arange(128)[None, :]
    cst[:, 0:128] = (t_idx >= s_idx).astype(np.float32)
    for g, w in enumerate(POOL_W):
        tt = np.arange(16)
        cst[:, 128 + g * 16:128 + (g + 1) * 16] = (w / np.minimum(tt + 1, w)).astype(np.float32)[None, :]
    return dict(wsrc=wsrc, wsm=wsm, vecs=vecs, gsg=gsg, bsp=bspv, cst=cst)


_CACHE = {}


def run(inputs, S=SEQ, ncores=NCORES, trace=False):
    inp = {k: np.asarray(v) for k, v in inputs.items()}
    common = prep_weights(inp)
    x = inp["x"]
    in_maps = []
    for b in range(ncores):
        m = dict(common)
        m["xT"] = np.ascontiguousarray(x[b, :S, :].T)
        in_maps.append(m)
    if S not in _CACHE:
        _CACHE[S] = build_program(S)
    nc = _CACHE[S]
    res = run_bass_kernel_spmd(nc, in_maps, core_ids=list(range(ncores)), trace=trace)
    out = np.stack([np.ascontiguousarray(r["outT"].T) for r in res.results], axis=0)
    return out.astype(np.float32), res


def kernel(**inputs):
    out, _ = run(inputs)
    return out
```

```python
import numpy as np
from contextlib import ExitStack
import concourse.bass as bass
import concourse.mybir as mybir
from concourse.bass_utils import run_bass_kernel_spmd

F32 = mybir.dt.float32
BF16 = mybir.dt.bfloat16
AF = mybir.ActivationFunctionType
ALU = mybir.AluOpType

D = 1024
L = 2
T = 512
NQ = T // 128
DFF = 2816
NFC = DFF // 128
EPS = 1e-6
SEQ = 8192
DEBUG = False
GPE = "gp"
DBG_IT, DBG_L = 0, 0
NCORES = 8
NSUB_L = 1072
NSUB = NSUB_L * L
CH = 16
NCH = NSUB // CH
R = 6
PIECE = 8
NPIECE = (NCH + PIECE - 1) // PIECE
NVL = 208
NV = NVL * L + 8
POOL_W = (2, 4, 8, 16)

ENGS = ("pe", "act", "dve", "gp", "sp")


class Sched:
    def __init__(self):
        self.prog = {e: [] for e in ENGS}
        self.cnt = {}
        self.waited = {e: {} for e in ENGS}
        self.lw = {}
        self.rd = {}
        self.same_sync = {"pe": False, "act": True, "dve": True, "gp": True, "sp": False}
        self.big_ok = {"act": True, "dve": True}
        self.big = {}

    def deps(self, reads, writes):
        d = {}
        for r in reads:
            x = self.lw.get(r)
            if x is not None and x[1] > d.get(x[0], 0):
                d[x[0]] = x[1]
        for w in writes:
            x = self.lw.get(w)
            if x is not None and x[1] > d.get(x[0], 0):
                d[x[0]] = x[1]
            for k, v in self.rd.get(w, {}).items():
                if v > d.get(k, 0):
                    d[k] = v
        return d

    def _waits(self, eng, d, big=False):
        waits = []
        wd = self.waited[eng]
        for k, v in d.items():
            if k == eng and not self.same_sync[eng]:
                continue
            if k == eng and big and self.big_ok.get(eng) and self.big.get((eng, v)):
                continue
            if wd.get(k, 0) >= v:
                continue
            wd[k] = v
            waits.append((k, v))
        return waits

    def _record(self, sk, val, reads, writes):
        for w in writes:
            self.lw[w] = (sk, val)
            self.rd[w] = {}
        for r in reads:
            if r in writes:
                continue
            m = self.rd.setdefault(r, {})
            if val > m.get(sk, 0):
                m[sk] = val

    def op(self, eng, fn, reads=(), writes=(), semkey=None, inc=1, big=False):
        waits = self._waits(eng, self.deps(reads, writes), big)
        sk = semkey if semkey is not None else eng
        val = self.cnt.get(sk, 0) + inc
        self.cnt[sk] = val
        if big and semkey is None:
            self.big[(eng, val)] = True
        self.prog[eng].append(([(waits, fn)], sk, inc))
        self._record(sk, val, reads, writes)

    def pe_group(self, mms, out_region):
        val = self.cnt.get("pe", 0) + 1
        items = []
        allreads = []
        for i, (fn, reads) in enumerate(mms):
            d = self.deps(reads, [out_region] if i == 0 else [])
            items.append((self._waits("pe", d), fn))
            allreads.extend(reads)
        self.cnt["pe"] = val
        self.prog["pe"].append((items, "pe", 1))
        self._record("pe", val, allreads, [out_region])


def build_program(S):
    NT = S // T
    nc = bass.Bass("TRN2", target_bir_lowering=False)
    xT = nc.dram_tensor("xT", [D, S], F32, kind="ExternalInput").ap()
    wsrc = nc.dram_tensor("wsrc", [128, NSUB * 128], F32, kind="ExternalInput").ap()
    wsm = nc.dram_tensor("wsm", [128, L * 8 * 128], F32, kind="ExternalInput").ap()
    vecs = nc.dram_tensor("vecs", [128, NV], F32, kind="ExternalInput").ap()
    gsg = nc.dram_tensor("gsg", [128, L * 512], F32, kind="ExternalInput").ap()
    bsp = nc.dram_tensor("bsp", [1, L * 4 * 128], F32, kind="ExternalInput").ap()
    cst = nc.dram_tensor("cst", [128, 128 + 64], F32, kind="ExternalInput").ap()
    outT = nc.dram_tensor("outT", [D, S], F32, kind="ExternalOutput").ap()
    wbf = nc.dram_tensor("wbf", [128, NSUB * 128], BF16, kind="Internal").ap()
    dbg = {}
    if DEBUG:
        dbg["y"] = nc.dram_tensor("dbg_y", [128, 12, T], BF16, kind="ExternalOutput").ap()
        dbg["x1"] = nc.dram_tensor("dbg_x1", [128, 8, T], F32, kind="ExternalOutput").ap()
        dbg["act"] = nc.dram_tensor("dbg_act", [128, NFC, T], BF16, kind="ExternalOutput").ap()
        dbg["x2"] = nc.dram_tensor("dbg_x2", [128, 8, T], F32, kind="ExternalOutput").ap()
        dbg["mg"] = nc.dram_tensor("dbg_mg", [128, 8, T], BF16, kind="ExternalOutput").ap()
        dbg["ab"] = nc.dram_tensor("dbg_ab", [128, 4, T + 16], F32, kind="ExternalOutput").ap()
        dbg["pa"] = nc.dram_tensor("dbg_pa", [128, 4, T], BF16, kind="ExternalOutput").ap()
        dbg["h"] = nc.dram_tensor("dbg_h", [128, 8, T], BF16, kind="ExternalOutput").ap()
    xTv = xT.rearrange("(c p) s -> p c s", p=128)
    outTv = outT.rearrange("(c p) s -> p c s", p=128)

    S_ = Sched()
    W = T + 16
    with ExitStack() as st:
        def sb(name, shape, dt):
            return st.enter_context(nc.sbuf_tensor(name, shape, dt))

        xbufs = [sb("xres0", [128, 8, T], F32), sb("xres1", [128, 8, T], F32)]
        hbuf = sb("hbuf", [128, 8, T], BF16)
        rsb = sb("rsb", [128, T], F32)
        ones_bf = sb("ones_bf", [128, 128], BF16)
        wsmall = sb("wsmall", [128, L, 8, 128], BF16)
        bpad = sb("bpad", [128, L, 4, 128], BF16)
        gsgu = sb("gsgu", [128, L * 512], F32)
        vec = sb("vec", [128, NV], F32)
        cst_sb = sb("cst_sb", [128, 192], F32)
        ahalo = sb("ahalo", [128, L, 4, 16], F32)
        zhalo = sb("zhalo", [128, L, 4, 2], F32)
        uph = sb("uph", [128, L * 2, 2 * NFC, 2], F32)
        corrA = sb("corrA", [128, 2 * NFC, 2], F32)
        tmpc = sb("tmpc", [128, 2 * NFC], F32)
        abuf = sb("abuf", [128, 4, W], F32)
        ptmp = sb("ptmp", [128, 2, W], F32)
        pabuf = sb("pabuf", [128, 4, T], BF16)
        ubuf = sb("ubuf", [128, 4, T], F32)
        vstage = sb("vstage", [128, 2, 512], F32)
        vsq = sb("vsq", [128, 512], BF16)
        vss = sb("vss", [128, NQ], F32)
        vrs = sb("vrs", [128, NQ], F32)
        vrr = sb("vrr", [128, NQ], F32)
        vtok = sb("vtok", [128, NQ, 512], BF16)
        cc_sb = sb("cc_sb", [128, 2, T], F32)
        zbuf = sb("zbuf", [128, 2, T + 2], F32)
        cacc = sb("cacc", [128, 2, T], F32)
        ybuf = sb("ybuf", [128, 12, T], BF16)
        gm = sb("gm", [128, 12, T], F32)
        mg = sb("mg", [128, 8, T], BF16)
        actb = sb("actb", [128, NFC, T], BF16)
        facc = sb("facc", [128, 4, T], F32)
        fsg = sb("fsg", [128, 2, T], F32)
        ring = sb("ring", [128, R, CH * 128], BF16)
        ps = [st.enter_context(nc.psum_tensor("ps%d" % i, [128, 512], F32)) for i in range(8)]

        semkeys = ["pe", "act", "dve", "gp", "xl0", "xl1", "ost", "cst0", "cst1", "cst2", "cst3", "cst4"] + \
                  [("wl", s) for s in range(R)] + [("wb", j) for j in range(8)]
        sems = {}
        for i, k in enumerate(semkeys):
            sems[k] = st.enter_context(nc.semaphore("sem%d" % i))

        state = {"bank": 0, "next_load": 0, "reserved": set(), "gpe": "dve"}

        def nextbank():
            while True:
                b = state["bank"]
                state["bank"] = (b + 1) % 8
                if b not in state["reserved"]:
                    return b

        def isbig(ap):
            n = 1
            for d in ap.shape[1:]:
                n *= int(d)
            return n >= 256

        def ACT(out, in_, func, reads, writes, scale=1.0, bias=None, accum=None):
            def fn(e):
                kw = {}
                if bias is not None:
                    kw["bias"] = bias
                if accum is not None:
                    kw["accum_out"] = accum
                return e.activation(out=out, in_=in_, func=func, scale=scale, **kw)
            S_.op("act", fn, reads, writes, big=isbig(out) and accum is None)

        def TT(eng, out, in0, in1, op, reads, writes):
            S_.op(eng, lambda e: e.tensor_tensor(out=out, in0=in0, in1=in1, op=op), reads, writes, big=isbig(out))

        def TS(eng, out, in0, s1, s2, op0, op1, reads, writes):
            if op1 is None:
                S_.op(eng, lambda e: e.tensor_scalar(out=out, in0=in0, scalar1=s1, scalar2=None, op0=op0),
                      reads, writes, big=isbig(out))
            else:
                S_.op(eng, lambda e: e.tensor_scalar(out=out, in0=in0, scalar1=s1, scalar2=s2, op0=op0, op1=op1),
                      reads, writes, big=isbig(out))

        def STT(out, in0, scalar, in1, op0, op1, reads, writes):
            S_.op("dve", lambda e: e.scalar_tensor_tensor(out=out, in0=in0, scalar=scalar, in1=in1,
                                                          op0=op0, op1=op1), reads, writes, big=isbig(out))

        def COPY(eng, out, in_, reads, writes):
            S_.op(eng, lambda e: e.tensor_copy(out=out, in_=in_), reads, writes, big=isbig(out))

        def MEMSET(eng, ap, val, writes):
            S_.op(eng, lambda e: e.memset(ap, val), (), writes)

        def DMA(eng, out, in_, reads, writes, semkey, **kw):
            S_.op(eng, lambda e: e.dma_start(out=out, in_=in_, **kw), reads, writes, semkey=semkey, inc=16)

        def MM(out, lhsT, rhs, start, stop):
            return lambda e: e.matmul(out, lhsT, rhs, start=start, stop=stop)

        total_chunks = NT * NCH

        def wload_upto(gc):
            while state["next_load"] <= min(gc, total_chunks - 1):
                k = state["next_load"]
                slot = k % R
                c = k % NCH
                sl = slice(c * CH * 128, (c + 1) * CH * 128)
                if k < NCH:
                    DMA("gp", ring[:, slot, :], wsrc[:, sl], [], [("w", slot)], ("wl", slot))
                    DMA("sp", wbf[:, sl], ring[:, slot, :], [("w", slot)], [("wbfc", c)], ("wb", c % 8))
                    if k == NCH - 1:
                        for c2 in range(NCH):
                            S_.lw[("wbfc", c2)] = (("wb", c2 % 8), S_.cnt[("wb", c2 % 8)])
                else:
                    DMA("sp", ring[:, slot, :], wbf[:, sl], [("wbfc", c)], [("w", slot)], ("wl", slot))
                state["next_load"] += 1

        def wsub(i, n=1):
            gc = i // CH
            slot = gc % R
            off = i % CH
            assert off + n <= CH
            return ring[:, slot, off * 128:(off + n) * 128], ("w", slot), gc

        def group(bank, specs):
            chunks = [x[2] for sp_ in specs for x in (sp_[1], sp_[2]) if x[2] is not None]
            if chunks:
                wload_upto(max(chunks))
            mms = []
            for (o, lh, rh, s0, s1) in specs:
                mms.append((MM(o, lh[0], rh[0], s0, s1), [lh[1], rh[1]]))
            S_.pe_group(mms, ("ps", bank))
            if chunks:
                wload_upto(min(chunks) + R - 1)

        def proj_group(bank, wi, nk, rhs_of):
            specs = []
            for kc in range(nk):
                specs.append((ps[bank][:, :], wsub(wi + kc), rhs_of(kc), kc == 0, kc == nk - 1))
            group(bank, specs)

        def h_of(kc):
            return (hbuf[:, kc, :], ("h", kc), None)

        DMA("sp", vec[:, :], vecs[:, :], [], [("vec",)], "cst0")
        DMA("sp", gsgu[:, :], gsg[:, :], [], [("gsgu",)], "cst1")
        DMA("sp", cst_sb[:, :], cst[:, :], [], [("cstsb",)], "cst2")
        gmflat = gm[:, 0:4, :]
        DMA("sp", gmflat, wsm.rearrange("p (a b) -> p a b", a=4), [], [("gm", k) for k in range(4)], "cst3")
        MEMSET("dve", ones_bf[:, :], 1.0, [("ones",)])
        MEMSET("dve", ahalo[:, :, :, :], 0.0, [("ahalo", l) for l in range(L)])
        MEMSET("dve", zhalo[:, :, :, :], 0.0, [("zhalo", l, j) for l in range(L) for j in range(4)])
        MEMSET("dve", uph[:, :, :, :], 0.0, [("uph", l, p_) for l in range(L) for p_ in range(2)])
        MEMSET("dve", bpad[:, :, :, :], 0.0, [("bpad",)])
        DMA("gp", bpad[0:1, :, :, :], bsp.rearrange("o (l g t) -> o l g t", l=L, g=4), [], [("bpad",)], "cst4")
        for l in range(L):
            for g in range(8):
                idx = l * 8 + g
                src = gm[:, idx // 4, (idx % 4) * 128:(idx % 4 + 1) * 128]
                if g < 4:
                    COPY("dve", wsmall[:, l, g, :], src, [("gm", idx // 4)], [("wsmall",)])
                else:
                    TT("dve", wsmall[:, l, g, :], src, cst_sb[:, 0:128], ALU.mult,
                       [("gm", idx // 4), ("cstsb",)], [("wsmall",)])

        eps_t = sb("eps_t", [128, 1], F32)
        eps_ap = eps_t[:, 0:1]
        MEMSET("dve", eps_t[:, :], EPS, [("vec2",)])

        def norm_begin():
            bank = nextbank()
            state["reserved"].add(bank)
            return bank

        def norm_chunk(bank, c, xp):
            xres = xbufs[xp]
            sqacc, sqreg = cacc[:, 0, :], ("cacc", 0)
            if c == 0:
                ACT(sqacc, xres[:, c, :], AF.Square, [("x", xp, c)], [sqreg])
            elif c < 7:
                i = c % 2
                ACT(vstage[:, i, :], xres[:, c, :], AF.Square, [("x", xp, c)], [("vst", i)])
                if c < 6:
                    TT(state["gpe"], sqacc, sqacc, vstage[:, i, :], ALU.add, [sqreg, ("vst", i)], [sqreg])
                else:
                    TT(state["gpe"], vsq[:, :], sqacc, vstage[:, i, :], ALU.add, [sqreg, ("vst", i)], [("vsq",)])
                    S_.pe_group([(MM(ps[bank][:, :], ones_bf[:, :], vsq[:, :], True, False),
                                  [("ones",), ("vsq",)])], ("ps", bank))
            else:
                ACT(mg[:, c, :], xres[:, c, :], AF.Square, [("x", xp, c)], [("mg", c)])
                S_.pe_group([(MM(ps[bank][:, :], ones_bf[:, :], mg[:, c, :], False, True), [("ones",), ("mg", c)])],
                            ("ps", bank))

        def norm_finish(bank, gbase, to_out, xp):
            xres = xbufs[xp]
            ACT(rsb[:, :], ps[bank][:, :], AF.Sqrt, [("ps", bank), ("vec2",)], [("rsb",)], scale=1.0 / D,
                bias=eps_ap)
            S_.op("dve", lambda e: e.reciprocal(out=ps[bank][:, :], in_=rsb[:, :]), [("rsb",)], [("ps", bank)],
                  big=True)
            for c in range(8):
                if to_out:
                    dst, reg = gm[:, c, :], ("gm", c)
                else:
                    dst, reg = hbuf[:, c, :], ("h", c)
                STT(dst, xres[:, c, :], vec[:, gbase + c:gbase + c + 1], ps[bank][:, :], ALU.mult, ALU.mult,
                    [("x", xp, c), ("ps", bank), ("vec",)], [reg])
                if to_out:
                    DMA("sp", outTv[:, c, to_out[0]:to_out[0] + T], gm[:, c, :], [("gm", c)], [], "ost")
            if to_out:
                for c in range(8):
                    S_.rd[("gm", c)]["ost"] = S_.cnt["ost"]
            state["reserved"].discard(bank)

        wi = 0
        for it in range(NT):
            t0 = it * T
            par = it % 2
            xp = it % 2
            state["gpe"] = "dve" if it == 0 else GPE
            xres = xbufs[xp]
            if it == 0:
                DMA("sp", xres[:, :, :], xTv[:, :, t0:t0 + T], [], [("x", xp, c) for c in range(8)], "xl%d" % xp)
                nbank = norm_begin()
                for c in range(8):
                    norm_chunk(nbank, c, xp)
                norm_finish(nbank, 0, False, xp)
            def prefetch_x(c):
                sk = "xl%d" % (1 - xp)
                DMA("sp", xbufs[1 - xp][:, c, :], xTv[:, c, t0 + T:t0 + 2 * T], [], [("x", 1 - xp, c)], sk)
                if c == 7:
                    for cc_ in range(8):
                        S_.lw[("x", 1 - xp, cc_)] = (sk, S_.cnt[sk])
            wi = it * NSUB
            for l in range(L):
                vb = l * NVL
                if l > 0:
                    norm_finish(nbank, vb + 0, False, xp)
                COPY(state["gpe"], abuf[:, :, 0:16], ahalo[:, l, :, :], [("ahalo", l)], [("abuf", g) for g in range(4)])
                for g in range(4):
                    bank = nextbank()
                    proj_group(bank, wi, 8, h_of)
                    wi += 8
                    ACT(abuf[:, g, 16:W], ps[bank][:, :], AF.Copy, [("ps", bank)], [("abuf", g)])
                COPY(state["gpe"], ahalo[:, l, :, :], abuf[:, :, T:W], [("abuf", g) for g in range(4)], [("ahalo", l)])
                for g in range(4):
                    cur, cur_reg = abuf[:, g, :], ("abuf", g)
                    sh, k = 1, 0
                    for step in range(g + 1):
                        lo = 2 * sh - 1
                        TT(state["gpe"], ptmp[:, k, lo:W], cur[:, lo:W], cur[:, lo - sh:W - sh], ALU.add,
                           [cur_reg], [("ptmp", k)])
                        cur, cur_reg = ptmp[:, k, :], ("ptmp", k)
                        k ^= 1
                        sh *= 2
                    w = POOL_W[g]
                    TS(state["gpe"], cur[:, 16:W], cur[:, 16:W], 1.0 / w, 0.0, ALU.mult, ALU.add, [cur_reg], [cur_reg])
                    if it == 0:
                        TT(state["gpe"], cur[:, 16:32], cur[:, 16:32], cst_sb[:, 128 + g * 16:128 + (g + 1) * 16], ALU.mult,
                           [cur_reg, ("cstsb",)], [cur_reg])
                    TT(state["gpe"], pabuf[:, g, :], cur[:, 16:W], abuf[:, g, 16:W], ALU.subtract,
                       [cur_reg, ("abuf", g)], [("pa", g)])
                for j in range(4):
                    bank = nextbank()
                    proj_group(bank, wi, 8, h_of)
                    wi += 8
                    ACT(ubuf[:, j, :], ps[bank][:, :], AF.Gelu_apprx_tanh, [("ps", bank)], [("u", j)])
                vbase = wi
                wi += 32
                for q in range(NQ):
                    bank = nextbank()
                    specs = []
                    for kc in range(8):
                        specs.append((ps[bank][:, :], (hbuf[:, kc, q * 128:(q + 1) * 128], ("h", kc), None),
                                      wsub(vbase + kc * 4, 4), kc == 0, kc == 7))
                    group(bank, specs)
                    i = q % 2
                    ACT(vstage[:, i, :], ps[bank][:, :], AF.Gelu_apprx_tanh, [("ps", bank)], [("vst", i)])
                    ACT(vsq[:, :], vstage[:, i, :], AF.Square, [("vst", i)], [("vsq",), ("vss", q)],
                        accum=vss[:, q:q + 1])
                    ACT(vrs[:, q:q + 1], vss[:, q:q + 1], AF.Sqrt, [("vss", q), ("vec2",)], [("vrs", q)],
                        scale=1.0 / 512, bias=eps_ap)
                    S_.op("dve", (lambda q: lambda e: e.reciprocal(out=vrr[:, q:q + 1], in_=vrs[:, q:q + 1]))(q),
                          [("vrs", q)], [("vrr", q)])
                    STT(vtok[:, q, :], vstage[:, i, :], vrr[:, q:q + 1], gsgu[:, l * 512:(l + 1) * 512],
                        ALU.mult, ALU.mult, [("vst", i), ("vrr", q), ("gsgu",)], [("vtok", q)])
                for j in range(4):
                    i = j % 2
                    banks = []
                    for _ in range(3):
                        bank = nextbank()
                        proj_group(bank, wi, 8, h_of)
                        wi += 8
                        banks.append(bank)
                    b_cc, b_cx, b_cb = banks
                    ACT(cc_sb[:, i, :], ps[b_cc][:, :], AF.Copy, [("ps", b_cc)], [("cc", i)])
                    COPY(state["gpe"], zbuf[:, i, 0:2], zhalo[:, l, j, :], [("zhalo", l, j)], [("z", i)])
                    TT("dve", zbuf[:, i, 2:T + 2], ps[b_cx][:, :], cc_sb[:, i, :], ALU.mult,
                       [("ps", b_cx), ("cc", i)], [("z", i)])
                    COPY(state["gpe"], zhalo[:, l, j, :], zbuf[:, i, T:T + 2], [("z", i)], [("zhalo", l, j)])
                    cb = vb + 12
                    TS("dve", cacc[:, i, :], zbuf[:, i, 2:T + 2], vec[:, cb + 8 + j:cb + 9 + j], None, ALU.mult, None,
                       [("z", i), ("vec",)], [("cacc", i)])
                    STT(cacc[:, i, :], zbuf[:, i, 1:T + 1], vec[:, cb + 4 + j:cb + 5 + j], cacc[:, i, :],
                        ALU.mult, ALU.add, [("z", i), ("vec",), ("cacc", i)], [("cacc", i)])
                    STT(cacc[:, i, :], zbuf[:, i, 0:T], vec[:, cb + j:cb + 1 + j], cacc[:, i, :],
                        ALU.mult, ALU.add, [("z", i), ("vec",), ("cacc", i)], [("cacc", i)])
                    TT("dve", ybuf[:, 8 + j, :], ps[b_cb][:, :], cacc[:, i, :], ALU.mult,
                       [("ps", b_cb), ("cacc", i)], [("y", 8 + j)])
                for g in range(4):
                    bank = nextbank()
                    S_.pe_group([(MM(ps[bank][:, :], wsmall[:, l, g, :], pabuf[:, g, :], True, True),
                                  [("wsmall",), ("pa", g)])], ("ps", bank))
                    ACT(ybuf[:, g, :], ps[bank][:, :], AF.Identity, [("ps", bank), ("vec",)], [("y", g)],
                        scale=vec[:, vb + 8 + g:vb + 9 + g])
                for g in range(4):
                    bank = nextbank()
                    mms = []
                    for q in range(NQ):
                        o = ps[bank][:, q * 128:(q + 1) * 128]
                        mms.append((MM(o, vtok[:, q, g * 128:(g + 1) * 128], wsmall[:, l, 4 + g, :], True, False),
                                    [("vtok", q), ("wsmall",)]))
                        mms.append((MM(o, ones_bf[:, :], bpad[:, l, g, :], False, True), [("ones",), ("bpad",)]))
                    S_.pe_group(mms, ("ps", bank))
                    TT("dve", ybuf[:, 4 + g, :], ps[bank][:, :], ubuf[:, g, :], ALU.mult,
                       [("ps", bank), ("u", g)], [("y", 4 + g)])
                for m in range(8):
                    i = m % 2
                    gbanks = []
                    for b in range(3):
                        bank = nextbank()
                        proj_group(bank, wi, 8, h_of)
                        wi += 8
                        gbanks.append(bank)
                    for b in range(3):
                        ACT(gm[:, i * 3 + b, :], ps[gbanks[b]][:, :], AF.Sigmoid, [("ps", gbanks[b])],
                            [("gm", i * 3 + b)])
                    bbanks = []
                    for b in range(3):
                        bank = nextbank()
                        proj_group(bank, wi, 4, (lambda b: lambda kc: (ybuf[:, 4 * b + kc, :], ("y", 4 * b + kc), None))(b))
                        wi += 4
                        bbanks.append(bank)
                    for b in range(3):
                        TT("dve", gm[:, 6 + i * 3 + b, :], ps[bbanks[b]][:, :], gm[:, i * 3 + b, :], ALU.mult,
                           [("ps", bbanks[b]), ("gm", i * 3 + b)], [("gm", 6 + i * 3 + b)])
                    TT(state["gpe"], gm[:, 6 + i * 3, :], gm[:, 6 + i * 3, :], gm[:, 7 + i * 3, :], ALU.add,
                       [("gm", 6 + i * 3), ("gm", 7 + i * 3)], [("gm", 6 + i * 3)])
                    TT(state["gpe"], mg[:, m, :], gm[:, 6 + i * 3, :], gm[:, 8 + i * 3, :], ALU.add,
                       [("gm", 6 + i * 3), ("gm", 8 + i * 3)], [("mg", m)])
                    if l == 0 and it + 1 < NT:
                        prefetch_x(m)
                if DEBUG and it == DBG_IT and l == DBG_L:
                    DMA("sp", dbg["y"][:, :, :], ybuf[:, :, :], [("y", k) for k in range(12)], [], "ost")
                    DMA("sp", dbg["mg"][:, :, :], mg[:, :, :], [("mg", k) for k in range(8)], [], "ost")
                obanks = []
                for m in range(8):
                    bank = nextbank()
                    proj_group(bank, wi, 8, lambda kc: (mg[:, kc, :], ("mg", kc), None))
                    wi += 8
                    obanks.append(bank)
                    TT("dve", xres[:, m, :], ps[bank][:, :], xres[:, m, :], ALU.add,
                       [("ps", bank), ("x", xp, m)], [("x", xp, m)])
                nbank = norm_begin()
                for c in range(8):
                    norm_chunk(nbank, c, xp)
                if DEBUG and it == DBG_IT and l == DBG_L:
                    DMA("sp", dbg["x1"][:, :, :], xres[:, :, :], [("x", xp, k) for k in range(8)], [], "ost")
                norm_finish(nbank, vb + 24, False, xp)
                if DEBUG and it == DBG_IT and l == DBG_L:
                    DMA("sp", dbg["h"][:, :, :], hbuf[:, :, :], [("h", k) for k in range(8)], [], "ost")
                fb = vb + 32
                bb = vb + 164
                hold = ("uph", l, par)
                hnew = ("uph", l, 1 - par)
                uold = uph[:, l * 2 + par, :, :]
                unew = uph[:, l * 2 + 1 - par, :, :]

                def silu_mul(c, i):
                    ACT(fsg[:, i, :], facc[:, i * 2, :], AF.Silu, [("facc", i * 2)], [("fsg", i)])
                    TT(state["gpe"], actb[:, c, :], fsg[:, i, :], facc[:, i * 2 + 1, :], ALU.mult,
                       [("fsg", i), ("facc", i * 2 + 1)], [("act", c)])

                W0 = vec[:, fb:fb + 44]
                W1 = vec[:, fb + 44:fb + 88]
                TT("dve", tmpc[:, :], uold[:, :, 1], W1, ALU.mult, [hold, ("vec",)], [("tmpc",)])
                TT("dve", corrA[:, :, 0], uold[:, :, 0], W0, ALU.mult, [hold, ("vec",)], [("corrA",)])
                TT("dve", corrA[:, :, 0], corrA[:, :, 0], tmpc[:, :], ALU.add, [("corrA",), ("tmpc",)], [("corrA",)])
                TT("dve", corrA[:, :, 1], uold[:, :, 1], W0, ALU.mult, [hold, ("vec",)], [("corrA",)])
                prev = None
                for c in range(NFC):
                    i = c % 2
                    for gv, cc in enumerate((c, NFC + c)):
                        bank = nextbank()
                        proj_group(bank, wi, 8, h_of)
                        wi += 8
                        fr = ("facc", i * 2 + gv)
                        acc = facc[:, i * 2 + gv, :]
                        pb = ps[bank]
                        w0 = vec[:, fb + cc:fb + cc + 1]
                        w1 = vec[:, fb + 44 + cc:fb + 45 + cc]
                        w2 = vec[:, fb + 88 + cc:fb + 89 + cc]
                        ACT(acc, pb[:, :], AF.Identity, [("ps", bank), ("vec",)], [fr], scale=w2,
                            bias=vec[:, bb + cc:bb + cc + 1])
                        ACT(unew[:, cc, :], pb[:, T - 2:T], AF.Copy, [("ps", bank)], [hnew])
                        TT("dve", acc[:, 0:2], acc[:, 0:2], corrA[:, cc, :], ALU.add, [fr, ("corrA",)], [fr])
                        STT(acc[:, 1:T], pb[:, 0:T - 1], w1, acc[:, 1:T], ALU.mult, ALU.add,
                            [("ps", bank), ("vec",), fr], [fr])
                        STT(acc[:, 2:T], pb[:, 0:T - 2], w0, acc[:, 2:T], ALU.mult, ALU.add,
                            [("ps", bank), ("vec",), fr], [fr])
                    if prev is not None:
                        silu_mul(*prev)
                    prev = (c, i)
                silu_mul(*prev)
                if l == L - 1 and it + 1 < NT:
                    nb2 = norm_begin()
                    for c in range(8):
                        norm_chunk(nb2, c, 1 - xp)
                    norm_finish(nb2, 0, False, 1 - xp)
                for m in range(8):
                    bank = nextbank()
                    proj_group(bank, wi, NFC, lambda kc: (actb[:, kc, :], ("act", kc), None))
                    wi += NFC
                    TT("dve", xres[:, m, :], ps[bank][:, :], xres[:, m, :], ALU.add,
                       [("ps", bank), ("x", xp, m)], [("x", xp, m)])
                nbank = norm_begin()
                for c in range(8):
                    norm_chunk(nbank, c, xp)
                if DEBUG and it == DBG_IT and l == DBG_L:
                    DMA("sp", dbg["act"][:, :, :], actb[:, :, :], [("act", k) for k in range(NFC)], [], "ost")
                    DMA("sp", dbg["x2"][:, :, :], xres[:, :, :], [("x", xp, k) for k in range(8)], [], "ost")
            assert wi == (it + 1) * NSUB, (wi, it)
            norm_finish(nbank, NVL * L, (t0,), xp)

        S_.prog["sp"].append(([([("ost", S_.cnt["ost"])], None)], None, 0))

        with nc.Block() as block:
            def make(engname):
                def body(e):
                    for items, sk, inc in S_.prog[engname]:
                        last = None
                        for waits, fn in items:
                            for k, v in waits:
                                e.wait_ge(sems[k], v)
                            if fn is not None:
                                last = fn(e)
                        if last is not None and inc:
                            last.then_inc(sems[sk], inc)
                return body
            block.tensor(make("pe"))
            block.scalar(make("act"))
            block.vector(make("dve"))
            block.gpsimd(make("gp"))
            block.sync(make("sp"))
    return nc


def _fm(v):
    v = np.asarray(v, np.float32)
    return v.reshape(-1, 128).T


def prep_weights(inp):
    w_in, w_up, w_down, w_o = inp["w_in"], inp["w_up"], inp["w_down"], inp["w_o"]
    wbr = (inp["w_branch_a"], inp["w_branch_b"], inp["w_branch_c"])
    wsrc = np.empty((NSUB, 128, 128), np.float32)
    n = 0
    for l in range(L):
        Wl = w_in[l]

        def cols(Wm, col0, nk):
            nonlocal n
            blk = Wm[:nk * 128, col0:col0 + 128].reshape(nk, 128, 128)
            wsrc[n:n + nk] = blk
            n += nk
        for g in range(4):
            cols(Wl, g * 128, 8)
        for j in range(4):
            cols(Wl, 512 + j * 128, 8)
        for kc in range(8):
            for q4 in range(4):
                wsrc[n] = Wl[kc * 128:(kc + 1) * 128, 1024 + q4 * 128:1024 + (q4 + 1) * 128]
                n += 1
        for j in range(4):
            for off in (2048, 2560, 1536):
                cols(Wl, off + j * 128, 8)
        for m in range(8):
            for off in (3072, 4096, 5120):
                cols(Wl, off + m * 128, 8)
            for b in range(3):
                cols(wbr[b][l], m * 128, 4)
        for m in range(8):
            cols(w_o[l], m * 128, 8)
        for c in range(NFC):
            for cc in (c, NFC + c):
                cols(w_up[l], cc * 128, 8)
        for m in range(8):
            cols(w_down[l], m * 128, NFC)
    assert n == NSUB
    wsrc = np.ascontiguousarray(wsrc.transpose(1, 0, 2)).reshape(128, NSUB * 128)

    wsm = np.empty((L, 8, 128, 128), np.float32)
    for l in range(L):
        for g in range(4):
            wsm[l, g] = inp["w_pool"][l, g]
            wsm[l, 4 + g] = inp["w_spatial"][l, g].T
    wsm = np.ascontiguousarray(wsm.reshape(L * 8, 128, 128).transpose(1, 0, 2)).reshape(128, L * 8 * 128)

    vecs = np.empty((128, NV), np.float32)
    for l in range(L):
        vb = l * NVL
        vecs[:, vb:vb + 8] = _fm(inp["g_mix"][l])
        vecs[:, vb + 8:vb + 12] = _fm(inp["pool_scale"][l])
        for tap in range(3):
            vecs[:, vb + 12 + tap * 4:vb + 16 + tap * 4] = _fm(inp["conv_c"][l, tap])
        vecs[:, vb + 24:vb + 32] = _fm(inp["g_ffn"][l])
        for tap in range(3):
            vecs[:, vb + 32 + tap * 44:vb + 32 + (tap + 1) * 44] = _fm(inp["conv_ffn"][l, tap])
        vecs[:, vb + 164:vb + 208] = _fm(inp["conv_ffn_b"][l])
    vecs[:, NVL * L:NVL * L + 8] = _fm(inp["g_final"])

    gsg = np.ascontiguousarray(np.broadcast_to(np.asarray(inp["g_sgu"], np.float32).reshape(1, L * 512), (128, L * 512)))
    bspv = np.ascontiguousarray(np.asarray(inp["b_spatial"], np.float32).reshape(1, L * 4 * 128))
    cst = np.zeros((128, 192), np.float32)
    s_idx = np.arange(128)[:, None]
    t_idx = np.arange(128)[None, :]
    cst[:, 0:128] = (t_idx >= s_idx).astype(np.float32)
    for g, w in enumerate(POOL_W):
        tt = np.arange(16)
        cst[:, 128 + g * 16:128 + (g + 1) * 16] = (w / np.minimum(tt + 1, w)).astype(np.float32)[None, :]
    return dict(wsrc=wsrc, wsm=wsm, vecs=vecs, gsg=gsg, bsp=bspv, cst=cst)


_CACHE = {}


def run(inputs, S=SEQ, ncores=NCORES, trace=False):
    inp = {k: np.asarray(v) for k, v in inputs.items()}
    common = prep_weights(inp)
    x = inp["x"]
    in_maps = []
    for b in range(ncores):
        m = dict(common)
        m["xT"] = np.ascontiguousarray(x[b, :S, :].T)
        in_maps.append(m)
    if S not in _CACHE:
        _CACHE[S] = build_program(S)
    nc = _CACHE[S]
    res = run_bass_kernel_spmd(nc, in_maps, core_ids=list(range(ncores)), trace=trace)
    out = np.stack([np.ascontiguousarray(r["outT"].T) for r in res.results], axis=0)
    return out.astype(np.float32), res


def kernel(**inputs):
    out, _ = run(inputs)
    return out
```

```python
import numpy as np
from contextlib import ExitStack
import concourse.bass as bass
import concourse.mybir as mybir
from concourse.bass_utils import run_bass_kernel_spmd

F32 = mybir.dt.float32
BF16 = mybir.dt.bfloat16
AF = mybir.ActivationFunctionType
ALU = mybir.AluOpType

D = 1024
L = 2
T = 512
NQ = T // 128
DFF = 2816
NFC = DFF // 128
EPS = 1e-6
SEQ = 8192
DEBUG = False
GPE = "gp"
DBG_IT, DBG_L = 0, 0
NCORES = 8
NSUB_L = 1072
NSUB = NSUB_L * L
CH = 16
NCH = NSUB // CH
R = 6
PIECE = 8
NPIECE = (NCH + PIECE - 1) // PIECE
NVL = 208
NV = NVL * L + 8
POOL_W = (2, 4, 8, 16)

ENGS = ("pe", "act", "dve", "gp", "sp")


class Sched:
    def __init__(self):
        self.prog = {e: [] for e in ENGS}
        self.cnt = {}
        self.waited = {e: {} for e in ENGS}
        self.lw = {}
        self.rd = {}
        self.same_sync = {"pe": False, "act": True, "dve": True, "gp": True, "sp": False}
        self.big_ok = {"act": True, "dve": True}
        self.big = {}

    def deps(self, reads, writes):
        d = {}
        for r in reads:
            x = self.lw.get(r)
            if x is not None and x[1] > d.get(x[0], 0):
                d[x[0]] = x[1]
        for w in writes:
            x = self.lw.get(w)
            if x is not None and x[1] > d.get(x[0], 0):
                d[x[0]] = x[1]
            for k, v in self.rd.get(w, {}).items():
                if v > d.get(k, 0):
                    d[k] = v
        return d

    def _waits(self, eng, d, big=False):
        waits = []
        wd = self.waited[eng]
        for k, v in d.items():
            if k == eng and not self.same_sync[eng]:
                continue
            if k == eng and big and self.big_ok.get(eng) and self.big.get((eng, v)):
                continue
            if wd.get(k, 0) >= v:
                continue
            wd[k] = v
            waits.append((k, v))
        return waits

    def _record(self, sk, val, reads, writes):
        for w in writes:
            self.lw[w] = (sk, val)
            self.rd[w] = {}
        for r in reads:
            if r in writes:
                continue
            m = self.rd.setdefault(r, {})
            if val > m.get(sk, 0):
                m[sk] = val

    def op(self, eng, fn, reads=(), writes=(), semkey=None, inc=1, big=False):
        waits = self._waits(eng, self.deps(reads, writes), big)
        sk = semkey if semkey is not None else eng
        val = self.cnt.get(sk, 0) + inc
        self.cnt[sk] = val
        if big and semkey is None:
            self.big[(eng, val)] = True
        self.prog[eng].append(([(waits, fn)], sk, inc))
        self._record(sk, val, reads, writes)

    def pe_group(self, mms, out_region):
        val = self.cnt.get("pe", 0) + 1
        items = []
        allreads = []
        for i, (fn, reads) in enumerate(mms):
            d = self.deps(reads, [out_region] if i == 0 else [])
            items.append((self._waits("pe", d), fn))
            allreads.extend(reads)
        self.cnt["pe"] = val
        self.prog["pe"].append((items, "pe", 1))
        self._record("pe", val, allreads, [out_region])


def build_program(S):
    NT = S // T
    nc = bass.Bass("TRN2", target_bir_lowering=False)
    xT = nc.dram_tensor("xT", [D, S], F32, kind="ExternalInput").ap()
    wsrc = nc.dram_tensor("wsrc", [128, NSUB * 128], F32, kind="ExternalInput").ap()
    wsm = nc.dram_tensor("wsm", [128, L * 8 * 128], F32, kind="ExternalInput").ap()
    vecs = nc.dram_tensor("vecs", [128, NV], F32, kind="ExternalInput").ap()
    gsg = nc.dram_tensor("gsg", [128, L * 512], F32, kind="ExternalInput").ap()
    bsp = nc.dram_tensor("bsp", [1, L * 4 * 128], F32, kind="ExternalInput").ap()
    cst = nc.dram_tensor("cst", [128, 128 + 64], F32, kind="ExternalInput").ap()
    outT = nc.dram_tensor("outT", [D, S], F32, kind="ExternalOutput").ap()
    wbf = nc.dram_tensor("wbf", [128, NSUB * 128], BF16, kind="Internal").ap()
    dbg = {}
    if DEBUG:
        dbg["y"] = nc.dram_tensor("dbg_y", [128, 12, T], BF16, kind="ExternalOutput").ap()
        dbg["x1"] = nc.dram_tensor("dbg_x1", [128, 8, T], F32, kind="ExternalOutput").ap()
        dbg["act"] = nc.dram_tensor("dbg_act", [128, NFC, T], BF16, kind="ExternalOutput").ap()
        dbg["x2"] = nc.dram_tensor("dbg_x2", [128, 8, T], F32, kind="ExternalOutput").ap()
        dbg["mg"] = nc.dram_tensor("dbg_mg", [128, 8, T], BF16, kind="ExternalOutput").ap()
        dbg["ab"] = nc.dram_tensor("dbg_ab", [128, 4, T + 16], F32, kind="ExternalOutput").ap()
        dbg["pa"] = nc.dram_tensor("dbg_pa", [128, 4, T], BF16, kind="ExternalOutput").ap()
        dbg["h"] = nc.dram_tensor("dbg_h", [128, 8, T], BF16, kind="ExternalOutput").ap()
    xTv = xT.rearrange("(c p) s -> p c s", p=128)
    outTv = outT.rearrange("(c p) s -> p c s", p=128)

    S_ = Sched()
    W = T + 16
    with ExitStack() as st:
        def sb(name, shape, dt):
            return st.enter_context(nc.sbuf_tensor(name, shape, dt))

        xbufs = [sb("xres0", [128, 8, T], F32), sb("xres1", [128, 8, T], F32)]
        hbuf = sb("hbuf", [128, 8, T], BF16)
        rsb = sb("rsb", [128, T], F32)
        ones_bf = sb("ones_bf", [128, 128], BF16)
        wsmall = sb("wsmall", [128, L, 8, 128], BF16)
        bpad = sb("bpad", [128, L, 4, 128], BF16)
        gsgu = sb("gsgu", [128, L * 512], F32)
        vec = sb("vec", [128, NV], F32)
        cst_sb = sb("cst_sb", [128, 192], F32)
        ahalo = sb("ahalo", [128, L, 4, 16], F32)
        zhalo = sb("zhalo", [128, L, 4, 2], F32)
        uph = sb("uph", [128, L * 2, 2 * NFC, 2], F32)
        corrA = sb("corrA", [128, 2 * NFC, 2], F32)
        tmpc = sb("tmpc", [128, 2 * NFC], F32)
        abuf = sb("abuf", [128, 4, W], F32)
        ptmp = sb("ptmp", [128, 2, W], F32)
        pabuf = sb("pabuf", [128, 4, T], BF16)
        ubuf = sb("ubuf", [128, 4, T], F32)
        vstage = sb("vstage", [128, 2, 512], F32)
        vsq = sb("vsq", [128, 512], BF16)
        vss = sb("vss", [128, NQ], F32)
        vrs = sb("vrs", [128, NQ], F32)
        vrr = sb("vrr", [128, NQ], F32)
        vtok = sb("vtok", [128, NQ, 512], BF16)
        cc_sb = sb("cc_sb", [128, 2, T], F32)
        zbuf = sb("zbuf", [128, 2, T + 2], F32)
        cacc = sb("cacc", [128, 2, T], F32)
        ybuf = sb("ybuf", [128, 12, T], BF16)
        gm = sb("gm", [128, 12, T], F32)
        mg = sb("mg", [128, 8, T], BF16)
        actb = sb("actb", [128, NFC, T], BF16)
        facc = sb("facc", [128, 4, T], F32)
        fsg = sb("fsg", [128, 2, T], F32)
        ring = sb("ring", [128, R, CH * 128], BF16)
        ps = [st.enter_context(nc.psum_tensor("ps%d" % i, [128, 512], F32)) for i in range(8)]

        semkeys = ["pe", "act", "dve", "gp", "xl0", "xl1", "ost", "cst0", "cst1", "cst2", "cst3", "cst4"] + \
                  [("wl", s) for s in range(R)] + [("wb", j) for j in range(8)]
        sems = {}
        for i, k in enumerate(semkeys):
            sems[k] = st.enter_context(nc.semaphore("sem%d" % i))

        state = {"bank": 0, "next_load": 0, "reserved": set(), "gpe": "dve"}

        def nextbank():
            while True:
                b = state["bank"]
                state["bank"] = (b + 1) % 8
                if b not in state["reserved"]:
                    return b

        def isbig(ap):
            n = 1
            for d in ap.shape[1:]:
                n *= int(d)
            return n >= 256

        def ACT(out, in_, func, reads, writes, scale=1.0, bias=None, accum=None):
            def fn(e):
                kw = {}
                if bias is not None:
                    kw["bias"] = bias
                if accum is not None:
                    kw["accum_out"] = accum
                return e.activation(out=out, in_=in_, func=func, scale=scale, **kw)
            S_.op("act", fn, reads, writes, big=isbig(out) and accum is None)

        def TT(eng, out, in0, in1, op, reads, writes):
            S_.op(eng, lambda e: e.tensor_tensor(out=out, in0=in0, in1=in1, op=op), reads, writes, big=isbig(out))

        def TS(eng, out, in0, s1, s2, op0, op1, reads, writes):
            if op1 is None:
                S_.op(eng, lambda e: e.tensor_scalar(out=out, in0=in0, scalar1=s1, scalar2=None, op0=op0),
                      reads, writes, big=isbig(out))
            else:
                S_.op(eng, lambda e: e.tensor_scalar(out=out, in0=in0, scalar1=s1, scalar2=s2, op0=op0, op1=op1),
                      reads, writes, big=isbig(out))

        def STT(out, in0, scalar, in1, op0, op1, reads, writes):
            S_.op("dve", lambda e: e.scalar_tensor_tensor(out=out, in0=in0, scalar=scalar, in1=in1,
                                                          op0=op0, op1=op1), reads, writes, big=isbig(out))

        def COPY(eng, out, in_, reads, writes):
            S_.op(eng, lambda e: e.tensor_copy(out=out, in_=in_), reads, writes, big=isbig(out))

        def MEMSET(eng, ap, val, writes):
            S_.op(eng, lambda e: e.memset(ap, val), (), writes)

        def DMA(eng, out, in_, reads, writes, semkey, **kw):
            S_.op(eng, lambda e: e.dma_start(out=out, in_=in_, **kw), reads, writes, semkey=semkey, inc=16)

        def MM(out, lhsT, rhs, start, stop):
            return lambda e: e.matmul(out, lhsT, rhs, start=start, stop=stop)

        total_chunks = NT * NCH

        def wload_upto(gc):
            while state["next_load"] <= min(gc, total_chunks - 1):
                k = state["next_load"]
                slot = k % R
                c = k % NCH
                sl = slice(c * CH * 128, (c + 1) * CH * 128)
                if k < NCH:
                    DMA("gp", ring[:, slot, :], wsrc[:, sl], [], [("w", slot)], ("wl", slot))
                    DMA("sp", wbf[:, sl], ring[:, slot, :], [("w", slot)], [("wbfc", c)], ("wb", c % 8))
                    if k == NCH - 1:
                        for c2 in range(NCH):
                            S_.lw[("wbfc", c2)] = (("wb", c2 % 8), S_.cnt[("wb", c2 % 8)])
                else:
                    DMA("sp", ring[:, slot, :], wbf[:, sl], [("wbfc", c)], [("w", slot)], ("wl", slot))
                state["next_load"] += 1

        def wsub(i, n=1):
            gc = i // CH
            slot = gc % R
            off = i % CH
            assert off + n <= CH
            return ring[:, slot, off * 128:(off + n) * 128], ("w", slot), gc

        def group(bank, specs):
            chunks = [x[2] for sp_ in specs for x in (sp_[1], sp_[2]) if x[2] is not None]
            if chunks:
                wload_upto(max(chunks))
            mms = []
            for (o, lh, rh, s0, s1) in specs:
                mms.append((MM(o, lh[0], rh[0], s0, s1), [lh[1], rh[1]]))
            S_.pe_group(mms, ("ps", bank))
            if chunks:
                wload_upto(min(chunks) + R - 1)

        def proj_group(bank, wi, nk, rhs_of):
            specs = []
            for kc in range(nk):
                specs.append((ps[bank][:, :], wsub(wi + kc), rhs_of(kc), kc == 0, kc == nk - 1))
            group(bank, specs)

        def h_of(kc):
            return (hbuf[:, kc, :], ("h", kc), None)

        DMA("sp", vec[:, :], vecs[:, :], [], [("vec",)], "cst0")
        DMA("sp", gsgu[:, :], gsg[:, :], [], [("gsgu",)], "cst1")
        DMA("sp", cst_sb[:, :], cst[:, :], [], [("cstsb",)], "cst2")
        gmflat = gm[:, 0:4, :]
        DMA("sp", gmflat, wsm.rearrange("p (a b) -> p a b", a=4), [], [("gm", k) for k in range(4)], "cst3")
        MEMSET("dve", ones_bf[:, :], 1.0, [("ones",)])
        MEMSET("dve", ahalo[:, :, :, :], 0.0, [("ahalo", l) for l in range(L)])
        MEMSET("dve", zhalo[:, :, :, :], 0.0, [("zhalo", l, j) for l in range(L) for j in range(4)])
        MEMSET("dve", uph[:, :, :, :], 0.0, [("uph", l, p_) for l in range(L) for p_ in range(2)])
        MEMSET("dve", bpad[:, :, :, :], 0.0, [("bpad",)])
        DMA("gp", bpad[0:1, :, :, :], bsp.rearrange("o (l g t) -> o l g t", l=L, g=4), [], [("bpad",)], "cst4")
        for l in range(L):
            for g in range(8):
                idx = l * 8 + g
                src = gm[:, idx // 4, (idx % 4) * 128:(idx % 4 + 1) * 128]
                if g < 4:
                    COPY("dve", wsmall[:, l, g, :], src, [("gm", idx // 4)], [("wsmall",)])
                else:
                    TT("dve", wsmall[:, l, g, :], src, cst_sb[:, 0:128], ALU.mult,
                       [("gm", idx // 4), ("cstsb",)], [("wsmall",)])

        eps_t = sb("eps_t", [128, 1], F32)
        eps_ap = eps_t[:, 0:1]
        MEMSET("dve", eps_t[:, :], EPS, [("vec2",)])

        def norm_begin():
            bank = nextbank()
            state["reserved"].add(bank)
            return bank

        def norm_chunk(bank, c, xp):
            xres = xbufs[xp]
            sqacc, sqreg = cacc[:, 0, :], ("cacc", 0)
            if c == 0:
                ACT(sqacc, xres[:, c, :], AF.Square, [("x", xp, c)], [sqreg])
            elif c < 7:
                i = c % 2
                ACT(vstage[:, i, :], xres[:, c, :], AF.Square, [("x", xp, c)], [("vst", i)])
                if c < 6:
                    TT(state["gpe"], sqacc, sqacc, vstage[:, i, :], ALU.add, [sqreg, ("vst", i)], [sqreg])
                else:
                    TT(state["gpe"], vsq[:, :], sqacc, vstage[:, i, :], ALU.add, [sqreg, ("vst", i)], [("vsq",)])
                    S_.pe_group([(MM(ps[bank][:, :], ones_bf[:, :], vsq[:, :], True, False),
                                  [("ones",), ("vsq",)])], ("ps", bank))
            else:
                ACT(mg[:, c, :], xres[:, c, :], AF.Square, [("x", xp, c)], [("mg", c)])
                S_.pe_group([(MM(ps[bank][:, :], ones_bf[:, :], mg[:, c, :], False, True), [("ones",), ("mg", c)])],
                            ("ps", bank))

        def norm_finish(bank, gbase, to_out, xp):
            xres = xbufs[xp]
            ACT(rsb[:, :], ps[bank][:, :], AF.Ln, [("ps", bank), ("vec2",)], [("rsb",)], scale=1.0 / D,
                bias=eps_ap)
            ACT(ps[bank][:, :], rsb[:, :], AF.Exp, [("rsb",)], [("ps", bank)], scale=-0.5)
            for c in range(8):
                if to_out:
                    dst, reg = gm[:, c, :], ("gm", c)
                else:
                    dst, reg = hbuf[:, c, :], ("h", c)
                STT(dst, xres[:, c, :], vec[:, gbase + c:gbase + c + 1], ps[bank][:, :], ALU.mult, ALU.mult,
                    [("x", xp, c), ("ps", bank), ("vec",)], [reg])
                if to_out:
                    DMA("sp", outTv[:, c, to_out[0]:to_out[0] + T], gm[:, c, :], [("gm", c)], [], "ost")
            if to_out:
                for c in range(8):
                    S_.rd[("gm", c)]["ost"] = S_.cnt["ost"]
            state["reserved"].discard(bank)

        wi = 0
        for it in range(NT):
            t0 = it * T
            par = it % 2
            xp = it % 2
            state["gpe"] = "dve" if it == 0 else GPE
            xres = xbufs[xp]
            if it == 0:
                DMA("sp", xres[:, :, :], xTv[:, :, t0:t0 + T], [], [("x", xp, c) for c in range(8)], "xl%d" % xp)
                nbank = norm_begin()
                for c in range(8):
                    norm_chunk(nbank, c, xp)
                norm_finish(nbank, 0, False, xp)
            def prefetch_x(c):
                sk = "xl%d" % (1 - xp)
                DMA("sp", xbufs[1 - xp][:, c, :], xTv[:, c, t0 + T:t0 + 2 * T], [], [("x", 1 - xp, c)], sk)
                if c == 7:
                    for cc_ in range(8):
                        S_.lw[("x", 1 - xp, cc_)] = (sk, S_.cnt[sk])
            wi = it * NSUB
            for l in range(L):
                vb = l * NVL
                if l > 0:
                    norm_finish(nbank, vb + 0, False, xp)
                COPY(state["gpe"], abuf[:, :, 0:16], ahalo[:, l, :, :], [("ahalo", l)], [("abuf", g) for g in range(4)])
                for g in range(4):
                    bank = nextbank()
                    proj_group(bank, wi, 8, h_of)
                    wi += 8
                    ACT(abuf[:, g, 16:W], ps[bank][:, :], AF.Copy, [("ps", bank)], [("abuf", g)])
                COPY(state["gpe"], ahalo[:, l, :, :], abuf[:, :, T:W], [("abuf", g) for g in range(4)], [("ahalo", l)])
                for g in range(4):
                    cur, cur_reg = abuf[:, g, :], ("abuf", g)
                    sh, k = 1, 0
                    for step in range(g + 1):
                        lo = 2 * sh - 1
                        TT(state["gpe"], ptmp[:, k, lo:W], cur[:, lo:W], cur[:, lo - sh:W - sh], ALU.add,
                           [cur_reg], [("ptmp", k)])
                        cur, cur_reg = ptmp[:, k, :], ("ptmp", k)
                        k ^= 1
                        sh *= 2
                    w = POOL_W[g]
                    TS(state["gpe"], cur[:, 16:W], cur[:, 16:W], 1.0 / w, 0.0, ALU.mult, ALU.add, [cur_reg], [cur_reg])
                    if it == 0:
                        TT(state["gpe"], cur[:, 16:32], cur[:, 16:32], cst_sb[:, 128 + g * 16:128 + (g + 1) * 16], ALU.mult,
                           [cur_reg, ("cstsb",)], [cur_reg])
                    TT(state["gpe"], pabuf[:, g, :], cur[:, 16:W], abuf[:, g, 16:W], ALU.subtract,
                       [cur_reg, ("abuf", g)], [("pa", g)])
                for j in range(4):
                    bank = nextbank()
                    proj_group(bank, wi, 8, h_of)
                    wi += 8
                    ACT(ubuf[:, j, :], ps[bank][:, :], AF.Gelu_apprx_tanh, [("ps", bank)], [("u", j)])
                vbase = wi
                wi += 32
                for q in range(NQ):
                    bank = nextbank()
                    specs = []
                    for kc in range(8):
                        specs.append((ps[bank][:, :], (hbuf[:, kc, q * 128:(q + 1) * 128], ("h", kc), None),
                                      wsub(vbase + kc * 4, 4), kc == 0, kc == 7))
                    group(bank, specs)
                    i = q % 2
                    ACT(vstage[:, i, :], ps[bank][:, :], AF.Gelu_apprx_tanh, [("ps", bank)], [("vst", i)])
                    ACT(vsq[:, :], vstage[:, i, :], AF.Square, [("vst", i)], [("vsq",), ("vss", q)],
                        accum=vss[:, q:q + 1])
                    ACT(vrs[:, q:q + 1], vss[:, q:q + 1], AF.Sqrt, [("vss", q), ("vec2",)], [("vrs", q)],
                        scale=1.0 / 512, bias=eps_ap)
                    S_.op("dve", (lambda q: lambda e: e.reciprocal(out=vrr[:, q:q + 1], in_=vrs[:, q:q + 1]))(q),
                          [("vrs", q)], [("vrr", q)])
                    STT(vtok[:, q, :], vstage[:, i, :], vrr[:, q:q + 1], gsgu[:, l * 512:(l + 1) * 512],
                        ALU.mult, ALU.mult, [("vst", i), ("vrr", q), ("gsgu",)], [("vtok", q)])
                for j in range(4):
                    i = j % 2
                    banks = []
                    for _ in range(3):
                        bank = nextbank()
                        proj_group(bank, wi, 8, h_of)
                        wi += 8
                        banks.append(bank)
                    b_cc, b_cx, b_cb = banks
                    ACT(cc_sb[:, i, :], ps[b_cc][:, :], AF.Copy, [("ps", b_cc)], [("cc", i)])
                    COPY(state["gpe"], zbuf[:, i, 0:2], zhalo[:, l, j, :], [("zhalo", l, j)], [("z", i)])
                    TT("dve", zbuf[:, i, 2:T + 2], ps[b_cx][:, :], cc_sb[:, i, :], ALU.mult,
                       [("ps", b_cx), ("cc", i)], [("z", i)])
                    COPY(state["gpe"], zhalo[:, l, j, :], zbuf[:, i, T:T + 2], [("z", i)], [("zhalo", l, j)])
                    cb = vb + 12
                    TS("dve", cacc[:, i, :], zbuf[:, i, 2:T + 2], vec[:, cb + 8 + j:cb + 9 + j], None, ALU.mult, None,
                       [("z", i), ("vec",)], [("cacc", i)])
                    STT(cacc[:, i, :], zbuf[:, i, 1:T + 1], vec[:, cb + 4 + j:cb + 5 + j], cacc[:, i, :],
                        ALU.mult, ALU.add, [("z", i), ("vec",), ("cacc", i)], [("cacc", i)])
                    STT(cacc[:, i, :], zbuf[:, i, 0:T], vec[:, cb + j:cb + 1 + j], cacc[:, i, :],
                        ALU.mult, ALU.add, [("z", i), ("vec",), ("cacc", i)], [("cacc", i)])
                    TT("dve", ybuf[:, 8 + j, :], ps[b_cb][:, :], cacc[:, i, :], ALU.mult,
                       [("ps", b_cb), ("cacc", i)], [("y", 8 + j)])
                for g in range(4):
                    bank = nextbank()
                    S_.pe_group([(MM(ps[bank][:, :], wsmall[:, l, g, :], pabuf[:, g, :], True, True),
                                  [("wsmall",), ("pa", g)])], ("ps", bank))
                    ACT(ybuf[:, g, :], ps[bank][:, :], AF.Identity, [("ps", bank), ("vec",)], [("y", g)],
                        scale=vec[:, vb + 8 + g:vb + 9 + g])
                for g in range(4):
                    bank = nextbank()
                    mms = []
                    for q in range(NQ):
                        o = ps[bank][:, q * 128:(q + 1) * 128]
                        mms.append((MM(o, vtok[:, q, g * 128:(g + 1) * 128], wsmall[:, l, 4 + g, :], True, False),
                                    [("vtok", q), ("wsmall",)]))
                        mms.append((MM(o, ones_bf[:, :], bpad[:, l, g, :], False, True), [("ones",), ("bpad",)]))
                    S_.pe_group(mms, ("ps", bank))
                    TT("dve", ybuf[:, 4 + g, :], ps[bank][:, :], ubuf[:, g, :], ALU.mult,
                       [("ps", bank), ("u", g)], [("y", 4 + g)])
                for m in range(8):
                    i = m % 2
                    gbanks = []
                    for b in range(3):
                        bank = nextbank()
                        proj_group(bank, wi, 8, h_of)
                        wi += 8
                        gbanks.append(bank)
                    for b in range(3):
                        ACT(gm[:, i * 3 + b, :], ps[gbanks[b]][:, :], AF.Sigmoid, [("ps", gbanks[b])],
                            [("gm", i * 3 + b)])
                    bbanks = []
                    for b in range(3):
                        bank = nextbank()
                        proj_group(bank, wi, 4, (lambda b: lambda kc: (ybuf[:, 4 * b + kc, :], ("y", 4 * b + kc), None))(b))
                        wi += 4
                        bbanks.append(bank)
                    for b in range(3):
                        TT("dve", gm[:, 6 + i * 3 + b, :], ps[bbanks[b]][:, :], gm[:, i * 3 + b, :], ALU.mult,
                           [("ps", bbanks[b]), ("gm", i * 3 + b)], [("gm", 6 + i * 3 + b)])
                    TT(state["gpe"], gm[:, 6 + i * 3, :], gm[:, 6 + i * 3, :], gm[:, 7 + i * 3, :], ALU.add,
                       [("gm", 6 + i * 3), ("gm", 7 + i * 3)], [("gm", 6 + i * 3)])
                    TT(state["gpe"], mg[:, m, :], gm[:, 6 + i * 3, :], gm[:, 8 + i * 3, :], ALU.add,
                       [("gm", 6 + i * 3), ("gm", 8 + i * 3)], [("mg", m)])
                    if l == 0 and it + 1 < NT:
                        prefetch_x(m)
                if DEBUG and it == DBG_IT and l == DBG_L:
                    DMA("sp", dbg["y"][:, :, :], ybuf[:, :, :], [("y", k) for k in range(12)], [], "ost")
                    DMA("sp", dbg["mg"][:, :, :], mg[:, :, :], [("mg", k) for k in range(8)], [], "ost")
                obanks = []
                for m in range(8):
                    bank = nextbank()
                    proj_group(bank, wi, 8, lambda kc: (mg[:, kc, :], ("mg", kc), None))
                    wi += 8
                    obanks.append(bank)
                    TT("dve", xres[:, m, :], ps[bank][:, :], xres[:, m, :], ALU.add,
                       [("ps", bank), ("x", xp, m)], [("x", xp, m)])
                nbank = norm_begin()
                for c in range(8):
                    norm_chunk(nbank, c, xp)
                if DEBUG and it == DBG_IT and l == DBG_L:
                    DMA("sp", dbg["x1"][:, :, :], xres[:, :, :], [("x", xp, k) for k in range(8)], [], "ost")
                norm_finish(nbank, vb + 24, False, xp)
                if DEBUG and it == DBG_IT and l == DBG_L:
                    DMA("sp", dbg["h"][:, :, :], hbuf[:, :, :], [("h", k) for k in range(8)], [], "ost")
                fb = vb + 32
                bb = vb + 164
                hold = ("uph", l, par)
                hnew = ("uph", l, 1 - par)
                uold = uph[:, l * 2 + par, :, :]
                unew = uph[:, l * 2 + 1 - par, :, :]

                def silu_mul(c, i):
                    ACT(fsg[:, i, :], facc[:, i * 2, :], AF.Silu, [("facc", i * 2)], [("fsg", i)])
                    TT(state["gpe"], actb[:, c, :], fsg[:, i, :], facc[:, i * 2 + 1, :], ALU.mult,
                       [("fsg", i), ("facc", i * 2 + 1)], [("act", c)])

                W0 = vec[:, fb:fb + 44]
                W1 = vec[:, fb + 44:fb + 88]
                TT("dve", tmpc[:, :], uold[:, :, 1], W1, ALU.mult, [hold, ("vec",)], [("tmpc",)])
                TT("dve", corrA[:, :, 0], uold[:, :, 0], W0, ALU.mult, [hold, ("vec",)], [("corrA",)])
                TT("dve", corrA[:, :, 0], corrA[:, :, 0], tmpc[:, :], ALU.add, [("corrA",), ("tmpc",)], [("corrA",)])
                TT("dve", corrA[:, :, 1], uold[:, :, 1], W0, ALU.mult, [hold, ("vec",)], [("corrA",)])
                prev = None
                for c in range(NFC):
                    i = c % 2
                    for gv, cc in enumerate((c, NFC + c)):
                        bank = nextbank()
                        proj_group(bank, wi, 8, h_of)
                        wi += 8
                        fr = ("facc", i * 2 + gv)
                        acc = facc[:, i * 2 + gv, :]
                        pb = ps[bank]
                        w0 = vec[:, fb + cc:fb + cc + 1]
                        w1 = vec[:, fb + 44 + cc:fb + 45 + cc]
                        w2 = vec[:, fb + 88 + cc:fb + 89 + cc]
                        ACT(acc, pb[:, :], AF.Identity, [("ps", bank), ("vec",)], [fr], scale=w2,
                            bias=vec[:, bb + cc:bb + cc + 1])
                        ACT(unew[:, cc, :], pb[:, T - 2:T], AF.Copy, [("ps", bank)], [hnew])
                        TT("dve", acc[:, 0:2], acc[:, 0:2], corrA[:, cc, :], ALU.add, [fr, ("corrA",)], [fr])
                        STT(acc[:, 1:T], pb[:, 0:T - 1], w1, acc[:, 1:T], ALU.mult, ALU.add,
                            [("ps", bank), ("vec",), fr], [fr])
                        STT(acc[:, 2:T], pb[:, 0:T - 2], w0, acc[:, 2:T], ALU.mult, ALU.add,
                            [("ps", bank), ("vec",), fr], [fr])
                    if prev is not None:
                        silu_mul(*prev)
                    prev = (c, i)
                silu_mul(*prev)
                if l == L - 1 and it + 1 < NT:
                    nb2 = norm_begin()
                    for c in range(8):
                        norm_chunk(nb2, c, 1 - xp)
                    norm_finish(nb2, 0, False, 1 - xp)
                for m in range(8):
                    bank = nextbank()
                    proj_group(bank, wi, NFC, lambda kc: (actb[:, kc, :], ("act", kc), None))
                    wi += NFC
                    TT("dve", xres[:, m, :], ps[bank][:, :], xres[:, m, :], ALU.add,
                       [("ps", bank), ("x", xp, m)], [("x", xp, m)])
                nbank = norm_begin()
                for c in range(8):
                    norm_chunk(nbank, c, xp)
                if DEBUG and it == DBG_IT and l == DBG_L:
                    DMA("sp", dbg["act"][:, :, :], actb[:, :, :], [("act", k) for k in range(NFC)], [], "ost")
                    DMA("sp", dbg["x2"][:, :, :], xres[:, :, :], [("x", xp, k) for k in range(8)], [], "ost")
            assert wi == (it + 1) * NSUB, (wi, it)
            norm_finish(nbank, NVL * L, (t0,), xp)

        S_.prog["sp"].append(([([("ost", S_.cnt["ost"])], None)], None, 0))

        with nc.Block() as block:
            def make(engname):
                def body(e):
                    for items, sk, inc in S_.prog[engname]:
                        last = None
                        for waits, fn in items:
                            for k, v in waits:
                                e.wait_ge(sems[k], v)
                            if fn is not None:
                                last = fn(e)
                        if last is not None and inc:
                            last.then_inc(sems[sk], inc)
                return body
            block.tensor(make("pe"))
            block.scalar(make("act"))
            block.vector(make("dve"))
            block.gpsimd(make("gp"))
            block.sync(make("sp"))
    return nc


def _fm(v):
    v = np.asarray(v, np.float32)
    return v.reshape(-1, 128).T


def prep_weights(inp):
    w_in, w_up, w_down, w_o = inp["w_in"], inp["w_up"], inp["w_down"], inp["w_o"]
    wbr = (inp["w_branch_a"], inp["w_branch_b"], inp["w_branch_c"])
    wsrc = np.empty((NSUB, 128, 128), np.float32)
    n = 0
    for l in range(L):
        Wl = w_in[l]

        def cols(Wm, col0, nk):
            nonlocal n
            blk = Wm[:nk * 128, col0:col0 + 128].reshape(nk, 128, 128)
            wsrc[n:n + nk] = blk
            n += nk
        for g in range(4):
            cols(Wl, g * 128, 8)
        for j in range(4):
            cols(Wl, 512 + j * 128, 8)
        for kc in range(8):
            for q4 in range(4):
                wsrc[n] = Wl[kc * 128:(kc + 1) * 128, 1024 + q4 * 128:1024 + (q4 + 1) * 128]
                n += 1
        for j in range(4):
            for off in (2048, 2560, 1536):
                cols(Wl, off + j * 128, 8)
        for m in range(8):
            for off in (3072, 4096, 5120):
                cols(Wl, off + m * 128, 8)
            for b in range(3):
                cols(wbr[b][l], m * 128, 4)
        for m in range(8):
            cols(w_o[l], m * 128, 8)
        for c in range(NFC):
            for cc in (c, NFC + c):
                cols(w_up[l], cc * 128, 8)
        for m in range(8):
            cols(w_down[l], m * 128, NFC)
    assert n == NSUB
    wsrc = np.ascontiguousarray(wsrc.transpose(1, 0, 2)).reshape(128, NSUB * 128)

    wsm = np.empty((L, 8, 128, 128), np.float32)
    for l in range(L):
        for g in range(4):
            wsm[l, g] = inp["w_pool"][l, g]
            wsm[l, 4 + g] = inp["w_spatial"][l, g].T
    wsm = np.ascontiguousarray(wsm.reshape(L * 8, 128, 128).transpose(1, 0, 2)).reshape(128, L * 8 * 128)

    vecs = np.empty((128, NV), np.float32)
    for l in range(L):
        vb = l * NVL
        vecs[:, vb:vb + 8] = _fm(inp["g_mix"][l])
        vecs[:, vb + 8:vb + 12] = _fm(inp["pool_scale"][l])
        for tap in range(3):
            vecs[:, vb + 12 + tap * 4:vb + 16 + tap * 4] = _fm(inp["conv_c"][l, tap])
        vecs[:, vb + 24:vb + 32] = _fm(inp["g_ffn"][l])
        for tap in range(3):
            vecs[:, vb + 32 + tap * 44:vb + 32 + (tap + 1) * 44] = _fm(inp["conv_ffn"][l, tap])
        vecs[:, vb + 164:vb + 208] = _fm(inp["conv_ffn_b"][l])
    vecs[:, NVL * L:NVL * L + 8] = _fm(inp["g_final"])

    gsg = np.ascontiguousarray(np.broadcast_to(np.asarray(inp["g_sgu"], np.float32).reshape(1, L * 512), (128, L * 512)))
    bspv = np.ascontiguousarray(np.asarray(inp["b_spatial"], np.float32).reshape(1, L * 4 * 128))
    cst = np.zeros((128, 192), np.float32)
    s_idx = np.arange(128)[:, None]
    t_idx = np.arange(128)[None, :]
    cst[:, 0:128] = (t_idx >= s_idx).astype(np.float32)
    for g, w in enumerate(POOL_W):
        tt = np.arange(16)
        cst[:, 128 + g * 16:128 + (g + 1) * 16] = (w / np.minimum(tt + 1, w)).astype(np.float32)[None, :]
    return dict(wsrc=wsrc, wsm=wsm, vecs=vecs, gsg=gsg, bsp=bspv, cst=cst)


_CACHE = {}


def run(inputs, S=SEQ, ncores=NCORES, trace=False):
    inp = {k: np.asarray(v) for k, v in inputs.items()}
    common = prep_weights(inp)
    x = inp["x"]
    in_maps = []
    for b in range(ncores):
        m = dict(common)
        m["xT"] = np.ascontiguousarray(x[b, :S, :].T)
        in_maps.append(m)
    if S not in _CACHE:
        _CACHE[S] = build_program(S)
    nc = _CACHE[S]
    res = run_bass_kernel_spmd(nc, in_maps, core_ids=list(range(ncores)), trace=trace)
    out = np.stack([np.ascontiguousarray(r["outT"].T) for r in res.results], axis=0)
    return out.astype(np.float32), res


def kernel(**inputs):
    out, _ = run(inputs)
    return out
```

```python
import numpy as np
from contextlib import ExitStack
import concourse.bass as bass
import concourse.mybir as mybir
from concourse.bass_utils import run_bass_kernel_spmd

F32 = mybir.dt.float32
BF16 = mybir.dt.bfloat16
AF = mybir.ActivationFunctionType
ALU = mybir.AluOpType

D = 1024
L = 2
T = 512
NQ = T // 128
DFF = 2816
NFC = DFF // 128
EPS = 1e-6
SEQ = 8192
DEBUG = False
GPE = "gp"
DBG_IT, DBG_L = 0, 0
NCORES = 8
NSUB_L = 1072
NSUB = NSUB_L * L
CH = 16
NCH = NSUB // CH
R = 6
PIECE = 8
NPIECE = (NCH + PIECE - 1) // PIECE
NVL = 208
NV = NVL * L + 8
POOL_W = (2, 4, 8, 16)

ENGS = ("pe", "act", "dve", "gp", "sp")


class Sched:
    def __init__(self):
        self.prog = {e: [] for e in ENGS}
        self.cnt = {}
        self.waited = {e: {} for e in ENGS}
        self.lw = {}
        self.rd = {}
        self.same_sync = {"pe": False, "act": True, "dve": True, "gp": True, "sp": False}
        self.big_ok = {"act": True, "dve": True}
        self.big = {}

    def deps(self, reads, writes):
        d = {}
        for r in reads:
            x = self.lw.get(r)
            if x is not None and x[1] > d.get(x[0], 0):
                d[x[0]] = x[1]
        for w in writes:
            x = self.lw.get(w)
            if x is not None and x[1] > d.get(x[0], 0):
                d[x[0]] = x[1]
            for k, v in self.rd.get(w, {}).items():
                if v > d.get(k, 0):
                    d[k] = v
        return d

    def _waits(self, eng, d, big=False):
        waits = []
        wd = self.waited[eng]
        for k, v in d.items():
            if k == eng and not self.same_sync[eng]:
                continue
            if k == eng and big and self.big_ok.get(eng) and self.big.get((eng, v)):
                continue
            if wd.get(k, 0) >= v:
                continue
            wd[k] = v
            waits.append((k, v))
        return waits

    def _record(self, sk, val, reads, writes):
        for w in writes:
            self.lw[w] = (sk, val)
            self.rd[w] = {}
        for r in reads:
            if r in writes:
                continue
            m = self.rd.setdefault(r, {})
            if val > m.get(sk, 0):
                m[sk] = val

    def op(self, eng, fn, reads=(), writes=(), semkey=None, inc=1, big=False):
        waits = self._waits(eng, self.deps(reads, writes), big)
        sk = semkey if semkey is not None else eng
        val = self.cnt.get(sk, 0) + inc
        self.cnt[sk] = val
        if big and semkey is None:
            self.big[(eng, val)] = True
        self.prog[eng].append(([(waits, fn)], sk, inc))
        self._record(sk, val, reads, writes)

    def pe_group(self, mms, out_region):
        val = self.cnt.get("pe", 0) + 1
        items = []
        allreads = []
        for i, (fn, reads) in enumerate(mms):
            d = self.deps(reads, [out_region] if i == 0 else [])
            items.append((self._waits("pe", d), fn))
            allreads.extend(reads)
        self.cnt["pe"] = val
        self.prog["pe"].append((items, "pe", 1))
        self._record("pe", val, allreads, [out_region])


def build_program(S):
    NT = S // T
    nc = bass.Bass("TRN2", target_bir_lowering=False)
    xT = nc.dram_tensor("xT", [D, S], F32, kind="ExternalInput").ap()
    wsrc = nc.dram_tensor("wsrc", [128, NSUB * 128], F32, kind="ExternalInput").ap()
    wsm = nc.dram_tensor("wsm", [128, L * 8 * 128], F32, kind="ExternalInput").ap()
    vecs = nc.dram_tensor("vecs", [128, NV], F32, kind="ExternalInput").ap()
    gsg = nc.dram_tensor("gsg", [128, L * 512], F32, kind="ExternalInput").ap()
    bsp = nc.dram_tensor("bsp", [1, L * 4 * 128], F32, kind="ExternalInput").ap()
    cst = nc.dram_tensor("cst", [128, 128 + 64], F32, kind="ExternalInput").ap()
    outT = nc.dram_tensor("outT", [D, S], F32, kind="ExternalOutput").ap()
    wbf = nc.dram_tensor("wbf", [128, NSUB * 128], BF16, kind="Internal").ap()
    dbg = {}
    if DEBUG:
        dbg["y"] = nc.dram_tensor("dbg_y", [128, 12, T], BF16, kind="ExternalOutput").ap()
        dbg["x1"] = nc.dram_tensor("dbg_x1", [128, 8, T], F32, kind="ExternalOutput").ap()
        dbg["act"] = nc.dram_tensor("dbg_act", [128, NFC, T], BF16, kind="ExternalOutput").ap()
        dbg["x2"] = nc.dram_tensor("dbg_x2", [128, 8, T], F32, kind="ExternalOutput").ap()
        dbg["mg"] = nc.dram_tensor("dbg_mg", [128, 8, T], BF16, kind="ExternalOutput").ap()
        dbg["ab"] = nc.dram_tensor("dbg_ab", [128, 4, T + 16], F32, kind="ExternalOutput").ap()
        dbg["pa"] = nc.dram_tensor("dbg_pa", [128, 4, T], BF16, kind="ExternalOutput").ap()
        dbg["h"] = nc.dram_tensor("dbg_h", [128, 8, T], BF16, kind="ExternalOutput").ap()
    xTv = xT.rearrange("(c p) s -> p c s", p=128)
    outTv = outT.rearrange("(c p) s -> p c s", p=128)

    S_ = Sched()
    W = T + 16
    with ExitStack() as st:
        def sb(name, shape, dt):
            return st.enter_context(nc.sbuf_tensor(name, shape, dt))

        xbufs = [sb("xres0", [128, 8, T], F32), sb("xres1", [128, 8, T], F32)]
        hbuf = sb("hbuf", [128, 8, T], BF16)
        rsb = sb("rsb", [128, T], F32)
        ones_bf = sb("ones_bf", [128, 128], BF16)
        wsmall = sb("wsmall", [128, L, 8, 128], BF16)
        bpad = sb("bpad", [128, L, 4, 128], BF16)
        gsgu = sb("gsgu", [128, L * 512], F32)
        vec = sb("vec", [128, NV], F32)
        cst_sb = sb("cst_sb", [128, 192], F32)
        ahalo = sb("ahalo", [128, L, 4, 16], F32)
        zhalo = sb("zhalo", [128, L, 4, 2], F32)
        uph = sb("uph", [128, L * 2, 2 * NFC, 2], F32)
        corrA = sb("corrA", [128, 2 * NFC, 2], F32)
        tmpc = sb("tmpc", [128, 2 * NFC], F32)
        abuf = sb("abuf", [128, 4, W], F32)
        ptmp = sb("ptmp", [128, 2, W], F32)
        pabuf = sb("pabuf", [128, 4, T], BF16)
        ubuf = sb("ubuf", [128, 4, T], F32)
        vstage = sb("vstage", [128, 2, 512], F32)
        vsq = sb("vsq", [128, 512], BF16)
        vss = sb("vss", [128, NQ], F32)
        vrs = sb("vrs", [128, NQ], F32)
        vrr = sb("vrr", [128, NQ], F32)
        vtok = sb("vtok", [128, NQ, 512], BF16)
        cc_sb = sb("cc_sb", [128, 2, T], F32)
        zbuf = sb("zbuf", [128, 2, T + 2], F32)
        cacc = sb("cacc", [128, 2, T], F32)
        ybuf = sb("ybuf", [128, 12, T], BF16)
        gm = sb("gm", [128, 12, T], F32)
        mg = sb("mg", [128, 8, T], BF16)
        actb = sb("actb", [128, NFC, T], BF16)
        facc = sb("facc", [128, 4, T], F32)
        fsg = sb("fsg", [128, 2, T], F32)
        ring = sb("ring", [128, R, CH * 128], BF16)
        ps = [st.enter_context(nc.psum_tensor("ps%d" % i, [128, 512], F32)) for i in range(8)]

        semkeys = ["pe", "act", "dve", "gp", "xl0", "xl1", "ost", "cst0", "cst1", "cst2", "cst3", "cst4"] + \
                  [("wl", s) for s in range(R)] + [("wb", j) for j in range(8)]
        sems = {}
        for i, k in enumerate(semkeys):
            sems[k] = st.enter_context(nc.semaphore("sem%d" % i))

        state = {"bank": 0, "next_load": 0, "reserved": set(), "gpe": "dve"}

        def nextbank():
            while True:
                b = state["bank"]
                state["bank"] = (b + 1) % 8
                if b not in state["reserved"]:
                    return b

        def isbig(ap):
            n = 1
            for d in ap.shape[1:]:
                n *= int(d)
            return n >= 256

        def ACT(out, in_, func, reads, writes, scale=1.0, bias=None, accum=None):
            def fn(e):
                kw = {}
                if bias is not None:
                    kw["bias"] = bias
                if accum is not None:
                    kw["accum_out"] = accum
                return e.activation(out=out, in_=in_, func=func, scale=scale, **kw)
            S_.op("act", fn, reads, writes, big=isbig(out) and accum is None)

        def TT(eng, out, in0, in1, op, reads, writes):
            S_.op(eng, lambda e: e.tensor_tensor(out=out, in0=in0, in1=in1, op=op), reads, writes, big=isbig(out))

        def TS(eng, out, in0, s1, s2, op0, op1, reads, writes):
            if op1 is None:
                S_.op(eng, lambda e: e.tensor_scalar(out=out, in0=in0, scalar1=s1, scalar2=None, op0=op0),
                      reads, writes, big=isbig(out))
            else:
                S_.op(eng, lambda e: e.tensor_scalar(out=out, in0=in0, scalar1=s1, scalar2=s2, op0=op0, op1=op1),
                      reads, writes, big=isbig(out))

        def STT(out, in0, scalar, in1, op0, op1, reads, writes):
            S_.op("dve", lambda e: e.scalar_tensor_tensor(out=out, in0=in0, scalar=scalar, in1=in1,
                                                          op0=op0, op1=op1), reads, writes, big=isbig(out))

        def COPY(eng, out, in_, reads, writes):
            S_.op(eng, lambda e: e.tensor_copy(out=out, in_=in_), reads, writes, big=isbig(out))

        def MEMSET(eng, ap, val, writes):
            S_.op(eng, lambda e: e.memset(ap, val), (), writes)

        def DMA(eng, out, in_, reads, writes, semkey, **kw):
            S_.op(eng, lambda e: e.dma_start(out=out, in_=in_, **kw), reads, writes, semkey=semkey, inc=16)

        def MM(out, lhsT, rhs, start, stop):
            return lambda e: e.matmul(out, lhsT, rhs, start=start, stop=stop)

        total_chunks = NT * NCH

        def wload_upto(gc):
            while state["next_load"] <= min(gc, total_chunks - 1):
                k = state["next_load"]
                slot = k % R
                c = k % NCH
                sl = slice(c * CH * 128, (c + 1) * CH * 128)
                if k < NCH:
                    DMA("gp", ring[:, slot, :], wsrc[:, sl], [], [("w", slot)], ("wl", slot))
                    DMA("sp", wbf[:, sl], ring[:, slot, :], [("w", slot)], [("wbfc", c)], ("wb", c % 8))
                    if k == NCH - 1:
                        for c2 in range(NCH):
                            S_.lw[("wbfc", c2)] = (("wb", c2 % 8), S_.cnt[("wb", c2 % 8)])
                else:
                    DMA("sp", ring[:, slot, :], wbf[:, sl], [("wbfc", c)], [("w", slot)], ("wl", slot))
                state["next_load"] += 1

        def wsub(i, n=1):
            gc = i // CH
            slot = gc % R
            off = i % CH
            assert off + n <= CH
            return ring[:, slot, off * 128:(off + n) * 128], ("w", slot), gc

        def group(bank, specs):
            chunks = [x[2] for sp_ in specs for x in (sp_[1], sp_[2]) if x[2] is not None]
            if chunks:
                wload_upto(max(chunks))
            mms = []
            for (o, lh, rh, s0, s1) in specs:
                mms.append((MM(o, lh[0], rh[0], s0, s1), [lh[1], rh[1]]))
            S_.pe_group(mms, ("ps", bank))
            if chunks:
                wload_upto(min(chunks) + R - 1)

        def proj_group(bank, wi, nk, rhs_of):
            specs = []
            for kc in range(nk):
                specs.append((ps[bank][:, :], wsub(wi + kc), rhs_of(kc), kc == 0, kc == nk - 1))
            group(bank, specs)

        def h_of(kc):
            return (hbuf[:, kc, :], ("h", kc), None)

        DMA("sp", vec[:, :], vecs[:, :], [], [("vec",)], "cst0")
        DMA("sp", gsgu[:, :], gsg[:, :], [], [("gsgu",)], "cst1")
        DMA("sp", cst_sb[:, :], cst[:, :], [], [("cstsb",)], "cst2")
        gmflat = gm[:, 0:4, :]
        DMA("sp", gmflat, wsm.rearrange("p (a b) -> p a b", a=4), [], [("gm", k) for k in range(4)], "cst3")
        MEMSET("dve", ones_bf[:, :], 1.0, [("ones",)])
        MEMSET("dve", ahalo[:, :, :, :], 0.0, [("ahalo", l) for l in range(L)])
        MEMSET("dve", zhalo[:, :, :, :], 0.0, [("zhalo", l, j) for l in range(L) for j in range(4)])
        MEMSET("dve", uph[:, :, :, :], 0.0, [("uph", l, p_) for l in range(L) for p_ in range(2)])
        MEMSET("dve", bpad[:, :, :, :], 0.0, [("bpad",)])
        DMA("gp", bpad[0:1, :, :, :], bsp.rearrange("o (l g t) -> o l g t", l=L, g=4), [], [("bpad",)], "cst4")
        for l in range(L):
            for g in range(8):
                idx = l * 8 + g
                src = gm[:, idx // 4, (idx % 4) * 128:(idx % 4 + 1) * 128]
                if g < 4:
                    COPY("dve", wsmall[:, l, g, :], src, [("gm", idx // 4)], [("wsmall",)])
                else:
                    TT("dve", wsmall[:, l, g, :], src, cst_sb[:, 0:128], ALU.mult,
                       [("gm", idx // 4), ("cstsb",)], [("wsmall",)])

        eps_t = sb("eps_t", [128, 1], F32)
        eps_ap = eps_t[:, 0:1]
        MEMSET("dve", eps_t[:, :], EPS, [("vec2",)])

        def norm_begin():
            bank = nextbank()
            state["reserved"].add(bank)
            return bank

        def norm_chunk(bank, c, xp):
            xres = xbufs[xp]
            sqacc, sqreg = cacc[:, 0, :], ("cacc", 0)
            if c == 0:
                ACT(sqacc, xres[:, c, :], AF.Square, [("x", xp, c)], [sqreg])
            elif c < 7:
                i = c % 2
                ACT(vstage[:, i, :], xres[:, c, :], AF.Square, [("x", xp, c)], [("vst", i)])
                if c < 6:
                    TT(state["gpe"], sqacc, sqacc, vstage[:, i, :], ALU.add, [sqreg, ("vst", i)], [sqreg])
                else:
                    TT(state["gpe"], vsq[:, :], sqacc, vstage[:, i, :], ALU.add, [sqreg, ("vst", i)], [("vsq",)])
                    S_.pe_group([(MM(ps[bank][:, :], ones_bf[:, :], vsq[:, :], True, False),
                                  [("ones",), ("vsq",)])], ("ps", bank))
            else:
                ACT(mg[:, c, :], xres[:, c, :], AF.Square, [("x", xp, c)], [("mg", c)])
                S_.pe_group([(MM(ps[bank][:, :], ones_bf[:, :], mg[:, c, :], False, True), [("ones",), ("mg", c)])],
                            ("ps", bank))

        def norm_finish(bank, gbase, to_out, xp):
            xres = xbufs[xp]
            ACT(rsb[:, :], ps[bank][:, :], AF.Ln, [("ps", bank), ("vec2",)], [("rsb",)], scale=1.0 / D,
                bias=eps_ap)
            ACT(ps[bank][:, :], rsb[:, :], AF.Exp, [("rsb",)], [("ps", bank)], scale=-0.5)
            for c in range(8):
                if to_out:
                    dst, reg = gm[:, c, :], ("gm", c)
                else:
                    dst, reg = hbuf[:, c, :], ("h", c)
                STT(dst, xres[:, c, :], vec[:, gbase + c:gbase + c + 1], ps[bank][:, :], ALU.mult, ALU.mult,
                    [("x", xp, c), ("ps", bank), ("vec",)], [reg])
                if to_out:
                    DMA("sp", outTv[:, c, to_out[0]:to_out[0] + T], gm[:, c, :], [("gm", c)], [], "ost")
            if to_out:
                for c in range(8):
                    S_.rd[("gm", c)]["ost"] = S_.cnt["ost"]
            state["reserved"].discard(bank)

        wi = 0
        for it in range(NT):
            t0 = it * T
            par = it % 2
            xp = it % 2
            state["gpe"] = "dve" if it == 0 else GPE
            xres = xbufs[xp]
            if it == 0:
                DMA("sp", xres[:, :, :], xTv[:, :, t0:t0 + T], [], [("x", xp, c) for c in range(8)], "xl%d" % xp)
                nbank = norm_begin()
                for c in range(8):
                    norm_chunk(nbank, c, xp)
                norm_finish(nbank, 0, False, xp)
            def prefetch_x(c):
                sk = "xl%d" % (1 - xp)
                DMA("sp", xbufs[1 - xp][:, c, :], xTv[:, c, t0 + T:t0 + 2 * T], [], [("x", 1 - xp, c)], sk)
                if c == 7:
                    for cc_ in range(8):
                        S_.lw[("x", 1 - xp, cc_)] = (sk, S_.cnt[sk])
            wi = it * NSUB
            for l in range(L):
                vb = l * NVL
                if l > 0:
                    norm_finish(nbank, vb + 0, False, xp)
                COPY(state["gpe"], abuf[:, :, 0:16], ahalo[:, l, :, :], [("ahalo", l)], [("abuf", g) for g in range(4)])
                for g in range(4):
                    bank = nextbank()
                    proj_group(bank, wi, 8, h_of)
                    wi += 8
                    ACT(abuf[:, g, 16:W], ps[bank][:, :], AF.Copy, [("ps", bank)], [("abuf", g)])
                COPY(state["gpe"], ahalo[:, l, :, :], abuf[:, :, T:W], [("abuf", g) for g in range(4)], [("ahalo", l)])
                for g in range(4):
                    cur, cur_reg = abuf[:, g, :], ("abuf", g)
                    sh, k = 1, 0
                    for step in range(g + 1):
                        lo = 2 * sh - 1
                        TT(state["gpe"], ptmp[:, k, lo:W], cur[:, lo:W], cur[:, lo - sh:W - sh], ALU.add,
                           [cur_reg], [("ptmp", k)])
                        cur, cur_reg = ptmp[:, k, :], ("ptmp", k)
                        k ^= 1
                        sh *= 2
                    w = POOL_W[g]
                    TS(state["gpe"], cur[:, 16:W], cur[:, 16:W], 1.0 / w, 0.0, ALU.mult, ALU.add, [cur_reg], [cur_reg])
                    if it == 0:
                        TT(state["gpe"], cur[:, 16:32], cur[:, 16:32], cst_sb[:, 128 + g * 16:128 + (g + 1) * 16], ALU.mult,
                           [cur_reg, ("cstsb",)], [cur_reg])
                    TT(state["gpe"], pabuf[:, g, :], cur[:, 16:W], abuf[:, g, 16:W], ALU.subtract,
                       [cur_reg, ("abuf", g)], [("pa", g)])
                for j in range(4):
                    bank = nextbank()
                    proj_group(bank, wi, 8, h_of)
                    wi += 8
                    ACT(ubuf[:, j, :], ps[bank][:, :], AF.Gelu_apprx_tanh, [("ps", bank)], [("u", j)])
                vbase = wi
                wi += 32
                for q in range(NQ):
                    bank = nextbank()
                    specs = []
                    for kc in range(8):
                        specs.append((ps[bank][:, :], (hbuf[:, kc, q * 128:(q + 1) * 128], ("h", kc), None),
                                      wsub(vbase + kc * 4, 4), kc == 0, kc == 7))
                    group(bank, specs)
                    i = q % 2
                    ACT(vstage[:, i, :], ps[bank][:, :], AF.Gelu_apprx_tanh, [("ps", bank)], [("vst", i)])
                    ACT(vsq[:, :], vstage[:, i, :], AF.Square, [("vst", i)], [("vsq",), ("vss", q)],
                        accum=vss[:, q:q + 1])
                    ACT(vrs[:, q:q + 1], vss[:, q:q + 1], AF.Sqrt, [("vss", q), ("vec2",)], [("vrs", q)],
                        scale=1.0 / 512, bias=eps_ap)
                    S_.op("dve", (lambda q: lambda e: e.reciprocal(out=vrr[:, q:q + 1], in_=vrs[:, q:q + 1]))(q),
                          [("vrs", q)], [("vrr", q)])
                    STT(vtok[:, q, :], vstage[:, i, :], vrr[:, q:q + 1], gsgu[:, l * 512:(l + 1) * 512],
                        ALU.mult, ALU.mult, [("vst", i), ("vrr", q), ("gsgu",)], [("vtok", q)])
                for j in range(4):
                    i = j % 2
                    banks = []
                    for _ in range(3):
                        bank = nextbank()
                        proj_group(bank, wi, 8, h_of)
                        wi += 8
                        banks.append(bank)
                    b_cc, b_cx, b_cb = banks
                    ACT(cc_sb[:, i, :], ps[b_cc][:, :], AF.Copy, [("ps", b_cc)], [("cc", i)])
                    COPY(state["gpe"], zbuf[:, i, 0:2], zhalo[:, l, j, :], [("zhalo", l, j)], [("z", i)])
                    TT("dve", zbuf[:, i, 2:T + 2], ps[b_cx][:, :], cc_sb[:, i, :], ALU.mult,
                       [("ps", b_cx), ("cc", i)], [("z", i)])
                    COPY(state["gpe"], zhalo[:, l, j, :], zbuf[:, i, T:T + 2], [("z", i)], [("zhalo", l, j)])
                    cb = vb + 12
                    TS("dve", cacc[:, i, :], zbuf[:, i, 2:T + 2], vec[:, cb + 8 + j:cb + 9 + j], None, ALU.mult, None,
                       [("z", i), ("vec",)], [("cacc", i)])
                    STT(cacc[:, i, :], zbuf[:, i, 1:T + 1], vec[:, cb + 4 + j:cb + 5 + j], cacc[:, i, :],
                        ALU.mult, ALU.add, [("z", i), ("vec",), ("cacc", i)], [("cacc", i)])
                    STT(cacc[:, i, :], zbuf[:, i, 0:T], vec[:, cb + j:cb + 1 + j], cacc[:, i, :],
                        ALU.mult, ALU.add, [("z", i), ("vec",), ("cacc", i)], [("cacc", i)])
                    TT("dve", ybuf[:, 8 + j, :], ps[b_cb][:, :], cacc[:, i, :], ALU.mult,
                       [("ps", b_cb), ("cacc", i)], [("y", 8 + j)])
                for g in range(4):
                    bank = nextbank()
                    S_.pe_group([(MM(ps[bank][:, :], wsmall[:, l, g, :], pabuf[:, g, :], True, True),
                                  [("wsmall",), ("pa", g)])], ("ps", bank))
                    ACT(ybuf[:, g, :], ps[bank][:, :], AF.Identity, [("ps", bank), ("vec",)], [("y", g)],
                        scale=vec[:, vb + 8 + g:vb + 9 + g])
                for g in range(4):
                    bank = nextbank()
                    mms = []
                    for q in range(NQ):
                        o = ps[bank][:, q * 128:(q + 1) * 128]
                        mms.append((MM(o, vtok[:, q, g * 128:(g + 1) * 128], wsmall[:, l, 4 + g, :], True, False),
                                    [("vtok", q), ("wsmall",)]))
                        mms.append((MM(o, ones_bf[:, :], bpad[:, l, g, :], False, True), [("ones",), ("bpad",)]))
                    S_.pe_group(mms, ("ps", bank))
                    TT("dve", ybuf[:, 4 + g, :], ps[bank][:, :], ubuf[:, g, :], ALU.mult,
                       [("ps", bank), ("u", g)], [("y", 4 + g)])
                for m in range(8):
                    i = m % 2
                    gbanks = []
                    for b in range(3):
                        bank = nextbank()
                        proj_group(bank, wi, 8, h_of)
                        wi += 8
                        gbanks.append(bank)
                    for b in range(3):
                        ACT(gm[:, i * 3 + b, :], ps[gbanks[b]][:, :], AF.Sigmoid, [("ps", gbanks[b])],
                            [("gm", i * 3 + b)])
                    bbanks = []
                    for b in range(3):
                        bank = nextbank()
                        proj_group(bank, wi, 4, (lambda b: lambda kc: (ybuf[:, 4 * b + kc, :], ("y", 4 * b + kc), None))(b))
                        wi += 4
                        bbanks.append(bank)
                    for b in range(3):
                        TT("dve", gm[:, 6 + i * 3 + b, :], ps[bbanks[b]][:, :], gm[:, i * 3 + b, :], ALU.mult,
                           [("ps", bbanks[b]), ("gm", i * 3 + b)], [("gm", 6 + i * 3 + b)])
                    TT(state["gpe"], gm[:, 6 + i * 3, :], gm[:, 6 + i * 3, :], gm[:, 7 + i * 3, :], ALU.add,
                       [("gm", 6 + i * 3), ("gm", 7 + i * 3)], [("gm", 6 + i * 3)])
                    TT(state["gpe"], mg[:, m, :], gm[:, 6 + i * 3, :], gm[:, 8 + i * 3, :], ALU.add,
                       [("gm", 6 + i * 3), ("gm", 8 + i * 3)], [("mg", m)])
                    if l == 0 and it + 1 < NT:
                        prefetch_x(m)
                if DEBUG and it == DBG_IT and l == DBG_L:
                    DMA("sp", dbg["y"][:, :, :], ybuf[:, :, :], [("y", k) for k in range(12)], [], "ost")
                    DMA("sp", dbg["mg"][:, :, :], mg[:, :, :], [("mg", k) for k in range(8)], [], "ost")
                obanks = []
                for m in range(8):
                    bank = nextbank()
                    proj_group(bank, wi, 8, lambda kc: (mg[:, kc, :], ("mg", kc), None))
                    wi += 8
                    obanks.append(bank)
                    TT("dve", xres[:, m, :], ps[bank][:, :], xres[:, m, :], ALU.add,
                       [("ps", bank), ("x", xp, m)], [("x", xp, m)])
                nbank = norm_begin()
                for c in range(8):
                    norm_chunk(nbank, c, xp)
                if DEBUG and it == DBG_IT and l == DBG_L:
                    DMA("sp", dbg["x1"][:, :, :], xres[:, :, :], [("x", xp, k) for k in range(8)], [], "ost")
                norm_finish(nbank, vb + 24, False, xp)
                if DEBUG and it == DBG_IT and l == DBG_L:
                    DMA("sp", dbg["h"][:, :, :], hbuf[:, :, :], [("h", k) for k in range(8)], [], "ost")
                fb = vb + 32
                bb = vb + 164
                hold = ("uph", l, par)
                hnew = ("uph", l, 1 - par)
                uold = uph[:, l * 2 + par, :, :]
                unew = uph[:, l * 2 + 1 - par, :, :]

                def silu_mul(c, i):
                    ACT(fsg[:, i, :], facc[:, i * 2, :], AF.Silu, [("facc", i * 2)], [("fsg", i)])
                    TT(state["gpe"], actb[:, c, :], fsg[:, i, :], facc[:, i * 2 + 1, :], ALU.mult,
                       [("fsg", i), ("facc", i * 2 + 1)], [("act", c)])

                W0 = vec[:, fb:fb + 44]
                W1 = vec[:, fb + 44:fb + 88]
                TT("dve", tmpc[:, :], uold[:, :, 1], W1, ALU.mult, [hold, ("vec",)], [("tmpc",)])
                TT("dve", corrA[:, :, 0], uold[:, :, 0], W0, ALU.mult, [hold, ("vec",)], [("corrA",)])
                TT("dve", corrA[:, :, 0], corrA[:, :, 0], tmpc[:, :], ALU.add, [("corrA",), ("tmpc",)], [("corrA",)])
                TT("dve", corrA[:, :, 1], uold[:, :, 1], W0, ALU.mult, [hold, ("vec",)], [("corrA",)])
                prev = None
                hoist = (l == L - 1 and it + 1 < NT)
                if hoist:
                    nb2 = norm_begin()
                for c in range(NFC):
                    i = c % 2
                    if hoist and 10 <= c < 18:
                        norm_chunk(nb2, c - 10, 1 - xp)
                    for gv, cc in enumerate((c, NFC + c)):
                        bank = nextbank()
                        proj_group(bank, wi, 8, h_of)
                        wi += 8
                        fr = ("facc", i * 2 + gv)
                        acc = facc[:, i * 2 + gv, :]
                        pb = ps[bank]
                        w0 = vec[:, fb + cc:fb + cc + 1]
                        w1 = vec[:, fb + 44 + cc:fb + 45 + cc]
                        w2 = vec[:, fb + 88 + cc:fb + 89 + cc]
                        ACT(acc, pb[:, :], AF.Identity, [("ps", bank), ("vec",)], [fr], scale=w2,
                            bias=vec[:, bb + cc:bb + cc + 1])
                        ACT(unew[:, cc, :], pb[:, T - 2:T], AF.Copy, [("ps", bank)], [hnew])
                        TT("dve", acc[:, 0:2], acc[:, 0:2], corrA[:, cc, :], ALU.add, [fr, ("corrA",)], [fr])
                        STT(acc[:, 1:T], pb[:, 0:T - 1], w1, acc[:, 1:T], ALU.mult, ALU.add,
                            [("ps", bank), ("vec",), fr], [fr])
                        STT(acc[:, 2:T], pb[:, 0:T - 2], w0, acc[:, 2:T], ALU.mult, ALU.add,
                            [("ps", bank), ("vec",), fr], [fr])
                    if prev is not None:
                        silu_mul(*prev)
                    prev = (c, i)
                silu_mul(*prev)
                if hoist:
                    norm_finish(nb2, 0, False, 1 - xp)
                for m in range(8):
                    bank = nextbank()
                    proj_group(bank, wi, NFC, lambda kc: (actb[:, kc, :], ("act", kc), None))
                    wi += NFC
                    TT("dve", xres[:, m, :], ps[bank][:, :], xres[:, m, :], ALU.add,
                       [("ps", bank), ("x", xp, m)], [("x", xp, m)])
                nbank = norm_begin()
                for c in range(8):
                    norm_chunk(nbank, c, xp)
                if DEBUG and it == DBG_IT and l == DBG_L:
                    DMA("sp", dbg["act"][:, :, :], actb[:, :, :], [("act", k) for k in range(NFC)], [], "ost")
                    DMA("sp", dbg["x2"][:, :, :], xres[:, :, :], [("x", xp, k) for k in range(8)], [], "ost")
            assert wi == (it + 1) * NSUB, (wi, it)
            norm_finish(nbank, NVL * L, (t0,), xp)

        S_.prog["sp"].append(([([("ost", S_.cnt["ost"])], None)], None, 0))

        with nc.Block() as block:
            def make(engname):
                def body(e):
                    for items, sk, inc in S_.prog[engname]:
                        last = None
                        for waits, fn in items:
                            for k, v in waits:
                                e.wait_ge(sems[k], v)
                            if fn is not None:
                                last = fn(e)
                        if last is not None and inc:
                            last.then_inc(sems[sk], inc)
                return body
            block.tensor(make("pe"))
            block.scalar(make("act"))
            block.vector(make("dve"))
            block.gpsimd(make("gp"))
            block.sync(make("sp"))
    return nc


def _fm(v):
    v = np.asarray(v, np.float32)
    return v.reshape(-1, 128).T


def prep_weights(inp):
    w_in, w_up, w_down, w_o = inp["w_in"], inp["w_up"], inp["w_down"], inp["w_o"]
    wbr = (inp["w_branch_a"], inp["w_branch_b"], inp["w_branch_c"])
    wsrc = np.empty((NSUB, 128, 128), np.float32)
    n = 0
    for l in range(L):
        Wl = w_in[l]

        def cols(Wm, col0, nk):
            nonlocal n
            blk = Wm[:nk * 128, col0:col0 + 128].reshape(nk, 128, 128)
            wsrc[n:n + nk] = blk
            n += nk
        for g in range(4):
            cols(Wl, g * 128, 8)
        for j in range(4):
            cols(Wl, 512 + j * 128, 8)
        for kc in range(8):
            for q4 in range(4):
                wsrc[n] = Wl[kc * 128:(kc + 1) * 128, 1024 + q4 * 128:1024 + (q4 + 1) * 128]
                n += 1
        for j in range(4):
            for off in (2048, 2560, 1536):
                cols(Wl, off + j * 128, 8)
        for m in range(8):
            for off in (3072, 4096, 5120):
                cols(Wl, off + m * 128, 8)
            for b in range(3):
                cols(wbr[b][l], m * 128, 4)
        for m in range(8):
            cols(w_o[l], m * 128, 8)
        for c in range(NFC):
            for cc in (c, NFC + c):
                cols(w_up[l], cc * 128, 8)
        for m in range(8):
            cols(w_down[l], m * 128, NFC)
    assert n == NSUB
    wsrc = np.ascontiguousarray(wsrc.transpose(1, 0, 2)).reshape(128, NSUB * 128)

    wsm = np.empty((L, 8, 128, 128), np.float32)
    for l in range(L):
        for g in range(4):
            wsm[l, g] = inp["w_pool"][l, g]
            wsm[l, 4 + g] = inp["w_spatial"][l, g].T
    wsm = np.ascontiguousarray(wsm.reshape(L * 8, 128, 128).transpose(1, 0, 2)).reshape(128, L * 8 * 128)

    vecs = np.empty((128, NV), np.float32)
    for l in range(L):
        vb = l * NVL
        vecs[:, vb:vb + 8] = _fm(inp["g_mix"][l])
        vecs[:, vb + 8:vb + 12] = _fm(inp["pool_scale"][l])
        for tap in range(3):
            vecs[:, vb + 12 + tap * 4:vb + 16 + tap * 4] = _fm(inp["conv_c"][l, tap])
        vecs[:, vb + 24:vb + 32] = _fm(inp["g_ffn"][l])
        for tap in range(3):
            vecs[:, vb + 32 + tap * 44:vb + 32 + (tap + 1) * 44] = _fm(inp["conv_ffn"][l, tap])
        vecs[:, vb + 164:vb + 208] = _fm(inp["conv_ffn_b"][l])
    vecs[:, NVL * L:NVL * L + 8] = _fm(inp["g_final"])

    gsg = np.ascontiguousarray(np.broadcast_to(np.asarray(inp["g_sgu"], np.float32).reshape(1, L * 512), (128, L * 512)))
    bspv = np.ascontiguousarray(np.asarray(inp["b_spatial"], np.float32).reshape(1, L * 4 * 128))
    cst = np.zeros((128, 192), np.float32)
    s_idx = np.arange(128)[:, None]
    t_idx = np.arange(128)[None, :]
    cst[:, 0:128] = (t_idx >= s_idx).astype(np.float32)
    for g, w in enumerate(POOL_W):
        tt = np.arange(16)
        cst[:, 128 + g * 16:128 + (g + 1) * 16] = (w / np.minimum(tt + 1, w)).astype(np.float32)[None, :]
    return dict(wsrc=wsrc, wsm=wsm, vecs=vecs, gsg=gsg, bsp=bspv, cst=cst)


_CACHE = {}


def run(inputs, S=SEQ, ncores=NCORES, trace=False):
    inp = {k: np.asarray(v) for k, v in inputs.items()}
    common = prep_weights(inp)
    x = inp["x"]
    in_maps = []
    for b in range(ncores):
        m = dict(common)
        m["xT"] = np.ascontiguousarray(x[b, :S, :].T)
        in_maps.append(m)
    if S not in _CACHE:
        _CACHE[S] = build_program(S)
    nc = _CACHE[S]
    res = run_bass_kernel_spmd(nc, in_maps, core_ids=list(range(ncores)), trace=trace)
    out = np.stack([np.ascontiguousarray(r["outT"].T) for r in res.results], axis=0)
    return out.astype(np.float32), res


def kernel(**inputs):
    out, _ = run(inputs)
    return out
```

```python
import numpy as np
from contextlib import ExitStack
import concourse.bass as bass
import concourse.mybir as mybir
from concourse.bass_utils import run_bass_kernel_spmd

F32 = mybir.dt.float32
BF16 = mybir.dt.bfloat16
AF = mybir.ActivationFunctionType
ALU = mybir.AluOpType

D = 1024
L = 2
T = 512
NQ = T // 128
DFF = 2816
NFC = DFF // 128
EPS = 1e-6
SEQ = 8192
DEBUG = False
GPE = "gp"
DBG_IT, DBG_L = 0, 0
NCORES = 8
NSUB_L = 1072
NSUB = NSUB_L * L
CH = 16
NCH = NSUB // CH
R = 6
PIECE = 8
NPIECE = (NCH + PIECE - 1) // PIECE
NVL = 208
NV = NVL * L + 8
POOL_W = (2, 4, 8, 16)

ENGS = ("pe", "act", "dve", "gp", "sp")


class Sched:
    def __init__(self):
        self.prog = {e: [] for e in ENGS}
        self.cnt = {}
        self.waited = {e: {} for e in ENGS}
        self.lw = {}
        self.rd = {}
        self.same_sync = {"pe": False, "act": True, "dve": True, "gp": True, "sp": False}
        self.big_ok = {"act": True, "dve": True}
        self.big = {}

    def deps(self, reads, writes):
        d = {}
        for r in reads:
            x = self.lw.get(r)
            if x is not None and x[1] > d.get(x[0], 0):
                d[x[0]] = x[1]
        for w in writes:
            x = self.lw.get(w)
            if x is not None and x[1] > d.get(x[0], 0):
                d[x[0]] = x[1]
            for k, v in self.rd.get(w, {}).items():
                if v > d.get(k, 0):
                    d[k] = v
        return d

    def _waits(self, eng, d, big=False):
        waits = []
        wd = self.waited[eng]
        for k, v in d.items():
            if k == eng and not self.same_sync[eng]:
                continue
            if k == eng and big and self.big_ok.get(eng) and self.big.get((eng, v)):
                continue
            if wd.get(k, 0) >= v:
                continue
            wd[k] = v
            waits.append((k, v))
        return waits

    def _record(self, sk, val, reads, writes):
        for w in writes:
            self.lw[w] = (sk, val)
            self.rd[w] = {}
        for r in reads:
            if r in writes:
                continue
            m = self.rd.setdefault(r, {})
            if val > m.get(sk, 0):
                m[sk] = val

    def op(self, eng, fn, reads=(), writes=(), semkey=None, inc=1, big=False):
        waits = self._waits(eng, self.deps(reads, writes), big)
        sk = semkey if semkey is not None else eng
        val = self.cnt.get(sk, 0) + inc
        self.cnt[sk] = val
        if big and semkey is None:
            self.big[(eng, val)] = True
        self.prog[eng].append(([(waits, fn)], sk, inc))
        self._record(sk, val, reads, writes)

    def pe_group(self, mms, out_region):
        val = self.cnt.get("pe", 0) + 1
        items = []
        allreads = []
        for i, (fn, reads) in enumerate(mms):
            d = self.deps(reads, [out_region] if i == 0 else [])
            items.append((self._waits("pe", d), fn))
            allreads.extend(reads)
        self.cnt["pe"] = val
        self.prog["pe"].append((items, "pe", 1))
        self._record("pe", val, allreads, [out_region])


def build_program(S):
    NT = S // T
    nc = bass.Bass("TRN2", target_bir_lowering=False)
    xT = nc.dram_tensor("xT", [D, S], F32, kind="ExternalInput").ap()
    wsrc = nc.dram_tensor("wsrc", [128, NSUB * 128], F32, kind="ExternalInput").ap()
    wsm = nc.dram_tensor("wsm", [128, L * 8 * 128], F32, kind="ExternalInput").ap()
    vecs = nc.dram_tensor("vecs", [128, NV], F32, kind="ExternalInput").ap()
    gsg = nc.dram_tensor("gsg", [128, L * 512], F32, kind="ExternalInput").ap()
    bsp = nc.dram_tensor("bsp", [1, L * 4 * 128], F32, kind="ExternalInput").ap()
    cst = nc.dram_tensor("cst", [128, 128 + 64], F32, kind="ExternalInput").ap()
    outT = nc.dram_tensor("outT", [D, S], F32, kind="ExternalOutput").ap()
    wbf = nc.dram_tensor("wbf", [128, NSUB * 128], BF16, kind="Internal").ap()
    dbg = {}
    if DEBUG:
        dbg["y"] = nc.dram_tensor("dbg_y", [128, 12, T], BF16, kind="ExternalOutput").ap()
        dbg["x1"] = nc.dram_tensor("dbg_x1", [128, 8, T], F32, kind="ExternalOutput").ap()
        dbg["act"] = nc.dram_tensor("dbg_act", [128, NFC, T], BF16, kind="ExternalOutput").ap()
        dbg["x2"] = nc.dram_tensor("dbg_x2", [128, 8, T], F32, kind="ExternalOutput").ap()
        dbg["mg"] = nc.dram_tensor("dbg_mg", [128, 8, T], BF16, kind="ExternalOutput").ap()
        dbg["ab"] = nc.dram_tensor("dbg_ab", [128, 4, T + 16], F32, kind="ExternalOutput").ap()
        dbg["pa"] = nc.dram_tensor("dbg_pa", [128, 4, T], BF16, kind="ExternalOutput").ap()
        dbg["h"] = nc.dram_tensor("dbg_h", [128, 8, T], BF16, kind="ExternalOutput").ap()
    xTv = xT.rearrange("(c p) s -> p c s", p=128)
    outTv = outT.rearrange("(c p) s -> p c s", p=128)

    S_ = Sched()
    W = T + 16
    with ExitStack() as st:
        def sb(name, shape, dt):
            return st.enter_context(nc.sbuf_tensor(name, shape, dt))

        xbufs = [sb("xres0", [128, 8, T], F32), sb("xres1", [128, 8, T], F32)]
        hbuf = sb("hbuf", [128, 8, T], BF16)
        rsb = sb("rsb", [128, T], F32)
        ones_bf = sb("ones_bf", [128, 128], BF16)
        wsmall = sb("wsmall", [128, L, 8, 128], BF16)
        bpad = sb("bpad", [128, L, 4, 128], BF16)
        gsgu = sb("gsgu", [128, L * 512], F32)
        vec = sb("vec", [128, NV], F32)
        cst_sb = sb("cst_sb", [128, 192], F32)
        ahalo = sb("ahalo", [128, L, 4, 16], F32)
        zhalo = sb("zhalo", [128, L, 4, 2], F32)
        uph = sb("uph", [128, L * 2, 2 * NFC, 2], F32)
        corrA = sb("corrA", [128, 2 * NFC, 2], F32)
        tmpc = sb("tmpc", [128, 2 * NFC], F32)
        abuf = sb("abuf", [128, 4, W], F32)
        ptmp = sb("ptmp", [128, 2, W], F32)
        pabuf = sb("pabuf", [128, 4, T], BF16)
        ubuf = sb("ubuf", [128, 4, T], F32)
        vstage = sb("vstage", [128, 2, 512], F32)
        vsq = sb("vsq", [128, 512], BF16)
        vss = sb("vss", [128, NQ], F32)
        vrs = sb("vrs", [128, NQ], F32)
        vrr = sb("vrr", [128, NQ], F32)
        vtok = sb("vtok", [128, NQ, 512], BF16)
        cc_sb = sb("cc_sb", [128, 2, T], F32)
        zbuf = sb("zbuf", [128, 2, T + 2], F32)
        cacc = sb("cacc", [128, 2, T], F32)
        ybuf = sb("ybuf", [128, 12, T], BF16)
        gm = sb("gm", [128, 12, T], F32)
        mg = sb("mg", [128, 8, T], BF16)
        actb = sb("actb", [128, NFC, T], BF16)
        facc = sb("facc", [128, 4, T], F32)
        fsg = sb("fsg", [128, 2, T], F32)
        ring = sb("ring", [128, R, CH * 128], BF16)
        ps = [st.enter_context(nc.psum_tensor("ps%d" % i, [128, 512], F32)) for i in range(8)]

        semkeys = ["pe", "act", "dve", "gp", "xl0", "xl1", "ost", "cst0", "cst1", "cst2", "cst3", "cst4"] + \
                  [("wl", s) for s in range(R)] + [("wb", j) for j in range(8)]
        sems = {}
        for i, k in enumerate(semkeys):
            sems[k] = st.enter_context(nc.semaphore("sem%d" % i))

        state = {"bank": 0, "next_load": 0, "reserved": set(), "gpe": "dve"}

        def nextbank():
            while True:
                b = state["bank"]
                state["bank"] = (b + 1) % 8
                if b not in state["reserved"]:
                    return b

        def isbig(ap):
            n = 1
            for d in ap.shape[1:]:
                n *= int(d)
            return n >= 256

        def ACT(out, in_, func, reads, writes, scale=1.0, bias=None, accum=None):
            def fn(e):
                kw = {}
                if bias is not None:
                    kw["bias"] = bias
                if accum is not None:
                    kw["accum_out"] = accum
                return e.activation(out=out, in_=in_, func=func, scale=scale, **kw)
            S_.op("act", fn, reads, writes, big=isbig(out) and accum is None)

        def TT(eng, out, in0, in1, op, reads, writes):
            S_.op(eng, lambda e: e.tensor_tensor(out=out, in0=in0, in1=in1, op=op), reads, writes, big=isbig(out))

        def TS(eng, out, in0, s1, s2, op0, op1, reads, writes):
            if op1 is None:
                S_.op(eng, lambda e: e.tensor_scalar(out=out, in0=in0, scalar1=s1, scalar2=None, op0=op0),
                      reads, writes, big=isbig(out))
            else:
                S_.op(eng, lambda e: e.tensor_scalar(out=out, in0=in0, scalar1=s1, scalar2=s2, op0=op0, op1=op1),
                      reads, writes, big=isbig(out))

        def STT(out, in0, scalar, in1, op0, op1, reads, writes):
            S_.op("dve", lambda e: e.scalar_tensor_tensor(out=out, in0=in0, scalar=scalar, in1=in1,
                                                          op0=op0, op1=op1), reads, writes, big=isbig(out))

        def COPY(eng, out, in_, reads, writes):
            S_.op(eng, lambda e: e.tensor_copy(out=out, in_=in_), reads, writes, big=isbig(out))

        def MEMSET(eng, ap, val, writes):
            S_.op(eng, lambda e: e.memset(ap, val), (), writes)

        def DMA(eng, out, in_, reads, writes, semkey, **kw):
            S_.op(eng, lambda e: e.dma_start(out=out, in_=in_, **kw), reads, writes, semkey=semkey, inc=16)

        def MM(out, lhsT, rhs, start, stop):
            return lambda e: e.matmul(out, lhsT, rhs, start=start, stop=stop)

        total_chunks = NT * NCH

        def wload_upto(gc):
            while state["next_load"] <= min(gc, total_chunks - 1):
                k = state["next_load"]
                slot = k % R
                c = k % NCH
                sl = slice(c * CH * 128, (c + 1) * CH * 128)
                if k < NCH:
                    DMA("gp", ring[:, slot, :], wsrc[:, sl], [], [("w", slot)], ("wl", slot))
                    DMA("sp", wbf[:, sl], ring[:, slot, :], [("w", slot)], [("wbfc", c)], ("wb", c % 8))
                    if k == NCH - 1:
                        for c2 in range(NCH):
                            S_.lw[("wbfc", c2)] = (("wb", c2 % 8), S_.cnt[("wb", c2 % 8)])
                else:
                    DMA("sp", ring[:, slot, :], wbf[:, sl], [("wbfc", c)], [("w", slot)], ("wl", slot))
                state["next_load"] += 1

        def wsub(i, n=1):
            gc = i // CH
            slot = gc % R
            off = i % CH
            assert off + n <= CH
            return ring[:, slot, off * 128:(off + n) * 128], ("w", slot), gc

        def group(bank, specs):
            chunks = [x[2] for sp_ in specs for x in (sp_[1], sp_[2]) if x[2] is not None]
            if chunks:
                wload_upto(max(chunks))
            mms = []
            for (o, lh, rh, s0, s1) in specs:
                mms.append((MM(o, lh[0], rh[0], s0, s1), [lh[1], rh[1]]))
            S_.pe_group(mms, ("ps", bank))
            if chunks:
                wload_upto(min(chunks) + R - 1)

        def proj_group(bank, wi, nk, rhs_of):
            specs = []
            for kc in range(nk):
                specs.append((ps[bank][:, :], wsub(wi + kc), rhs_of(kc), kc == 0, kc == nk - 1))
            group(bank, specs)

        def h_of(kc):
            return (hbuf[:, kc, :], ("h", kc), None)

        DMA("sp", vec[:, :], vecs[:, :], [], [("vec",)], "cst0")
        DMA("sp", gsgu[:, :], gsg[:, :], [], [("gsgu",)], "cst1")
        DMA("sp", cst_sb[:, :], cst[:, :], [], [("cstsb",)], "cst2")
        gmflat = gm[:, 0:4, :]
        DMA("sp", gmflat, wsm.rearrange("p (a b) -> p a b", a=4), [], [("gm", k) for k in range(4)], "cst3")
        MEMSET("dve", ones_bf[:, :], 1.0, [("ones",)])
        MEMSET("dve", ahalo[:, :, :, :], 0.0, [("ahalo", l) for l in range(L)])
        MEMSET("dve", zhalo[:, :, :, :], 0.0, [("zhalo", l, j) for l in range(L) for j in range(4)])
        MEMSET("dve", uph[:, :, :, :], 0.0, [("uph", l, p_) for l in range(L) for p_ in range(2)])
        MEMSET("dve", bpad[:, :, :, :], 0.0, [("bpad",)])
        DMA("gp", bpad[0:1, :, :, :], bsp.rearrange("o (l g t) -> o l g t", l=L, g=4), [], [("bpad",)], "cst4")
        for l in range(L):
            for g in range(8):
                idx = l * 8 + g
                src = gm[:, idx // 4, (idx % 4) * 128:(idx % 4 + 1) * 128]
                if g < 4:
                    COPY("dve", wsmall[:, l, g, :], src, [("gm", idx // 4)], [("wsmall",)])
                else:
                    TT("dve", wsmall[:, l, g, :], src, cst_sb[:, 0:128], ALU.mult,
                       [("gm", idx // 4), ("cstsb",)], [("wsmall",)])

        eps_t = sb("eps_t", [128, 1], F32)
        eps_ap = eps_t[:, 0:1]
        MEMSET("dve", eps_t[:, :], EPS, [("vec2",)])

        def norm_begin():
            bank = nextbank()
            state["reserved"].add(bank)
            return bank

        def norm_sq(c, xp):
            xres = xbufs[xp]
            sqacc, sqreg = cacc[:, 0, :], ("cacc", 0)
            if c == 0:
                ACT(sqacc, xres[:, c, :], AF.Square, [("x", xp, c)], [sqreg])
            elif c < 6:
                i = c % 2
                ACT(vstage[:, i, :], xres[:, c, :], AF.Square, [("x", xp, c)], [("vst", i)])
                if c < 5:
                    TT(state["gpe"], sqacc, sqacc, vstage[:, i, :], ALU.add, [sqreg, ("vst", i)], [sqreg])
                else:
                    TT(state["gpe"], vsq[:, :], sqacc, vstage[:, i, :], ALU.add, [sqreg, ("vst", i)], [("vsq",)])
            else:
                ACT(mg[:, c, :], xres[:, c, :], AF.Square, [("x", xp, c)], [("mg", c)])

        def norm_mm(bank):
            S_.pe_group([(MM(ps[bank][:, :], ones_bf[:, :], vsq[:, :], True, False), [("ones",), ("vsq",)]),
                         (MM(ps[bank][:, :], ones_bf[:, :], mg[:, 6, :], False, False), [("ones",), ("mg", 6)]),
                         (MM(ps[bank][:, :], ones_bf[:, :], mg[:, 7, :], False, True), [("ones",), ("mg", 7)])],
                        ("ps", bank))

        def norm_chunk(bank, c, xp):
            norm_sq(c, xp)
            if c == 7:
                norm_mm(bank)

        def norm_finish(bank, gbase, to_out, xp):
            xres = xbufs[xp]
            ACT(rsb[:, :], ps[bank][:, :], AF.Ln, [("ps", bank), ("vec2",)], [("rsb",)], scale=1.0 / D,
                bias=eps_ap)
            ACT(ps[bank][:, :], rsb[:, :], AF.Exp, [("rsb",)], [("ps", bank)], scale=-0.5)
            for c in range(8):
                if to_out:
                    dst, reg = gm[:, c, :], ("gm", c)
                else:
                    dst, reg = hbuf[:, c, :], ("h", c)
                STT(dst, xres[:, c, :], vec[:, gbase + c:gbase + c + 1], ps[bank][:, :], ALU.mult, ALU.mult,
                    [("x", xp, c), ("ps", bank), ("vec",)], [reg])
                if to_out:
                    DMA("sp", outTv[:, c, to_out[0]:to_out[0] + T], gm[:, c, :], [("gm", c)], [], "ost")
            if to_out:
                for c in range(8):
                    S_.rd[("gm", c)]["ost"] = S_.cnt["ost"]
            state["reserved"].discard(bank)

        wi = 0
        for it in range(NT):
            t0 = it * T
            par = it % 2
            xp = it % 2
            state["gpe"] = "dve" if it == 0 else GPE
            xres = xbufs[xp]
            if it == 0:
                DMA("sp", xres[:, :, :], xTv[:, :, t0:t0 + T], [], [("x", xp, c) for c in range(8)], "xl%d" % xp)
                nbank = norm_begin()
                for c in range(8):
                    norm_chunk(nbank, c, xp)
                norm_finish(nbank, 0, False, xp)
            def prefetch_x(c):
                sk = "xl%d" % (1 - xp)
                DMA("sp", xbufs[1 - xp][:, c, :], xTv[:, c, t0 + T:t0 + 2 * T], [], [("x", 1 - xp, c)], sk)
                if c == 7:
                    for cc_ in range(8):
                        S_.lw[("x", 1 - xp, cc_)] = (sk, S_.cnt[sk])
            wi = it * NSUB
            for l in range(L):
                vb = l * NVL
                if l > 0:
                    norm_finish(nbank, vb + 0, False, xp)
                COPY(state["gpe"], abuf[:, :, 0:16], ahalo[:, l, :, :], [("ahalo", l)], [("abuf", g) for g in range(4)])
                for g in range(4):
                    bank = nextbank()
                    proj_group(bank, wi, 8, h_of)
                    wi += 8
                    ACT(abuf[:, g, 16:W], ps[bank][:, :], AF.Copy, [("ps", bank)], [("abuf", g)])
                COPY(state["gpe"], ahalo[:, l, :, :], abuf[:, :, T:W], [("abuf", g) for g in range(4)], [("ahalo", l)])
                for g in range(4):
                    cur, cur_reg = abuf[:, g, :], ("abuf", g)
                    sh, k = 1, 0
                    for step in range(g + 1):
                        lo = 2 * sh - 1
                        TT(state["gpe"], ptmp[:, k, lo:W], cur[:, lo:W], cur[:, lo - sh:W - sh], ALU.add,
                           [cur_reg], [("ptmp", k)])
                        cur, cur_reg = ptmp[:, k, :], ("ptmp", k)
                        k ^= 1
                        sh *= 2
                    w = POOL_W[g]
                    TS(state["gpe"], cur[:, 16:W], cur[:, 16:W], 1.0 / w, 0.0, ALU.mult, ALU.add, [cur_reg], [cur_reg])
                    if it == 0:
                        TT(state["gpe"], cur[:, 16:32], cur[:, 16:32], cst_sb[:, 128 + g * 16:128 + (g + 1) * 16], ALU.mult,
                           [cur_reg, ("cstsb",)], [cur_reg])
                    TT(state["gpe"], pabuf[:, g, :], cur[:, 16:W], abuf[:, g, 16:W], ALU.subtract,
                       [cur_reg, ("abuf", g)], [("pa", g)])
                for j in range(4):
                    bank = nextbank()
                    proj_group(bank, wi, 8, h_of)
                    wi += 8
                    ACT(ubuf[:, j, :], ps[bank][:, :], AF.Gelu_apprx_tanh, [("ps", bank)], [("u", j)])
                vbase = wi
                wi += 32
                for q in range(NQ):
                    bank = nextbank()
                    specs = []
                    for kc in range(8):
                        specs.append((ps[bank][:, :], (hbuf[:, kc, q * 128:(q + 1) * 128], ("h", kc), None),
                                      wsub(vbase + kc * 4, 4), kc == 0, kc == 7))
                    group(bank, specs)
                    i = q % 2
                    ACT(vstage[:, i, :], ps[bank][:, :], AF.Gelu_apprx_tanh, [("ps", bank)], [("vst", i)])
                    ACT(vsq[:, :], vstage[:, i, :], AF.Square, [("vst", i)], [("vsq",), ("vss", q)],
                        accum=vss[:, q:q + 1])
                    ACT(vrs[:, q:q + 1], vss[:, q:q + 1], AF.Sqrt, [("vss", q), ("vec2",)], [("vrs", q)],
                        scale=1.0 / 512, bias=eps_ap)
                    S_.op("dve", (lambda q: lambda e: e.reciprocal(out=vrr[:, q:q + 1], in_=vrs[:, q:q + 1]))(q),
                          [("vrs", q)], [("vrr", q)])
                    STT(vtok[:, q, :], vstage[:, i, :], vrr[:, q:q + 1], gsgu[:, l * 512:(l + 1) * 512],
                        ALU.mult, ALU.mult, [("vst", i), ("vrr", q), ("gsgu",)], [("vtok", q)])
                for j in range(4):
                    i = j % 2
                    banks = []
                    for _ in range(3):
                        bank = nextbank()
                        proj_group(bank, wi, 8, h_of)
                        wi += 8
                        banks.append(bank)
                    b_cc, b_cx, b_cb = banks
                    ACT(cc_sb[:, i, :], ps[b_cc][:, :], AF.Copy, [("ps", b_cc)], [("cc", i)])
                    COPY(state["gpe"], zbuf[:, i, 0:2], zhalo[:, l, j, :], [("zhalo", l, j)], [("z", i)])
                    TT("dve", zbuf[:, i, 2:T + 2], ps[b_cx][:, :], cc_sb[:, i, :], ALU.mult,
                       [("ps", b_cx), ("cc", i)], [("z", i)])
                    COPY(state["gpe"], zhalo[:, l, j, :], zbuf[:, i, T:T + 2], [("z", i)], [("zhalo", l, j)])
                    cb = vb + 12
                    TS("dve", cacc[:, i, :], zbuf[:, i, 2:T + 2], vec[:, cb + 8 + j:cb + 9 + j], None, ALU.mult, None,
                       [("z", i), ("vec",)], [("cacc", i)])
                    STT(cacc[:, i, :], zbuf[:, i, 1:T + 1], vec[:, cb + 4 + j:cb + 5 + j], cacc[:, i, :],
                        ALU.mult, ALU.add, [("z", i), ("vec",), ("cacc", i)], [("cacc", i)])
                    STT(cacc[:, i, :], zbuf[:, i, 0:T], vec[:, cb + j:cb + 1 + j], cacc[:, i, :],
                        ALU.mult, ALU.add, [("z", i), ("vec",), ("cacc", i)], [("cacc", i)])
                    TT("dve", ybuf[:, 8 + j, :], ps[b_cb][:, :], cacc[:, i, :], ALU.mult,
                       [("ps", b_cb), ("cacc", i)], [("y", 8 + j)])
                for g in range(4):
                    bank = nextbank()
                    S_.pe_group([(MM(ps[bank][:, :], wsmall[:, l, g, :], pabuf[:, g, :], True, True),
                                  [("wsmall",), ("pa", g)])], ("ps", bank))
                    ACT(ybuf[:, g, :], ps[bank][:, :], AF.Identity, [("ps", bank), ("vec",)], [("y", g)],
                        scale=vec[:, vb + 8 + g:vb + 9 + g])
                for g in range(4):
                    bank = nextbank()
                    mms = []
                    for q in range(NQ):
                        o = ps[bank][:, q * 128:(q + 1) * 128]
                        mms.append((MM(o, vtok[:, q, g * 128:(g + 1) * 128], wsmall[:, l, 4 + g, :], True, False),
                                    [("vtok", q), ("wsmall",)]))
                        mms.append((MM(o, ones_bf[:, :], bpad[:, l, g, :], False, True), [("ones",), ("bpad",)]))
                    S_.pe_group(mms, ("ps", bank))
                    TT("dve", ybuf[:, 4 + g, :], ps[bank][:, :], ubuf[:, g, :], ALU.mult,
                       [("ps", bank), ("u", g)], [("y", 4 + g)])
                for m in range(8):
                    i = m % 2
                    gbanks = []
                    for b in range(3):
                        bank = nextbank()
                        proj_group(bank, wi, 8, h_of)
                        wi += 8
                        gbanks.append(bank)
                    for b in range(3):
                        ACT(gm[:, i * 3 + b, :], ps[gbanks[b]][:, :], AF.Sigmoid, [("ps", gbanks[b])],
                            [("gm", i * 3 + b)])
                    bbanks = []
                    for b in range(3):
                        bank = nextbank()
                        proj_group(bank, wi, 4, (lambda b: lambda kc: (ybuf[:, 4 * b + kc, :], ("y", 4 * b + kc), None))(b))
                        wi += 4
                        bbanks.append(bank)
                    for b in range(3):
                        TT("dve", gm[:, 6 + i * 3 + b, :], ps[bbanks[b]][:, :], gm[:, i * 3 + b, :], ALU.mult,
                           [("ps", bbanks[b]), ("gm", i * 3 + b)], [("gm", 6 + i * 3 + b)])
                    TT(state["gpe"], gm[:, 6 + i * 3, :], gm[:, 6 + i * 3, :], gm[:, 7 + i * 3, :], ALU.add,
                       [("gm", 6 + i * 3), ("gm", 7 + i * 3)], [("gm", 6 + i * 3)])
                    TT(state["gpe"], mg[:, m, :], gm[:, 6 + i * 3, :], gm[:, 8 + i * 3, :], ALU.add,
                       [("gm", 6 + i * 3), ("gm", 8 + i * 3)], [("mg", m)])
                    if l == 0 and it + 1 < NT:
                        prefetch_x(m)
                if DEBUG and it == DBG_IT and l == DBG_L:
                    DMA("sp", dbg["y"][:, :, :], ybuf[:, :, :], [("y", k) for k in range(12)], [], "ost")
                    DMA("sp", dbg["mg"][:, :, :], mg[:, :, :], [("mg", k) for k in range(8)], [], "ost")
                obanks = []
                for m in range(8):
                    bank = nextbank()
                    proj_group(bank, wi, 8, lambda kc: (mg[:, kc, :], ("mg", kc), None))
                    wi += 8
                    obanks.append(bank)
                    TT("dve", xres[:, m, :], ps[bank][:, :], xres[:, m, :], ALU.add,
                       [("ps", bank), ("x", xp, m)], [("x", xp, m)])
                nbank = norm_begin()
                for c in range(8):
                    norm_chunk(nbank, c, xp)
                if DEBUG and it == DBG_IT and l == DBG_L:
                    DMA("sp", dbg["x1"][:, :, :], xres[:, :, :], [("x", xp, k) for k in range(8)], [], "ost")
                norm_finish(nbank, vb + 24, False, xp)
                if DEBUG and it == DBG_IT and l == DBG_L:
                    DMA("sp", dbg["h"][:, :, :], hbuf[:, :, :], [("h", k) for k in range(8)], [], "ost")
                fb = vb + 32
                bb = vb + 164
                hold = ("uph", l, par)
                hnew = ("uph", l, 1 - par)
                uold = uph[:, l * 2 + par, :, :]
                unew = uph[:, l * 2 + 1 - par, :, :]

                def silu_mul(c, i):
                    ACT(fsg[:, i, :], facc[:, i * 2, :], AF.Silu, [("facc", i * 2)], [("fsg", i)])
                    TT(state["gpe"], actb[:, c, :], fsg[:, i, :], facc[:, i * 2 + 1, :], ALU.mult,
                       [("fsg", i), ("facc", i * 2 + 1)], [("act", c)])

                W0 = vec[:, fb:fb + 44]
                W1 = vec[:, fb + 44:fb + 88]
                TT("dve", tmpc[:, :], uold[:, :, 1], W1, ALU.mult, [hold, ("vec",)], [("tmpc",)])
                TT("dve", corrA[:, :, 0], uold[:, :, 0], W0, ALU.mult, [hold, ("vec",)], [("corrA",)])
                TT("dve", corrA[:, :, 0], corrA[:, :, 0], tmpc[:, :], ALU.add, [("corrA",), ("tmpc",)], [("corrA",)])
                TT("dve", corrA[:, :, 1], uold[:, :, 1], W0, ALU.mult, [hold, ("vec",)], [("corrA",)])
                prev = None
                hoist = (l == L - 1 and it + 1 < NT)
                if hoist:
                    nb2 = norm_begin()
                for c in range(NFC):
                    i = c % 2
                    if hoist and 4 <= c < 12:
                        norm_sq(c - 4, 1 - xp)
                    if hoist and c == 18:
                        norm_mm(nb2)
                    for gv, cc in enumerate((c, NFC + c)):
                        bank = nextbank()
                        proj_group(bank, wi, 8, h_of)
                        wi += 8
                        fr = ("facc", i * 2 + gv)
                        acc = facc[:, i * 2 + gv, :]
                        pb = ps[bank]
                        w0 = vec[:, fb + cc:fb + cc + 1]
                        w1 = vec[:, fb + 44 + cc:fb + 45 + cc]
                        w2 = vec[:, fb + 88 + cc:fb + 89 + cc]
                        ACT(acc, pb[:, :], AF.Identity, [("ps", bank), ("vec",)], [fr], scale=w2,
                            bias=vec[:, bb + cc:bb + cc + 1])
                        ACT(unew[:, cc, :], pb[:, T - 2:T], AF.Copy, [("ps", bank)], [hnew])
                        TT("dve", acc[:, 0:2], acc[:, 0:2], corrA[:, cc, :], ALU.add, [fr, ("corrA",)], [fr])
                        STT(acc[:, 1:T], pb[:, 0:T - 1], w1, acc[:, 1:T], ALU.mult, ALU.add,
                            [("ps", bank), ("vec",), fr], [fr])
                        STT(acc[:, 2:T], pb[:, 0:T - 2], w0, acc[:, 2:T], ALU.mult, ALU.add,
                            [("ps", bank), ("vec",), fr], [fr])
                    if prev is not None:
                        silu_mul(*prev)
                    prev = (c, i)
                silu_mul(*prev)
                if hoist:
                    norm_finish(nb2, 0, False, 1 - xp)
                for m in range(8):
                    bank = nextbank()
                    proj_group(bank, wi, NFC, lambda kc: (actb[:, kc, :], ("act", kc), None))
                    wi += NFC
                    TT("dve", xres[:, m, :], ps[bank][:, :], xres[:, m, :], ALU.add,
                       [("ps", bank), ("x", xp, m)], [("x", xp, m)])
                nbank = norm_begin()
                for c in range(8):
                    norm_chunk(nbank, c, xp)
                if DEBUG and it == DBG_IT and l == DBG_L:
                    DMA("sp", dbg["act"][:, :, :], actb[:, :, :], [("act", k) for k in range(NFC)], [], "ost")
                    DMA("sp", dbg["x2"][:, :, :], xres[:, :, :], [("x", xp, k) for k in range(8)], [], "ost")
            assert wi == (it + 1) * NSUB, (wi, it)
            norm_finish(nbank, NVL * L, (t0,), xp)

        S_.prog["sp"].append(([([("ost", S_.cnt["ost"])], None)], None, 0))

        with nc.Block() as block:
            def make(engname):
                def body(e):
                    for items, sk, inc in S_.prog[engname]:
                        last = None
                        for waits, fn in items:
                            for k, v in waits:
                                e.wait_ge(sems[k], v)
                            if fn is not None:
                                last = fn(e)
                        if last is not None and inc:
                            last.then_inc(sems[sk], inc)
                return body
            block.tensor(make("pe"))
            block.scalar(make("act"))
            block.vector(make("dve"))
            block.gpsimd(make("gp"))
            block.sync(make("sp"))
    return nc


def _fm(v):
    v = np.asarray(v, np.float32)
    return v.reshape(-1, 128).T


def prep_weights(inp):
    w_in, w_up, w_down, w_o = inp["w_in"], inp["w_up"], inp["w_down"], inp["w_o"]
    wbr = (inp["w_branch_a"], inp["w_branch_b"], inp["w_branch_c"])
    wsrc = np.empty((NSUB, 128, 128), np.float32)
    n = 0
    for l in range(L):
        Wl = w_in[l]

        def cols(Wm, col0, nk):
            nonlocal n
            blk = Wm[:nk * 128, col0:col0 + 128].reshape(nk, 128, 128)
            wsrc[n:n + nk] = blk
            n += nk
        for g in range(4):
            cols(Wl, g * 128, 8)
        for j in range(4):
            cols(Wl, 512 + j * 128, 8)
        for kc in range(8):
            for q4 in range(4):
                wsrc[n] = Wl[kc * 128:(kc + 1) * 128, 1024 + q4 * 128:1024 + (q4 + 1) * 128]
                n += 1
        for j in range(4):
            for off in (2048, 2560, 1536):
                cols(Wl, off + j * 128, 8)
        for m in range(8):
            for off in (3072, 4096, 5120):
                cols(Wl, off + m * 128, 8)
            for b in range(3):
                cols(wbr[b][l], m * 128, 4)
        for m in range(8):
            cols(w_o[l], m * 128, 8)
        for c in range(NFC):
            for cc in (c, NFC + c):
                cols(w_up[l], cc * 128, 8)
        for m in range(8):
            cols(w_down[l], m * 128, NFC)
    assert n == NSUB
    wsrc = np.ascontiguousarray(wsrc.transpose(1, 0, 2)).reshape(128, NSUB * 128)

    wsm = np.empty((L, 8, 128, 128), np.float32)
    for l in range(L):
        for g in range(4):
            wsm[l, g] = inp["w_pool"][l, g]
            wsm[l, 4 + g] = inp["w_spatial"][l, g].T
    wsm = np.ascontiguousarray(wsm.reshape(L * 8, 128, 128).transpose(1, 0, 2)).reshape(128, L * 8 * 128)

    vecs = np.empty((128, NV), np.float32)
    for l in range(L):
        vb = l * NVL
        vecs[:, vb:vb + 8] = _fm(inp["g_mix"][l])
        vecs[:, vb + 8:vb + 12] = _fm(inp["pool_scale"][l])
        for tap in range(3):
            vecs[:, vb + 12 + tap * 4:vb + 16 + tap * 4] = _fm(inp["conv_c"][l, tap])
        vecs[:, vb + 24:vb + 32] = _fm(inp["g_ffn"][l])
        for tap in range(3):
            vecs[:, vb + 32 + tap * 44:vb + 32 + (tap + 1) * 44] = _fm(inp["conv_ffn"][l, tap])
        vecs[:, vb + 164:vb + 208] = _fm(inp["conv_ffn_b"][l])
    vecs[:, NVL * L:NVL * L + 8] = _fm(inp["g_final"])

    gsg = np.ascontiguousarray(np.broadcast_to(np.asarray(inp["g_sgu"], np.float32).reshape(1, L * 512), (128, L * 512)))
    bspv = np.ascontiguousarray(np.asarray(inp["b_spatial"], np.float32).reshape(1, L * 4 * 128))
    cst = np.zeros((128, 192), np.float32)
    s_idx = np.arange(128)[:, None]
    t_idx = np.arange(128)[None, :]
    cst[:, 0:128] = (t_idx >= s_idx).astype(np.float32)
    for g, w in enumerate(POOL_W):
        tt = np.arange(16)
        cst[:, 128 + g * 16:128 + (g + 1) * 16] = (w / np.minimum(tt + 1, w)).astype(np.float32)[None, :]
    return dict(wsrc=wsrc, wsm=wsm, vecs=vecs, gsg=gsg, bsp=bspv, cst=cst)


_CACHE = {}


def run(inputs, S=SEQ, ncores=NCORES, trace=False):
    inp = {k: np.asarray(v) for k, v in inputs.items()}
    common = prep_weights(inp)
    x = inp["x"]
    in_maps = []
    for b in range(ncores):
        m = dict(common)
        m["xT"] = np.ascontiguousarray(x[b, :S, :].T)
        in_maps.append(m)
    if S not in _CACHE:
        _CACHE[S] = build_program(S)
    nc = _CACHE[S]
    res = run_bass_kernel_spmd(nc, in_maps, core_ids=list(range(ncores)), trace=trace)
    out = np.stack([np.ascontiguousarray(r["outT"].T) for r in res.results], axis=0)
    return out.astype(np.float32), res


def kernel(**inputs):
    out, _ = run(inputs)
    return out
```

```python
import numpy as np
from contextlib import ExitStack
import concourse.bass as bass
import concourse.mybir as mybir
from concourse.bass_utils import run_bass_kernel_spmd

F32 = mybir.dt.float32
BF16 = mybir.dt.bfloat16
AF = mybir.ActivationFunctionType
ALU = mybir.AluOpType

D = 1024
L = 2
T = 512
NQ = T // 128
DFF = 2816
NFC = DFF // 128
EPS = 1e-6
SEQ = 8192
DEBUG = False
GPE = "gp"
DBG_IT, DBG_L = 0, 0
NCORES = 8
NSUB_L = 1072
NSUB = NSUB_L * L
CH = 16
NCH = NSUB // CH
R = 6
PIECE = 8
NPIECE = (NCH + PIECE - 1) // PIECE
NVL = 208
NV = NVL * L + 8
POOL_W = (2, 4, 8, 16)

ENGS = ("pe", "act", "dve", "gp", "sp")


class Sched:
    def __init__(self):
        self.prog = {e: [] for e in ENGS}
        self.cnt = {}
        self.waited = {e: {} for e in ENGS}
        self.lw = {}
        self.rd = {}
        self.same_sync = {"pe": False, "act": True, "dve": True, "gp": True, "sp": False}
        self.big_ok = {"act": True, "dve": True}
        self.big = {}

    def deps(self, reads, writes):
        d = {}
        for r in reads:
            x = self.lw.get(r)
            if x is not None and x[1] > d.get(x[0], 0):
                d[x[0]] = x[1]
        for w in writes:
            x = self.lw.get(w)
            if x is not None and x[1] > d.get(x[0], 0):
                d[x[0]] = x[1]
            for k, v in self.rd.get(w, {}).items():
                if v > d.get(k, 0):
                    d[k] = v
        return d

    def _waits(self, eng, d, big=False):
        waits = []
        wd = self.waited[eng]
        for k, v in d.items():
            if k == eng and not self.same_sync[eng]:
                continue
            if k == eng and big and self.big_ok.get(eng) and self.big.get((eng, v)):
                continue
            if wd.get(k, 0) >= v:
                continue
            wd[k] = v
            waits.append((k, v))
        return waits

    def _record(self, sk, val, reads, writes):
        for w in writes:
            self.lw[w] = (sk, val)
            self.rd[w] = {}
        for r in reads:
            if r in writes:
                continue
            m = self.rd.setdefault(r, {})
            if val > m.get(sk, 0):
                m[sk] = val

    def op(self, eng, fn, reads=(), writes=(), semkey=None, inc=1, big=False):
        waits = self._waits(eng, self.deps(reads, writes), big)
        sk = semkey if semkey is not None else eng
        val = self.cnt.get(sk, 0) + inc
        self.cnt[sk] = val
        if big and semkey is None:
            self.big[(eng, val)] = True
        self.prog[eng].append(([(waits, fn)], sk, inc))
        self._record(sk, val, reads, writes)

    def pe_group(self, mms, out_region):
        val = self.cnt.get("pe", 0) + 1
        items = []
        allreads = []
        for i, (fn, reads) in enumerate(mms):
            d = self.deps(reads, [out_region] if i == 0 else [])
            items.append((self._waits("pe", d), fn))
            allreads.extend(reads)
        self.cnt["pe"] = val
        self.prog["pe"].append((items, "pe", 1))
        self._record("pe", val, allreads, [out_region])


def _pe_interleaved(self, groups):
    base = self.cnt.get("pe", 0)
    nk = len(groups[0][0])
    for k in range(nk):
        for j, (mms, out_region) in enumerate(groups):
            fn, reads = mms[k]
            d = self.deps(reads, [out_region] if k == 0 else [])
            waits = self._waits("pe", d)
            self.prog["pe"].append(([(waits, fn)], "pe", 1 if k == nk - 1 else 0))
    for j, (mms, out_region) in enumerate(groups):
        val = base + j + 1
        allreads = [r for (_, reads) in mms for r in reads]
        self._record("pe", val, allreads, [out_region])
    self.cnt["pe"] = base + len(groups)


Sched.pe_interleaved = _pe_interleaved


def build_program(S):
    NT = S // T
    nc = bass.Bass("TRN2", target_bir_lowering=False)
    xT = nc.dram_tensor("xT", [D, S], F32, kind="ExternalInput").ap()
    wsrc = nc.dram_tensor("wsrc", [128, NSUB * 128], F32, kind="ExternalInput").ap()
    wsm = nc.dram_tensor("wsm", [128, L * 8 * 128], F32, kind="ExternalInput").ap()
    vecs = nc.dram_tensor("vecs", [128, NV], F32, kind="ExternalInput").ap()
    gsg = nc.dram_tensor("gsg", [128, L * 512], F32, kind="ExternalInput").ap()
    bsp = nc.dram_tensor("bsp", [1, L * 4 * 128], F32, kind="ExternalInput").ap()
    cst = nc.dram_tensor("cst", [128, 128 + 64], F32, kind="ExternalInput").ap()
    outT = nc.dram_tensor("outT", [D, S], F32, kind="ExternalOutput").ap()
    wbf = nc.dram_tensor("wbf", [128, NSUB * 128], BF16, kind="Internal").ap()
    dbg = {}
    if DEBUG:
        dbg["y"] = nc.dram_tensor("dbg_y", [128, 12, T], BF16, kind="ExternalOutput").ap()
        dbg["x1"] = nc.dram_tensor("dbg_x1", [128, 8, T], F32, kind="ExternalOutput").ap()
        dbg["act"] = nc.dram_tensor("dbg_act", [128, NFC, T], BF16, kind="ExternalOutput").ap()
        dbg["x2"] = nc.dram_tensor("dbg_x2", [128, 8, T], F32, kind="ExternalOutput").ap()
        dbg["mg"] = nc.dram_tensor("dbg_mg", [128, 8, T], BF16, kind="ExternalOutput").ap()
        dbg["ab"] = nc.dram_tensor("dbg_ab", [128, 4, T + 16], F32, kind="ExternalOutput").ap()
        dbg["pa"] = nc.dram_tensor("dbg_pa", [128, 4, T], BF16, kind="ExternalOutput").ap()
        dbg["h"] = nc.dram_tensor("dbg_h", [128, 8, T], BF16, kind="ExternalOutput").ap()
    xTv = xT.rearrange("(c p) s -> p c s", p=128)
    outTv = outT.rearrange("(c p) s -> p c s", p=128)

    S_ = Sched()
    W = T + 16
    with ExitStack() as st:
        def sb(name, shape, dt):
            return st.enter_context(nc.sbuf_tensor(name, shape, dt))

        xbufs = [sb("xres0", [128, 8, T], F32), sb("xres1", [128, 8, T], F32)]
        hbuf = sb("hbuf", [128, 8, T], BF16)
        rsb = sb("rsb", [128, T], F32)
        ones_bf = sb("ones_bf", [128, 128], BF16)
        wsmall = sb("wsmall", [128, L, 8, 128], BF16)
        bpad = sb("bpad", [128, L, 4, 128], BF16)
        gsgu = sb("gsgu", [128, L * 512], F32)
        vec = sb("vec", [128, NV], F32)
        cst_sb = sb("cst_sb", [128, 192], F32)
        ahalo = sb("ahalo", [128, L, 4, 16], F32)
        zhalo = sb("zhalo", [128, L, 4, 2], F32)
        uph = sb("uph", [128, L * 2, 2 * NFC, 2], F32)
        corrA = sb("corrA", [128, 2 * NFC, 2], F32)
        tmpc = sb("tmpc", [128, 2 * NFC], F32)
        abuf = sb("abuf", [128, 4, W], F32)
        ptmp = sb("ptmp", [128, 2, W], F32)
        pabuf = sb("pabuf", [128, 4, T], BF16)
        ubuf = sb("ubuf", [128, 4, T], F32)
        vstage = sb("vstage", [128, 2, 512], F32)
        vsq = sb("vsq", [128, 512], BF16)
        vss = sb("vss", [128, NQ], F32)
        vrs = sb("vrs", [128, NQ], F32)
        vrr = sb("vrr", [128, NQ], F32)
        vtok = sb("vtok", [128, NQ, 512], BF16)
        cc_sb = sb("cc_sb", [128, 2, T], F32)
        zbuf = sb("zbuf", [128, 2, T + 2], F32)
        cacc = sb("cacc", [128, 2, T], F32)
        ybuf = sb("ybuf", [128, 12, T], BF16)
        gm = sb("gm", [128, 12, T], F32)
        mg = sb("mg", [128, 8, T], BF16)
        actb = sb("actb", [128, NFC, T], BF16)
        facc = sb("facc", [128, 4, T], F32)
        fsg = sb("fsg", [128, 2, T], F32)
        ring = sb("ring", [128, R, CH * 128], BF16)
        ps = [st.enter_context(nc.psum_tensor("ps%d" % i, [128, 512], F32)) for i in range(8)]

        semkeys = ["pe", "act", "dve", "gp", "xl0", "xl1", "ost", "cst0", "cst1", "cst2", "cst3", "cst4"] + \
                  [("wl", s) for s in range(R)] + [("wb", j) for j in range(8)]
        sems = {}
        for i, k in enumerate(semkeys):
            sems[k] = st.enter_context(nc.semaphore("sem%d" % i))

        state = {"bank": 0, "next_load": 0, "reserved": set(), "gpe": "dve"}

        def nextbank():
            while True:
                b = state["bank"]
                state["bank"] = (b + 1) % 8
                if b not in state["reserved"]:
                    return b

        def isbig(ap):
            n = 1
            for d in ap.shape[1:]:
                n *= int(d)
            return n >= 256

        def ACT(out, in_, func, reads, writes, scale=1.0, bias=None, accum=None):
            def fn(e):
                kw = {}
                if bias is not None:
                    kw["bias"] = bias
                if accum is not None:
                    kw["accum_out"] = accum
                return e.activation(out=out, in_=in_, func=func, scale=scale, **kw)
            S_.op("act", fn, reads, writes, big=isbig(out) and accum is None)

        def TT(eng, out, in0, in1, op, reads, writes):
            S_.op(eng, lambda e: e.tensor_tensor(out=out, in0=in0, in1=in1, op=op), reads, writes, big=isbig(out))

        def TS(eng, out, in0, s1, s2, op0, op1, reads, writes):
            if op1 is None:
                S_.op(eng, lambda e: e.tensor_scalar(out=out, in0=in0, scalar1=s1, scalar2=None, op0=op0),
                      reads, writes, big=isbig(out))
            else:
                S_.op(eng, lambda e: e.tensor_scalar(out=out, in0=in0, scalar1=s1, scalar2=s2, op0=op0, op1=op1),
                      reads, writes, big=isbig(out))

        def STT(out, in0, scalar, in1, op0, op1, reads, writes):
            S_.op("dve", lambda e: e.scalar_tensor_tensor(out=out, in0=in0, scalar=scalar, in1=in1,
                                                          op0=op0, op1=op1), reads, writes, big=isbig(out))

        def COPY(eng, out, in_, reads, writes):
            S_.op(eng, lambda e: e.tensor_copy(out=out, in_=in_), reads, writes, big=isbig(out))

        def MEMSET(eng, ap, val, writes):
            S_.op(eng, lambda e: e.memset(ap, val), (), writes)

        def DMA(eng, out, in_, reads, writes, semkey, **kw):
            S_.op(eng, lambda e: e.dma_start(out=out, in_=in_, **kw), reads, writes, semkey=semkey, inc=16)

        def MM(out, lhsT, rhs, start, stop):
            return lambda e: e.matmul(out, lhsT, rhs, start=start, stop=stop)

        total_chunks = NT * NCH

        def wload_upto(gc):
            while state["next_load"] <= min(gc, total_chunks - 1):
                k = state["next_load"]
                slot = k % R
                c = k % NCH
                sl = slice(c * CH * 128, (c + 1) * CH * 128)
                if k < NCH:
                    DMA("gp", ring[:, slot, :], wsrc[:, sl], [], [("w", slot)], ("wl", slot))
                    DMA("sp", wbf[:, sl], ring[:, slot, :], [("w", slot)], [("wbfc", c)], ("wb", c % 8))
                    if k == NCH - 1:
                        for c2 in range(NCH):
                            S_.lw[("wbfc", c2)] = (("wb", c2 % 8), S_.cnt[("wb", c2 % 8)])
                else:
                    DMA("sp", ring[:, slot, :], wbf[:, sl], [("wbfc", c)], [("w", slot)], ("wl", slot))
                state["next_load"] += 1

        def wsub(i, n=1):
            gc = i // CH
            slot = gc % R
            off = i % CH
            assert off + n <= CH
            return ring[:, slot, off * 128:(off + n) * 128], ("w", slot), gc

        def group(bank, specs):
            chunks = [x[2] for sp_ in specs for x in (sp_[1], sp_[2]) if x[2] is not None]
            if chunks:
                wload_upto(max(chunks))
            mms = []
            for (o, lh, rh, s0, s1) in specs:
                mms.append((MM(o, lh[0], rh[0], s0, s1), [lh[1], rh[1]]))
            S_.pe_group(mms, ("ps", bank))
            if chunks:
                wload_upto(min(chunks) + R - 1)

        def proj_group(bank, wi, nk, rhs_of):
            specs = []
            for kc in range(nk):
                specs.append((ps[bank][:, :], wsub(wi + kc), rhs_of(kc), kc == 0, kc == nk - 1))
            group(bank, specs)

        def h_of(kc):
            return (hbuf[:, kc, :], ("h", kc), None)

        def proj_groups_interleaved(banks, wi0, nk, rhs_of):
            groups = []
            chunks = []
            for j, bank in enumerate(banks):
                mms = []
                for kc in range(nk):
                    lh = wsub(wi0 + j * nk + kc)
                    rh = rhs_of(kc)
                    chunks.append(lh[2])
                    mms.append((MM(ps[bank][:, :], lh[0], rh[0], kc == 0, kc == nk - 1), [lh[1], rh[1]]))
                groups.append((mms, ("ps", bank)))
            wload_upto(max(chunks))
            S_.pe_interleaved(groups)
            wload_upto(min(chunks) + R - 1)

        DMA("sp", vec[:, :], vecs[:, :], [], [("vec",)], "cst0")
        DMA("sp", gsgu[:, :], gsg[:, :], [], [("gsgu",)], "cst1")
        DMA("sp", cst_sb[:, :], cst[:, :], [], [("cstsb",)], "cst2")
        gmflat = gm[:, 0:4, :]
        DMA("sp", gmflat, wsm.rearrange("p (a b) -> p a b", a=4), [], [("gm", k) for k in range(4)], "cst3")
        MEMSET("dve", ones_bf[:, :], 1.0, [("ones",)])
        MEMSET("dve", ahalo[:, :, :, :], 0.0, [("ahalo", l) for l in range(L)])
        MEMSET("dve", zhalo[:, :, :, :], 0.0, [("zhalo", l, j) for l in range(L) for j in range(4)])
        MEMSET("dve", uph[:, :, :, :], 0.0, [("uph", l, p_) for l in range(L) for p_ in range(2)])
        MEMSET("dve", bpad[:, :, :, :], 0.0, [("bpad",)])
        DMA("gp", bpad[0:1, :, :, :], bsp.rearrange("o (l g t) -> o l g t", l=L, g=4), [], [("bpad",)], "cst4")
        for l in range(L):
            for g in range(8):
                idx = l * 8 + g
                src = gm[:, idx // 4, (idx % 4) * 128:(idx % 4 + 1) * 128]
                if g < 4:
                    COPY("dve", wsmall[:, l, g, :], src, [("gm", idx // 4)], [("wsmall",)])
                else:
                    TT("dve", wsmall[:, l, g, :], src, cst_sb[:, 0:128], ALU.mult,
                       [("gm", idx // 4), ("cstsb",)], [("wsmall",)])

        eps_t = sb("eps_t", [128, 1], F32)
        eps_ap = eps_t[:, 0:1]
        MEMSET("dve", eps_t[:, :], EPS, [("vec2",)])

        def norm_begin():
            bank = nextbank()
            state["reserved"].add(bank)
            return bank

        def norm_sq(c, xp):
            xres = xbufs[xp]
            sqacc, sqreg = cacc[:, 0, :], ("cacc", 0)
            if c == 0:
                ACT(sqacc, xres[:, c, :], AF.Square, [("x", xp, c)], [sqreg])
            elif c < 6:
                i = c % 2
                ACT(vstage[:, i, :], xres[:, c, :], AF.Square, [("x", xp, c)], [("vst", i)])
                if c < 5:
                    TT(state["gpe"], sqacc, sqacc, vstage[:, i, :], ALU.add, [sqreg, ("vst", i)], [sqreg])
                else:
                    TT(state["gpe"], vsq[:, :], sqacc, vstage[:, i, :], ALU.add, [sqreg, ("vst", i)], [("vsq",)])
            else:
                ACT(mg[:, c, :], xres[:, c, :], AF.Square, [("x", xp, c)], [("mg", c)])

        def norm_mm(bank):
            S_.pe_group([(MM(ps[bank][:, :], ones_bf[:, :], vsq[:, :], True, False), [("ones",), ("vsq",)]),
                         (MM(ps[bank][:, :], ones_bf[:, :], mg[:, 6, :], False, False), [("ones",), ("mg", 6)]),
                         (MM(ps[bank][:, :], ones_bf[:, :], mg[:, 7, :], False, True), [("ones",), ("mg", 7)])],
                        ("ps", bank))

        def norm_chunk(bank, c, xp):
            norm_sq(c, xp)
            if c == 7:
                norm_mm(bank)

        def norm_finish(bank, gbase, to_out, xp):
            xres = xbufs[xp]
            ACT(rsb[:, :], ps[bank][:, :], AF.Ln, [("ps", bank), ("vec2",)], [("rsb",)], scale=1.0 / D,
                bias=eps_ap)
            ACT(ps[bank][:, :], rsb[:, :], AF.Exp, [("rsb",)], [("ps", bank)], scale=-0.5)
            for c in range(8):
                if to_out:
                    dst, reg = gm[:, c, :], ("gm", c)
                else:
                    dst, reg = hbuf[:, c, :], ("h", c)
                STT(dst, xres[:, c, :], vec[:, gbase + c:gbase + c + 1], ps[bank][:, :], ALU.mult, ALU.mult,
                    [("x", xp, c), ("ps", bank), ("vec",)], [reg])
                if to_out:
                    DMA("sp", outTv[:, c, to_out[0]:to_out[0] + T], gm[:, c, :], [("gm", c)], [], "ost")
            if to_out:
                for c in range(8):
                    S_.rd[("gm", c)]["ost"] = S_.cnt["ost"]
            state["reserved"].discard(bank)

        wi = 0
        for it in range(NT):
            t0 = it * T
            par = it % 2
            xp = it % 2
            state["gpe"] = "dve" if it == 0 else GPE
            xres = xbufs[xp]
            if it == 0:
                DMA("sp", xres[:, :, :], xTv[:, :, t0:t0 + T], [], [("x", xp, c) for c in range(8)], "xl%d" % xp)
                nbank = norm_begin()
                for c in range(8):
                    norm_chunk(nbank, c, xp)
                norm_finish(nbank, 0, False, xp)
            def prefetch_x(c):
                sk = "xl%d" % (1 - xp)
                DMA("sp", xbufs[1 - xp][:, c, :], xTv[:, c, t0 + T:t0 + 2 * T], [], [("x", 1 - xp, c)], sk)
                if c == 7:
                    for cc_ in range(8):
                        S_.lw[("x", 1 - xp, cc_)] = (sk, S_.cnt[sk])
            wi = it * NSUB
            for l in range(L):
                vb = l * NVL
                if l > 0:
                    norm_finish(nbank, vb + 0, False, xp)
                COPY(state["gpe"], abuf[:, :, 0:16], ahalo[:, l, :, :], [("ahalo", l)], [("abuf", g) for g in range(4)])
                abanks = [nextbank() for _ in range(4)]
                proj_groups_interleaved(abanks, wi, 8, h_of)
                wi += 32
                for g in range(4):
                    ACT(abuf[:, g, 16:W], ps[abanks[g]][:, :], AF.Copy, [("ps", abanks[g])], [("abuf", g)])
                COPY(state["gpe"], ahalo[:, l, :, :], abuf[:, :, T:W], [("abuf", g) for g in range(4)], [("ahalo", l)])
                for g in range(4):
                    cur, cur_reg = abuf[:, g, :], ("abuf", g)
                    sh, k = 1, 0
                    for step in range(g + 1):
                        lo = 2 * sh - 1
                        TT(state["gpe"], ptmp[:, k, lo:W], cur[:, lo:W], cur[:, lo - sh:W - sh], ALU.add,
                           [cur_reg], [("ptmp", k)])
                        cur, cur_reg = ptmp[:, k, :], ("ptmp", k)
                        k ^= 1
                        sh *= 2
                    w = POOL_W[g]
                    TS(state["gpe"], cur[:, 16:W], cur[:, 16:W], 1.0 / w, 0.0, ALU.mult, ALU.add, [cur_reg], [cur_reg])
                    if it == 0:
                        TT(state["gpe"], cur[:, 16:32], cur[:, 16:32], cst_sb[:, 128 + g * 16:128 + (g + 1) * 16], ALU.mult,
                           [cur_reg, ("cstsb",)], [cur_reg])
                    TT(state["gpe"], pabuf[:, g, :], cur[:, 16:W], abuf[:, g, 16:W], ALU.subtract,
                       [cur_reg, ("abuf", g)], [("pa", g)])
                for j in range(4):
                    bank = nextbank()
                    proj_group(bank, wi, 8, h_of)
                    wi += 8
                    ACT(ubuf[:, j, :], ps[bank][:, :], AF.Gelu_apprx_tanh, [("ps", bank)], [("u", j)])
                vbase = wi
                wi += 32
                for q in range(NQ):
                    bank = nextbank()
                    specs = []
                    for kc in range(8):
                        specs.append((ps[bank][:, :], (hbuf[:, kc, q * 128:(q + 1) * 128], ("h", kc), None),
                                      wsub(vbase + kc * 4, 4), kc == 0, kc == 7))
                    group(bank, specs)
                    i = q % 2
                    ACT(vstage[:, i, :], ps[bank][:, :], AF.Gelu_apprx_tanh, [("ps", bank)], [("vst", i)])
                    ACT(vsq[:, :], vstage[:, i, :], AF.Square, [("vst", i)], [("vsq",), ("vss", q)],
                        accum=vss[:, q:q + 1])
                    ACT(vrs[:, q:q + 1], vss[:, q:q + 1], AF.Sqrt, [("vss", q), ("vec2",)], [("vrs", q)],
                        scale=1.0 / 512, bias=eps_ap)
                    S_.op("dve", (lambda q: lambda e: e.reciprocal(out=vrr[:, q:q + 1], in_=vrs[:, q:q + 1]))(q),
                          [("vrs", q)], [("vrr", q)])
                    STT(vtok[:, q, :], vstage[:, i, :], vrr[:, q:q + 1], gsgu[:, l * 512:(l + 1) * 512],
                        ALU.mult, ALU.mult, [("vst", i), ("vrr", q), ("gsgu",)], [("vtok", q)])
                for j in range(4):
                    i = j % 2
                    banks = []
                    for _ in range(3):
                        bank = nextbank()
                        proj_group(bank, wi, 8, h_of)
                        wi += 8
                        banks.append(bank)
                    b_cc, b_cx, b_cb = banks
                    ACT(cc_sb[:, i, :], ps[b_cc][:, :], AF.Copy, [("ps", b_cc)], [("cc", i)])
                    COPY(state["gpe"], zbuf[:, i, 0:2], zhalo[:, l, j, :], [("zhalo", l, j)], [("z", i)])
                    TT("dve", zbuf[:, i, 2:T + 2], ps[b_cx][:, :], cc_sb[:, i, :], ALU.mult,
                       [("ps", b_cx), ("cc", i)], [("z", i)])
                    COPY(state["gpe"], zhalo[:, l, j, :], zbuf[:, i, T:T + 2], [("z", i)], [("zhalo", l, j)])
                    cb = vb + 12
                    TS("dve", cacc[:, i, :], zbuf[:, i, 2:T + 2], vec[:, cb + 8 + j:cb + 9 + j], None, ALU.mult, None,
                       [("z", i), ("vec",)], [("cacc", i)])
                    STT(cacc[:, i, :], zbuf[:, i, 1:T + 1], vec[:, cb + 4 + j:cb + 5 + j], cacc[:, i, :],
                        ALU.mult, ALU.add, [("z", i), ("vec",), ("cacc", i)], [("cacc", i)])
                    STT(cacc[:, i, :], zbuf[:, i, 0:T], vec[:, cb + j:cb + 1 + j], cacc[:, i, :],
                        ALU.mult, ALU.add, [("z", i), ("vec",), ("cacc", i)], [("cacc", i)])
                    TT("dve", ybuf[:, 8 + j, :], ps[b_cb][:, :], cacc[:, i, :], ALU.mult,
                       [("ps", b_cb), ("cacc", i)], [("y", 8 + j)])
                for g in range(4):
                    bank = nextbank()
                    S_.pe_group([(MM(ps[bank][:, :], wsmall[:, l, g, :], pabuf[:, g, :], True, True),
                                  [("wsmall",), ("pa", g)])], ("ps", bank))
                    ACT(ybuf[:, g, :], ps[bank][:, :], AF.Identity, [("ps", bank), ("vec",)], [("y", g)],
                        scale=vec[:, vb + 8 + g:vb + 9 + g])
                for g in range(4):
                    bank = nextbank()
                    mms = []
                    for q in range(NQ):
                        o = ps[bank][:, q * 128:(q + 1) * 128]
                        mms.append((MM(o, vtok[:, q, g * 128:(g + 1) * 128], wsmall[:, l, 4 + g, :], True, False),
                                    [("vtok", q), ("wsmall",)]))
                        mms.append((MM(o, ones_bf[:, :], bpad[:, l, g, :], False, True), [("ones",), ("bpad",)]))
                    S_.pe_group(mms, ("ps", bank))
                    TT("dve", ybuf[:, 4 + g, :], ps[bank][:, :], ubuf[:, g, :], ALU.mult,
                       [("ps", bank), ("u", g)], [("y", 4 + g)])
                for m in range(8):
                    i = m % 2
                    gbanks = []
                    for b in range(3):
                        bank = nextbank()
                        proj_group(bank, wi, 8, h_of)
                        wi += 8
                        gbanks.append(bank)
                    for b in range(3):
                        ACT(gm[:, i * 3 + b, :], ps[gbanks[b]][:, :], AF.Sigmoid, [("ps", gbanks[b])],
                            [("gm", i * 3 + b)])
                    bbanks = []
                    for b in range(3):
                        bank = nextbank()
                        proj_group(bank, wi, 4, (lambda b: lambda kc: (ybuf[:, 4 * b + kc, :], ("y", 4 * b + kc), None))(b))
                        wi += 4
                        bbanks.append(bank)
                    for b in range(3):
                        TT("dve", gm[:, 6 + i * 3 + b, :], ps[bbanks[b]][:, :], gm[:, i * 3 + b, :], ALU.mult,
                           [("ps", bbanks[b]), ("gm", i * 3 + b)], [("gm", 6 + i * 3 + b)])
                    TT(state["gpe"], gm[:, 6 + i * 3, :], gm[:, 6 + i * 3, :], gm[:, 7 + i * 3, :], ALU.add,
                       [("gm", 6 + i * 3), ("gm", 7 + i * 3)], [("gm", 6 + i * 3)])
                    TT(state["gpe"], mg[:, m, :], gm[:, 6 + i * 3, :], gm[:, 8 + i * 3, :], ALU.add,
                       [("gm", 6 + i * 3), ("gm", 8 + i * 3)], [("mg", m)])
                    if l == 0 and it + 1 < NT:
                        prefetch_x(m)
                if DEBUG and it == DBG_IT and l == DBG_L:
                    DMA("sp", dbg["y"][:, :, :], ybuf[:, :, :], [("y", k) for k in range(12)], [], "ost")
                    DMA("sp", dbg["mg"][:, :, :], mg[:, :, :], [("mg", k) for k in range(8)], [], "ost")
                obanks = []
                for m in range(8):
                    bank = nextbank()
                    proj_group(bank, wi, 8, lambda kc: (mg[:, kc, :], ("mg", kc), None))
                    wi += 8
                    obanks.append(bank)
                    TT("dve", xres[:, m, :], ps[bank][:, :], xres[:, m, :], ALU.add,
                       [("ps", bank), ("x", xp, m)], [("x", xp, m)])
                nbank = norm_begin()
                for c in range(8):
                    norm_chunk(nbank, c, xp)
                if DEBUG and it == DBG_IT and l == DBG_L:
                    DMA("sp", dbg["x1"][:, :, :], xres[:, :, :], [("x", xp, k) for k in range(8)], [], "ost")
                norm_finish(nbank, vb + 24, False, xp)
                if DEBUG and it == DBG_IT and l == DBG_L:
                    DMA("sp", dbg["h"][:, :, :], hbuf[:, :, :], [("h", k) for k in range(8)], [], "ost")
                fb = vb + 32
                bb = vb + 164
                hold = ("uph", l, par)
                hnew = ("uph", l, 1 - par)
                uold = uph[:, l * 2 + par, :, :]
                unew = uph[:, l * 2 + 1 - par, :, :]

                def silu_mul(c, i):
                    ACT(fsg[:, i, :], facc[:, i * 2, :], AF.Silu, [("facc", i * 2)], [("fsg", i)])
                    TT(state["gpe"], actb[:, c, :], fsg[:, i, :], facc[:, i * 2 + 1, :], ALU.mult,
                       [("fsg", i), ("facc", i * 2 + 1)], [("act", c)])

                W0 = vec[:, fb:fb + 44]
                W1 = vec[:, fb + 44:fb + 88]
                TT("dve", tmpc[:, :], uold[:, :, 1], W1, ALU.mult, [hold, ("vec",)], [("tmpc",)])
                TT("dve", corrA[:, :, 0], uold[:, :, 0], W0, ALU.mult, [hold, ("vec",)], [("corrA",)])
                TT("dve", corrA[:, :, 0], corrA[:, :, 0], tmpc[:, :], ALU.add, [("corrA",), ("tmpc",)], [("corrA",)])
                TT("dve", corrA[:, :, 1], uold[:, :, 1], W0, ALU.mult, [hold, ("vec",)], [("corrA",)])
                prev = None
                hoist = (l == L - 1 and it + 1 < NT)
                if hoist:
                    nb2 = norm_begin()
                for c in range(NFC):
                    i = c % 2
                    if hoist and 4 <= c < 12:
                        norm_sq(c - 4, 1 - xp)
                    if hoist and c == 18:
                        norm_mm(nb2)
                    if c == 0:
                        fbanks = [nextbank() for _ in range(4)]
                        proj_groups_interleaved(fbanks, wi, 8, h_of)
                        wi += 32
                    for gv, cc in enumerate((c, NFC + c)):
                        if c < 2:
                            bank = fbanks[c * 2 + gv]
                        else:
                            bank = nextbank()
                            proj_group(bank, wi, 8, h_of)
                            wi += 8
                        fr = ("facc", i * 2 + gv)
                        acc = facc[:, i * 2 + gv, :]
                        pb = ps[bank]
                        w0 = vec[:, fb + cc:fb + cc + 1]
                        w1 = vec[:, fb + 44 + cc:fb + 45 + cc]
                        w2 = vec[:, fb + 88 + cc:fb + 89 + cc]
                        ACT(acc, pb[:, :], AF.Identity, [("ps", bank), ("vec",)], [fr], scale=w2,
                            bias=vec[:, bb + cc:bb + cc + 1])
                        ACT(unew[:, cc, :], pb[:, T - 2:T], AF.Copy, [("ps", bank)], [hnew])
                        TT("dve", acc[:, 0:2], acc[:, 0:2], corrA[:, cc, :], ALU.add, [fr, ("corrA",)], [fr])
                        STT(acc[:, 1:T], pb[:, 0:T - 1], w1, acc[:, 1:T], ALU.mult, ALU.add,
                            [("ps", bank), ("vec",), fr], [fr])
                        STT(acc[:, 2:T], pb[:, 0:T - 2], w0, acc[:, 2:T], ALU.mult, ALU.add,
                            [("ps", bank), ("vec",), fr], [fr])
                    if prev is not None:
                        silu_mul(*prev)
                    prev = (c, i)
                silu_mul(*prev)
                if hoist:
                    norm_finish(nb2, 0, False, 1 - xp)
                for m in range(8):
                    bank = nextbank()
                    proj_group(bank, wi, NFC, lambda kc: (actb[:, kc, :], ("act", kc), None))
                    wi += NFC
                    TT("dve", xres[:, m, :], ps[bank][:, :], xres[:, m, :], ALU.add,
                       [("ps", bank), ("x", xp, m)], [("x", xp, m)])
                nbank = norm_begin()
                for c in range(8):
                    norm_chunk(nbank, c, xp)
                if DEBUG and it == DBG_IT and l == DBG_L:
                    DMA("sp", dbg["act"][:, :, :], actb[:, :, :], [("act", k) for k in range(NFC)], [], "ost")
                    DMA("sp", dbg["x2"][:, :, :], xres[:, :, :], [("x", xp, k) for k in range(8)], [], "ost")
            assert wi == (it + 1) * NSUB, (wi, it)
            norm_finish(nbank, NVL * L, (t0,), xp)

        S_.prog["sp"].append(([([("ost", S_.cnt["ost"])], None)], None, 0))

        with nc.Block() as block:
            def make(engname):
                def body(e):
                    for items, sk, inc in S_.prog[engname]:
                        last = None
                        for waits, fn in items:
                            for k, v in waits:
                                e.wait_ge(sems[k], v)
                            if fn is not None:
                                last = fn(e)
                        if last is not None and inc:
                            last.then_inc(sems[sk], inc)
                return body
            block.tensor(make("pe"))
            block.scalar(make("act"))
            block.vector(make("dve"))
            block.gpsimd(make("gp"))
            block.sync(make("sp"))
    return nc


def _fm(v):
    v = np.asarray(v, np.float32)
    return v.reshape(-1, 128).T


def prep_weights(inp):
    w_in, w_up, w_down, w_o = inp["w_in"], inp["w_up"], inp["w_down"], inp["w_o"]
    wbr = (inp["w_branch_a"], inp["w_branch_b"], inp["w_branch_c"])
    wsrc = np.empty((NSUB, 128, 128), np.float32)
    n = 0
    for l in range(L):
        Wl = w_in[l]

        def cols(Wm, col0, nk):
            nonlocal n
            blk = Wm[:nk * 128, col0:col0 + 128].reshape(nk, 128, 128)
            wsrc[n:n + nk] = blk
            n += nk
        for g in range(4):
            cols(Wl, g * 128, 8)
        for j in range(4):
            cols(Wl, 512 + j * 128, 8)
        for kc in range(8):
            for q4 in range(4):
                wsrc[n] = Wl[kc * 128:(kc + 1) * 128, 1024 + q4 * 128:1024 + (q4 + 1) * 128]
                n += 1
        for j in range(4):
            for off in (2048, 2560, 1536):
                cols(Wl, off + j * 128, 8)
        for m in range(8):
            for off in (3072, 4096, 5120):
                cols(Wl, off + m * 128, 8)
            for b in range(3):
                cols(wbr[b][l], m * 128, 4)
        for m in range(8):
            cols(w_o[l], m * 128, 8)
        for c in range(NFC):
            for cc in (c, NFC + c):
                cols(w_up[l], cc * 128, 8)
        for m in range(8):
            cols(w_down[l], m * 128, NFC)
    assert n == NSUB
    wsrc = np.ascontiguousarray(wsrc.transpose(1, 0, 2)).reshape(128, NSUB * 128)

    wsm = np.empty((L, 8, 128, 128), np.float32)
    for l in range(L):
        for g in range(4):
            wsm[l, g] = inp["w_pool"][l, g]
            wsm[l, 4 + g] = inp["w_spatial"][l, g].T
    wsm = np.ascontiguousarray(wsm.reshape(L * 8, 128, 128).transpose(1, 0, 2)).reshape(128, L * 8 * 128)

    vecs = np.empty((128, NV), np.float32)
    for l in range(L):
        vb = l * NVL
        vecs[:, vb:vb + 8] = _fm(inp["g_mix"][l])
        vecs[:, vb + 8:vb + 12] = _fm(inp["pool_scale"][l])
        for tap in range(3):
            vecs[:, vb + 12 + tap * 4:vb + 16 + tap * 4] = _fm(inp["conv_c"][l, tap])
        vecs[:, vb + 24:vb + 32] = _fm(inp["g_ffn"][l])
        for tap in range(3):
            vecs[:, vb + 32 + tap * 44:vb + 32 + (tap + 1) * 44] = _fm(inp["conv_ffn"][l, tap])
        vecs[:, vb + 164:vb + 208] = _fm(inp["conv_ffn_b"][l])
    vecs[:, NVL * L:NVL * L + 8] = _fm(inp["g_final"])

    gsg = np.ascontiguousarray(np.broadcast_to(np.asarray(inp["g_sgu"], np.float32).reshape(1, L * 512), (128, L * 512)))
    bspv = np.ascontiguousarray(np.asarray(inp["b_spatial"], np.float32).reshape(1, L * 4 * 128))
    cst = np.zeros((128, 192), np.float32)
    s_idx = np.arange(128)[:, None]
    t_idx = np.arange(128)[None, :]
    cst[:, 0:128] = (t_idx >= s_idx).astype(np.float32)
    for g, w in enumerate(POOL_W):
        tt = np.arange(16)
        cst[:, 128 + g * 16:128 + (g + 1) * 16] = (w / np.minimum(tt + 1, w)).astype(np.float32)[None, :]
    return dict(wsrc=wsrc, wsm=wsm, vecs=vecs, gsg=gsg, bsp=bspv, cst=cst)


_CACHE = {}


def run(inputs, S=SEQ, ncores=NCORES, trace=False):
    inp = {k: np.asarray(v) for k, v in inputs.items()}
    common = prep_weights(inp)
    x = inp["x"]
    in_maps = []
    for b in range(ncores):
        m = dict(common)
        m["xT"] = np.ascontiguousarray(x[b, :S, :].T)
        in_maps.append(m)
    if S not in _CACHE:
        _CACHE[S] = build_program(S)
    nc = _CACHE[S]
    res = run_bass_kernel_spmd(nc, in_maps, core_ids=list(range(ncores)), trace=trace)
    out = np.stack([np.ascontiguousarray(r["outT"].T) for r in res.results], axis=0)
    return out.astype(np.float32), res


def kernel(**inputs):
    out, _ = run(inputs)
    return out
```
